# Optimizing a Trainium2 kernel written in Bass

```python
import math
import jax, jax.numpy as jnp
from jax import lax
import numpy as np

D_MODEL = 1024
BATCH = 8
SEQ = 2048
DEPTH = 2

CTX_LEN = 256
GRID_W = 64
D_FF = 2816
N_MOD = 9
EPS = 1e-6

RWKV_HEADS = 4
RWKV_HEAD_DIM = 64
D_RWKV = RWKV_HEADS * RWKV_HEAD_DIM
DECAY_LORA = 64
AAA_LORA = 64
GATE_LORA = 128
GN_EPS = 64e-5
NORM_EPS = 1e-12
D_CONV = 256
CONV_WIDTH = 31
DIFF_HEADS = 4
DIFF_QK_DIM = 64
DIFF_V_DIM = 2 * DIFF_QK_DIM
D_DIFF = DIFF_HEADS * DIFF_V_DIM
Q_BLOCK = 128
ROPE_THETA = 10000.0
AXIS_DIM = DIFF_QK_DIM // 2
ROPE_FREQS = AXIS_DIM // 2

D_MIX = D_RWKV + D_CONV + D_DIFF
RWKV_IN = 3 * D_RWKV + 2 * DECAY_LORA + 2 * AAA_LORA + GATE_LORA
CONV_IN = 2 * D_CONV
DIFF_IN = 2 * DIFF_HEADS * 2 * DIFF_QK_DIM + D_DIFF
P_IN = RWKV_IN + CONV_IN + DIFF_IN

kernel_name = 'hybrid_rwkv7_conformer_diffattn_dit'

f32 = jnp.float32


def rmsnorm(x, g, eps=EPS):
    xf = x.astype(f32)
    y = xf * lax.rsqrt(jnp.mean(xf * xf, axis=-1, keepdims=True) + eps)
    return (y * g.astype(f32)).astype(x.dtype)


def layernorm(x, g, b, eps):
    xf = x.astype(f32)
    mu = jnp.mean(xf, axis=-1, keepdims=True)
    xc = xf - mu
    var = jnp.mean(xc * xc, axis=-1, keepdims=True)
    return (xc * lax.rsqrt(var + eps) * g.astype(f32) + b.astype(f32)).astype(x.dtype)


def modulate(h, shift, scale):
    return h * (1.0 + scale) + shift


def swiglu(h, w_in, w_out):
    gate, up = jnp.split(h @ w_in, 2, axis=-1)
    return (jax.nn.silu(gate) * up) @ w_out


def to_heads(t):
    return t.reshape(t.shape[:-1] + (RWKV_HEADS, RWKV_HEAD_DIM))


def centred_shift(f, mu_prev, mu_next):
    zero = jnp.zeros_like(f[:, :1])
    prev = jnp.concatenate([zero, f[:, :-1]], axis=1)
    nxt = jnp.concatenate([f[:, 1:], zero], axis=1)
    return f + mu_prev * (prev - f) + mu_next * (nxt - f)


def rwkv_prepare(f, mu, w0, w2, a0, a2, g2, kk_scale, ka_scale):
    b, n, _ = f.shape
    f = centred_shift(f, mu[0], mu[1])
    r = f[..., 0:D_RWKV]
    k = f[..., D_RWKV:2 * D_RWKV]
    v = f[..., 2 * D_RWKV:3 * D_RWKV]
    o = 3 * D_RWKV
    wd = f[..., o:o + 2 * DECAY_LORA].reshape(b, n, 2, DECAY_LORA)
    o = o + 2 * DECAY_LORA
    ad = f[..., o:o + 2 * AAA_LORA].reshape(b, n, 2, AAA_LORA)
    o = o + 2 * AAA_LORA
    gd = f[..., o:o + GATE_LORA]
    w_raw = w0[:, None, None, :] + jnp.einsum('bndr,drc->dbnc', jnp.tanh(wd), w2)
    decay = jnp.exp(-jnp.exp(-jax.nn.softplus(-w_raw.astype(f32)) - 0.5))
    a = jax.nn.sigmoid((a0[:, None, None, :] + jnp.einsum('bndr,drc->dbnc', ad, a2)).astype(f32))
    g = jax.nn.sigmoid(gd) @ g2
    kk = to_heads((k * kk_scale).astype(f32))
    kk = kk * lax.rsqrt(jnp.sum(kk * kk, axis=-1, keepdims=True) + NORM_EPS)
    k_dir = k.astype(f32)[None] * (1.0 + (a - 1.0) * ka_scale.astype(f32))
    return (to_heads(r.astype(f32)), to_heads(v.astype(f32)), g, to_heads(k_dir),
            to_heads(decay), -kk, kk[None] * to_heads(a))


def wkv_scan(r, decay, k, v, avec, bvec, s0, reverse):
    xs = tuple(jnp.moveaxis(t, 1, 0) for t in (r, decay, k, v, avec, bvec))

    def step(s, inp):
        r_t, w_t, k_t, v_t, a_t, b_t = inp
        sa = jnp.einsum('bhij,bhj->bhi', s, a_t)
        s = s * w_t[:, :, None, :] + sa[..., None] * b_t[:, :, None, :] + v_t[..., None] * k_t[:, :, None, :]
        y = jnp.einsum('bhij,bhj->bhi', s, r_t)
        return s, y

    s_final, ys = lax.scan(step, s0, xs, reverse=reverse)
    return jnp.moveaxis(ys, 0, 1), s_final


def rwkv_finish(y, r, k2, v, g, rk, ln_g, ln_b):
    b, n = y.shape[:2]
    yn = layernorm(y, ln_g.reshape(RWKV_HEADS, RWKV_HEAD_DIM), ln_b.reshape(RWKV_HEADS, RWKV_HEAD_DIM), GN_EPS)
    bonus = jnp.sum(r * (k2[0] + k2[1]) * rk.astype(f32), axis=-1, keepdims=True) * v
    return ((yn + bonus).reshape(b, n, D_RWKV) * g.astype(f32)).astype(g.dtype)


def conv_module(f, dw_w, dw_b, ln_g, ln_b):
    val, gate = jnp.split(f, 2, axis=-1)
    h = val * jax.nn.sigmoid(gate)
    h = lax.conv_general_dilated(h, dw_w[:, None, :].astype(h.dtype), window_strides=(1,),
                                 padding=((CONV_WIDTH // 2, CONV_WIDTH // 2),),
                                 dimension_numbers=('NWC', 'WIO', 'NWC'),
                                 feature_group_count=D_CONV) + dw_b
    return jax.nn.silu(layernorm(h, ln_g, ln_b, 1e-5))


def rope_2d_tables(n_rows):
    row = jnp.repeat(jnp.arange(n_rows, dtype=jnp.int32), GRID_W)
    col = jnp.tile(jnp.arange(GRID_W, dtype=jnp.int32), n_rows)
    inv = 1.0 / (ROPE_THETA ** (jnp.arange(ROPE_FREQS, dtype=f32) * 2.0 / AXIS_DIM))
    ang = jnp.stack([row, col], axis=-1).astype(f32)[..., None] * inv
    return jnp.cos(ang), jnp.sin(ang)


def apply_rope_2d(t, cos, sin):
    ts = t.astype(f32).reshape(t.shape[:-1] + (2, 2, ROPE_FREQS))
    t1 = ts[..., 0, :]
    t2 = ts[..., 1, :]
    cs = cos[None, :, None, None]
    sn = sin[None, :, None, None]
    out = jnp.stack([t1 * cs - t2 * sn, t2 * cs + t1 * sn], axis=-2)
    return out.reshape(t.shape).astype(t.dtype)


def diff_qkv(f):
    b, n, _ = f.shape
    hq = DIFF_HEADS * 2 * DIFF_QK_DIM
    q = f[..., 0:hq].reshape(b, n, DIFF_HEADS, 2, DIFF_QK_DIM)
    k = f[..., hq:2 * hq].reshape(b, n, DIFF_HEADS, 2, DIFF_QK_DIM)
    v = f[..., 2 * hq:].reshape(b, n, DIFF_HEADS, DIFF_V_DIM)
    return q, k, v


def diff_attend(q, k, v, lam):
    s = jnp.einsum('bqhmd,bkhmd->bhmqk', q, k).astype(f32) * (DIFF_QK_DIM ** -0.5)
    p = jax.nn.softmax(s, axis=-1)
    attn = p[:, :, 0] - lam * p[:, :, 1]
    return jnp.einsum('bhqk,bkhe->bqhe', attn.astype(v.dtype), v)


def diff_heads_out(o, norm_g, lam_init):
    b, n = o.shape[:2]
    return (rmsnorm(o, norm_g, 1e-5) * (1.0 - lam_init)).reshape(b, n, D_DIFF)


def token_mix(hx, hc, layer, w_in, w_out, mu, w0, w2, a0, a2, g2, kk_s, ka_s, rk, lnx_g, lnx_b,
              dw_w, dw_b, cln_g, cln_b, lam_vecs, dnorm_g, cos, sin, need_ctx):
    b, n, _ = hx.shape
    fx = hx @ w_in
    fc = hc @ w_in
    ax, bx, cx = fx[..., :RWKV_IN], fx[..., RWKV_IN:RWKV_IN + CONV_IN], fx[..., RWKV_IN + CONV_IN:]
    ac, bc, cc = fc[..., :RWKV_IN], fc[..., RWKV_IN:RWKV_IN + CONV_IN], fc[..., RWKV_IN + CONV_IN:]

    r_x, v_x, g_x, k_x2, w_x2, av_x, bv_x2 = rwkv_prepare(ax, mu, w0, w2, a0, a2, g2, kk_s, ka_s)
    r_c, v_c, g_c, k_c2, w_c2, av_c, bv_c2 = rwkv_prepare(ac, mu, w0, w2, a0, a2, g2, kk_s, ka_s)
    s0 = jnp.zeros((b, RWKV_HEADS, RWKV_HEAD_DIM, RWKV_HEAD_DIM), f32)
    y_cf, s_cf = wkv_scan(r_c, w_c2[0], k_c2[0], v_c, av_c, bv_c2[0], s0, False)
    y_cb, s_cb = wkv_scan(r_c, w_c2[1], k_c2[1], v_c, av_c, bv_c2[1], s0, True)
    y_xf, _ = wkv_scan(r_x, w_x2[0], k_x2[0], v_x, av_x, bv_x2[0], s_cf, False)
    y_xb, _ = wkv_scan(r_x, w_x2[1], k_x2[1], v_x, av_x, bv_x2[1], s_cb, True)
    out_a_x = rwkv_finish(y_xf + y_xb, r_x, k_x2, v_x, g_x, rk, lnx_g, lnx_b)

    out_b_x = conv_module(bx, dw_w, dw_b, cln_g, cln_b)

    lam_init = 0.8 - 0.6 * math.exp(-0.3 * layer)
    lv = lam_vecs.astype(f32)
    lam = jnp.exp(jnp.sum(lv[0] * lv[1])) - jnp.exp(jnp.sum(lv[2] * lv[3])) + lam_init
    q_x, k_x, v_xa = diff_qkv(cx)
    q_c, k_c, v_ca = diff_qkv(cc)
    q_x = apply_rope_2d(q_x, cos, sin)
    k_x = apply_rope_2d(k_x, cos, sin)
    k_all = jnp.concatenate([k_x, k_c], axis=1)
    v_all = jnp.concatenate([v_xa, v_ca], axis=1)
    nb = n // Q_BLOCK
    qb = jnp.moveaxis(q_x.reshape(b, nb, Q_BLOCK, DIFF_HEADS, 2, DIFF_QK_DIM), 1, 0)
    ob = lax.map(lambda qq: diff_attend(qq, k_all, v_all, lam), qb)
    o_x = jnp.moveaxis(ob, 0, 1).reshape(b, n, DIFF_HEADS, DIFF_V_DIM)
    out_c_x = diff_heads_out(o_x, dnorm_g, lam_init)

    out_x = jnp.concatenate([out_a_x, out_b_x, out_c_x], axis=-1) @ w_out
    if not need_ctx:
        return out_x, None
    out_a_c = rwkv_finish(y_cf + y_cb, r_c, k_c2, v_c, g_c, rk, lnx_g, lnx_b)
    out_b_c = conv_module(bc, dw_w, dw_b, cln_g, cln_b)
    out_c_c = diff_heads_out(diff_attend(q_c, k_c, v_ca, lam), dnorm_g, lam_init)
    out_c = jnp.concatenate([out_a_c, out_b_c, out_c_c], axis=-1) @ w_out
    return out_x, out_c


def setup_inputs(seed: int = 0) -> dict:
    key = jax.random.key(seed)
    ks = jax.random.split(key, 32)

    def nrm(k, shape, scale):
        return jax.random.normal(k, shape, f32) * scale

    return {
        'x': nrm(ks[0], (BATCH, SEQ, D_MODEL), 1.0),
        'c': nrm(ks[1], (BATCH, D_MODEL), 1.0),
        'ctx': nrm(ks[2], (BATCH, CTX_LEN, D_MODEL), 1.0),
        'c_ctx': nrm(ks[3], (D_MODEL,), 1.0),
        'ada_w': nrm(ks[4], (DEPTH, D_MODEL, N_MOD * D_MODEL), 0.5 * D_MODEL ** -0.5),
        'ada_b': nrm(ks[5], (DEPTH, N_MOD * D_MODEL), 0.02),
        'norm_g': 1.0 + nrm(ks[6], (DEPTH, 3, D_MODEL), 0.02),
        'ffn_w_in': nrm(ks[7], (DEPTH, 2, D_MODEL, 2 * D_FF), D_MODEL ** -0.5),
        'ffn_w_out': nrm(ks[8], (DEPTH, 2, D_FF, D_MODEL), D_FF ** -0.5),
        'mix_w_in': nrm(ks[9], (DEPTH, D_MODEL, P_IN), D_MODEL ** -0.5),
        'mix_w_out': nrm(ks[10], (DEPTH, D_MIX, D_MODEL), D_MIX ** -0.5),
        'rwkv_mu': jax.random.uniform(ks[11], (DEPTH, 2, RWKV_IN), f32, 0.0, 0.5),
        'rwkv_w0': jax.random.uniform(ks[12], (DEPTH, 2, D_RWKV), f32, -6.0, 0.0),
        'rwkv_w2': nrm(ks[13], (DEPTH, 2, DECAY_LORA, D_RWKV), 0.5 * DECAY_LORA ** -0.5),
        'rwkv_a0': nrm(ks[14], (DEPTH, 2, D_RWKV), 0.1),
        'rwkv_a2': nrm(ks[15], (DEPTH, 2, AAA_LORA, D_RWKV), 0.5 * AAA_LORA ** -0.5),
        'rwkv_g2': nrm(ks[16], (DEPTH, GATE_LORA, D_RWKV), GATE_LORA ** -0.5),
        'rwkv_kk': 0.85 + nrm(ks[17], (DEPTH, D_RWKV), 0.02),
        'rwkv_ka': 1.0 + nrm(ks[18], (DEPTH, D_RWKV), 0.02),
        'rwkv_rk': nrm(ks[19], (DEPTH, RWKV_HEADS, RWKV_HEAD_DIM), 0.1),
        'rwkv_ln_g': 1.0 + nrm(ks[20], (DEPTH, D_RWKV), 0.02),
        'rwkv_ln_b': nrm(ks[21], (DEPTH, D_RWKV), 0.02),
        'conv_dw_w': nrm(ks[22], (DEPTH, CONV_WIDTH, D_CONV), CONV_WIDTH ** -0.5),
        'conv_dw_b': nrm(ks[23], (DEPTH, D_CONV), 0.02),
        'conv_ln_g': 1.0 + nrm(ks[24], (DEPTH, D_CONV), 0.02),
        'conv_ln_b': nrm(ks[25], (DEPTH, D_CONV), 0.02),
        'diff_lam': nrm(ks[26], (DEPTH, 4, DIFF_QK_DIM), 0.1),
        'diff_norm_g': 1.0 + nrm(ks[27], (DEPTH, DIFF_V_DIM), 0.02),
        'final_g': 1.0 + nrm(ks[28], (D_MODEL,), 0.02),
    }


def reference(x, c, ctx, c_ctx, ada_w, ada_b, norm_g, ffn_w_in, ffn_w_out, mix_w_in, mix_w_out,
              rwkv_mu, rwkv_w0, rwkv_w2, rwkv_a0, rwkv_a2, rwkv_g2, rwkv_kk, rwkv_ka, rwkv_rk,
              rwkv_ln_g, rwkv_ln_b, conv_dw_w, conv_dw_b, conv_ln_g, conv_ln_b,
              diff_lam, diff_norm_g, final_g):
    n_rows = x.shape[1] // GRID_W
    cos, sin = rope_2d_tables(n_rows)
    cond_x = jax.nn.silu(c)
    cond_c = jax.nn.silu(c_ctx)
    for l in range(DEPTH):
        need_ctx = l < DEPTH - 1
        ml = jnp.split((cond_x @ ada_w[l] + ada_b[l])[:, None, :], N_MOD, axis=-1)
        mc = jnp.split((cond_c @ ada_w[l] + ada_b[l])[None, None, :], N_MOD, axis=-1)
        x = x + 0.5 * ml[2] * swiglu(modulate(rmsnorm(x, norm_g[l, 0]), ml[0], ml[1]), ffn_w_in[l, 0], ffn_w_out[l, 0])
        ctx = ctx + 0.5 * mc[2] * swiglu(modulate(rmsnorm(ctx, norm_g[l, 0]), mc[0], mc[1]), ffn_w_in[l, 0], ffn_w_out[l, 0])
        hx = modulate(rmsnorm(x, norm_g[l, 1]), ml[3], ml[4])
        hc = modulate(rmsnorm(ctx, norm_g[l, 1]), mc[3], mc[4])
        ox, oc = token_mix(hx, hc, l, mix_w_in[l], mix_w_out[l], rwkv_mu[l], rwkv_w0[l], rwkv_w2[l],
                           rwkv_a0[l], rwkv_a2[l], rwkv_g2[l], rwkv_kk[l], rwkv_ka[l], rwkv_rk[l],
                           rwkv_ln_g[l], rwkv_ln_b[l], conv_dw_w[l], conv_dw_b[l], conv_ln_g[l], conv_ln_b[l],
                           diff_lam[l], diff_norm_g[l], cos, sin, need_ctx)
        x = x + ml[5] * ox
        x = x + 0.5 * ml[8] * swiglu(modulate(rmsnorm(x, norm_g[l, 2]), ml[6], ml[7]), ffn_w_in[l, 1], ffn_w_out[l, 1])
        if need_ctx:
            ctx = ctx + mc[5] * oc
            ctx = ctx + 0.5 * mc[8] * swiglu(modulate(rmsnorm(ctx, norm_g[l, 2]), mc[6], mc[7]), ffn_w_in[l, 1], ffn_w_out[l, 1])
    return rmsnorm(x, final_g)
```

```python
import contextlib
import math
import numpy as np
import concourse.bass as bass
import concourse.mybir as mybir
from concourse.bass_utils import run_bass_kernel_spmd

F32 = mybir.dt.float32
F32R = mybir.dt.float32r
BF16 = mybir.dt.bfloat16
ALU = mybir.AluOpType
AF = mybir.ActivationFunctionType
AX = mybir.AxisListType

ENG = ("pe", "dve", "act", "pool", "sp")
_ESZ = {F32: 4, F32R: 4, BF16: 2}
EPOCH = 4000

D = 1024
NL = 2048
NCX = 256
T = NL + NCX
DFF = 2816
PIN = 3200
CDEC = math.exp(-0.5)
SKIP = set()
RW_NSEG = 18
RW_STAGE = 99
RW_VAR = ""


def _region(ap):
    t = ap.tensor
    row = 1
    for s in list(t.shape)[1:]:
        row *= int(s)
    off = int(ap.offset)
    pairs = ap.ap
    p0 = off // row
    f0 = off % row
    pstep, pc = pairs[0]
    if pstep == 0:
        pc = 1
    ext = 1
    for st, c in pairs[1:]:
        ext += (int(c) - 1) * abs(int(st))
    esz = _ESZ[ap.dtype]
    b0, b1 = f0 * esz, (f0 + ext) * esz
    if t.name == "ps":
        b0 = (b0 // 2048) * 2048
        b1 = ((b1 + 2047) // 2048) * 2048
    return (t.name, p0, p0 + int(pc), b0, b1)


class Prog:
    def __init__(self, nc, stack):
        self.nc = nc
        self.stack = stack
        self.q = {e: [] for e in ENG}
        self.esem = {e: [] for e in ENG}
        self.ecnt = {e: 0 for e in ENG}
        self.pend = {e: False for e in ENG}
        self.last = {e: None for e in ENG}
        self.pe_base = None
        self.pe_idx = None
        self.waited = {e: {} for e in ENG}
        self.recs = {}
        self.dsem = {}
        self.nsem = 0
        self.n_ins = {e: 0 for e in ENG}
        for e in ENG:
            self._new_epoch(e)

    def _mksem(self, name):
        self.nsem += 1
        return self.stack.enter_context(self.nc.semaphore(name))

    def _new_epoch(self, e):
        s = self._mksem(f"s_{e}_{len(self.esem[e])}")
        self.esem[e].append(s)
        self.ecnt[e] = 0

    def _deps(self, eng, reads, writes, is_dma):
        deps = {}

        def need(r):
            k = id(r[5])
            v = deps.get(k)
            if v is None or v[1] < r[6]:
                deps[k] = (r[5], r[6])

        for ap in reads:
            name, p0, p1, f0, f1 = _region(ap)
            for r in self.recs.get(name, ()):
                if (r[4] or (name == "ps" and r[7] != eng)) and r[0] < p1 and p0 < r[1] and r[2] < f1 and f0 < r[3]:
                    need(r)
        for ap in writes:
            name, p0, p1, f0, f1 = _region(ap)
            for r in self.recs.get(name, ()):
                if r[0] < p1 and p0 < r[1] and r[2] < f1 and f0 < r[3]:
                    if (not is_dma) and r[7] == eng and eng == "pe":
                        continue
                    need(r)
        return deps

    def _record(self, reads, writes, eng, sem, val):
        for ap in writes:
            name, p0, p1, f0, f1 = _region(ap)
            lst = self.recs.setdefault(name, [])
            lst[:] = [r for r in lst if not (p0 <= r[0] and r[1] <= p1 and f0 <= r[2] and r[3] <= f1)]
            lst.append((p0, p1, f0, f1, True, sem, val, eng))
        for ap in reads:
            name, p0, p1, f0, f1 = _region(ap)
            lst = self.recs.setdefault(name, [])
            lst[:] = [r for r in lst if not ((not r[4]) and r[5] is sem and p0 <= r[0] and r[1] <= p1
                                             and f0 <= r[2] and r[3] <= f1)]
            lst.append((p0, p1, f0, f1, False, sem, val, eng))

    def _waits(self, eng, deps):
        w = []
        for k, (sem, val) in deps.items():
            if self.waited[eng].get(k, 0) < val:
                self.waited[eng][k] = val
                w.append((sem, val))
        return w

    def op(self, eng, fn, reads=(), writes=(), inc=True, pe_base=None):
        reads = [a for a in reads if a is not None and not isinstance(a, (int, float))]
        deps = self._deps(eng, reads, writes, False)
        if eng == "pe" and pe_base is not None:
            if self.pe_base is not None and pe_base != self.pe_base and self.pe_idx is not None:
                if self.pend["pe"]:
                    w_, f_, _s, _a = self.q["pe"][self.pe_idx]
                    sem_ = self.esem["pe"][-1]
                    self.ecnt["pe"] += 1
                    self.q["pe"][self.pe_idx] = (w_, f_, sem_, 1)
                    self.pend["pe"] = False
                    self.last["pe"] = (sem_, self.ecnt["pe"])
                ls, lv = self.last["pe"]
                deps[id(ls)] = (ls, max(lv, deps.get(id(ls), (ls, 0))[1]))
            self.pe_base = pe_base
        waits = self._waits(eng, deps)
        sem = self.esem[eng][-1]
        val = self.ecnt[eng] + 1
        if inc:
            self.ecnt[eng] = val
            self.pend[eng] = False
            self.last[eng] = (sem, val)
        else:
            self.pend[eng] = True
        self.q[eng].append((waits, fn, sem if inc else None, 1))
        if eng == "pe":
            self.pe_idx = len(self.q[eng]) - 1
        self.n_ins[eng] += 1
        self._record(reads, writes, eng, sem, val)
        if inc and val >= EPOCH:
            self._new_epoch(eng)

    def dma(self, eng, out, in_, key, reads=(), writes=(), **kw):
        deps = self._deps(eng, reads, writes, True)
        waits = self._waits(eng, deps)
        if key not in self.dsem:
            self.dsem[key] = [self._mksem("d_" + str(key).replace(" ", "")), 0]
        ent = self.dsem[key]
        ent[1] += 16
        sem, val = ent[0], ent[1]
        self.q[eng].append((waits, lambda e: e.dma_start(out=out, in_=in_, **kw), sem, 16))
        self.n_ins[eng] += 1
        self._record(reads, writes, None, sem, val)

    def wait_dma(self, eng, key):
        ent = self.dsem[key]
        self.q[eng].append(([(ent[0], ent[1])], None, None, 0))

    def barrier(self):
        for e in ENG:
            if self.pend[e]:
                self.op(e, lambda x: x.nop(), (), (), inc=True)
        targets = [(e, self.last[e]) for e in ENG if self.last[e] is not None]
        targets = [(e, t[0], t[1]) for (e, t) in targets]
        targets += [(None, s, v) for (s, v) in self.dsem.values()]
        for e in ENG:
            w = []
            for (te, s, v) in targets:
                if te == e:
                    continue
                if self.waited[e].get(id(s), 0) < v:
                    self.waited[e][id(s)] = v
                    w.append((s, v))
            if w:
                self.q[e].append((w, None, None, 0))
        self.recs = {}

    def replay(self):
        nc = self.nc
        with nc.Block() as block:
            def run(engname):
                def body(eobj):
                    for waits, fn, sem, amt in self.q[engname]:
                        for (s, v) in waits:
                            eobj.wait_ge(s, v)
                        if fn is None:
                            continue
                        ins = fn(eobj)
                        if sem is not None:
                            ins.then_inc(sem, amt)
                return body
            block.tensor(run("pe"))
            block.vector(run("dve"))
            block.scalar(run("act"))
            block.gpsimd(run("pool"))
            block.sync(run("sp"))

    def mm(self, out, lhsT, rhs, start=True, stop=True):
        self.op("pe", lambda e: e.matmul(out, lhsT, rhs, start=start, stop=stop),
                [lhsT, rhs], [out], inc=stop, pe_base=(int(lhsT.base_partition()), int(lhsT.partition_size()) > 64))

    def transpose(self, out, in_, ident):
        self.op("pe", lambda e: e.transpose(out, in_, ident), [in_, ident], [out], pe_base=(0, True))

    def tt(self, eng, out, a, b, op):
        self.op(eng, lambda e: e.tensor_tensor(out, a, b, op), [a, b], [out])

    def ts(self, eng, out, a, s1, s2, op0, op1=None):
        if op1 is None:
            self.op(eng, lambda e: e.tensor_scalar(out, a, s1, None, op0), [a, s1], [out])
        else:
            self.op(eng, lambda e: e.tensor_scalar(out, a, s1, s2, op0, op1), [a, s1, s2], [out])

    def stt(self, eng, out, a, s, b, op0, op1):
        self.op(eng, lambda e: e.scalar_tensor_tensor(out, a, s, b, op0, op1), [a, s, b], [out])

    def copy(self, eng, out, a):
        if eng == "act":
            self.op(eng, lambda e: e.copy(out, a), [a], [out])
        else:
            self.op(eng, lambda e: e.tensor_copy(out, a), [a], [out])

    def act(self, out, a, func, bias=None, scale=None):
        kw = {}
        if bias is not None:
            kw["bias"] = bias
        if scale is not None:
            kw["scale"] = scale
        self.op("act", lambda e: e.activation(out, a, func, **kw), [a, bias, scale], [out])

    def memset(self, eng, out, v):
        self.op(eng, lambda e: e.memset(out, v), [], [out])

    def recip(self, out, a):
        self.op("dve", lambda e: e.reciprocal(out, a), [a], [out])

    def scan(self, out, d0, d1):
        self.op("dve", lambda e: e.tensor_tensor_scan(out, d0, d1, 0.0, ALU.mult, ALU.add), [d0, d1], [out])


class _Cols:
    def __init__(self):
        self.off = {}
        self.n = 0

    def add(self, name, n):
        self.off[name] = self.n
        self.n += n
        return self.off[name]


def _prm_layout():
    c = _Cols()
    for l in range(2):
        pre = f"L{l}_"
        for name, n in (("adab", 72), ("ng", 24), ("mu", 18), ("w0", 4), ("a0", 4), ("kk", 2), ("ka", 2),
                        ("rk", 2), ("lng", 2), ("lnb", 2), ("dww", 62), ("dwb", 2), ("clg", 2), ("clb", 2),
                        ("dng", 1), ("lam", 4)):
            c.add(pre + name, n)
    c.add("fg", 8)
    c.add("ident", 128)
    c.add("mask", 256)
    c.add("cond", 16)
    return c


PRM = _prm_layout()


def _fm(v, nchunk):
    return np.ascontiguousarray(np.asarray(v, np.float32).reshape(nchunk, 128).T)


def _pack_prm(inp, b):
    P = np.zeros((128, PRM.n), np.float32)

    def put(name, arr):
        o = PRM.off[name]
        P[: arr.shape[0], o:o + arr.shape[1]] = arr

    for l in range(2):
        pre = f"L{l}_"
        put(pre + "adab", _fm(inp["ada_b"][l], 72))
        put(pre + "ng", np.concatenate([_fm(inp["norm_g"][l, i], 8) for i in range(3)], axis=1))
        put(pre + "mu", np.concatenate([_fm(inp["rwkv_mu"][l, d], 9) for d in range(2)], axis=1))
        put(pre + "w0", np.concatenate([_fm(inp["rwkv_w0"][l, d], 2) for d in range(2)], axis=1))
        put(pre + "a0", np.concatenate([_fm(inp["rwkv_a0"][l, d], 2) for d in range(2)], axis=1))
        put(pre + "kk", _fm(inp["rwkv_kk"][l], 2))
        put(pre + "ka", _fm(inp["rwkv_ka"][l], 2))
        put(pre + "rk", _fm(np.asarray(inp["rwkv_rk"][l]).reshape(256), 2))
        put(pre + "lng", _fm(inp["rwkv_ln_g"][l], 2))
        put(pre + "lnb", _fm(inp["rwkv_ln_b"][l], 2))
        dw = np.asarray(inp["conv_dw_w"][l], np.float32)
        put(pre + "dww", np.concatenate([np.ascontiguousarray(dw[:, hc * 128:(hc + 1) * 128].T) for hc in range(2)], axis=1))
        put(pre + "dwb", _fm(inp["conv_dw_b"][l], 2))
        put(pre + "clg", _fm(inp["conv_ln_g"][l], 2))
        put(pre + "clb", _fm(inp["conv_ln_b"][l], 2))
        put(pre + "dng", _fm(inp["diff_norm_g"][l], 1))
        put(pre + "lam", np.ascontiguousarray(np.asarray(inp["diff_lam"][l], np.float32).T))
    put("fg", _fm(inp["final_g"], 8))
    put("ident", np.eye(128, dtype=np.float32))
    i = np.arange(64)
    lt = (i[:, None] < i[None, :]).astype(np.float32)
    le = (i[:, None] <= i[None, :]).astype(np.float32)
    put("mask", np.concatenate([lt, le, lt.T, le.T], axis=1))
    cond = np.stack([_fm(inp["c"][b], 8), _fm(inp["c_ctx"], 8)], axis=2).reshape(128, 16)
    put("cond", cond)
    return P


def _rope_tables():
    n = np.arange(NL)
    row = (n // 64).astype(np.float32)
    col = (n % 64).astype(np.float32)
    inv = (1.0 / (10000.0 ** (np.arange(16, dtype=np.float32) * 2.0 / 32.0))).astype(np.float32)
    tab = np.zeros((128, 2, T), np.float32)
    tab[:, 0, NL:] = 1.0
    for p in range(128):
        d = p % 64
        axis = d // 32
        half = (d % 32) // 16
        f = d % 16
        pos = row if axis == 0 else col
        ang = (pos * inv[f]).astype(np.float32)
        tab[p, 0, :NL] = np.cos(ang)
        tab[p, 1, :NL] = np.sin(ang) * (-1.0 if half == 0 else 1.0)
    return tab


def _perm_cols():
    idx = np.arange(1024)
    d = idx % 64
    half = (d % 32) // 16
    partner = np.where(half == 0, idx + 16, idx - 16)
    return partner


TILES = [(0, 512), (512, 512), (1024, 512), (1536, 512), (2048, 256)]


def build(n_layers=2, taps=None, stop_at=None):
    nc = bass.Bass("TRN2", target_bir_lowering=False)
    dr = lambda n, s, k="ExternalInput": nc.dram_tensor(n, s, F32, kind=k).ap()
    xT_d = dr("xT", [D, NL])
    cT_d = dr("cT", [D, NCX])
    prm_d = dr("prm", [128, PRM.n])
    rope_d = dr("rope", [128, 2, T])
    lw_d = dr("lw", [2, 128, 768])
    adaw_d = dr("ada_w", [2, D, 9 * D])
    win_d = dr("ffn_w_in", [2, 2, D, 2 * DFF])
    wout_d = dr("ffn_w_out", [2, 2, DFF, D])
    mwin_d = dr("mix_w_in", [2, D, PIN + 1024])
    mwout_d = dr("mix_w_out", [2, D, D])
    out_d = dr("outT", [D, NL], "ExternalOutput")
    tap_list = []
    dbg_d = None
    if taps is not None:
        dbg_d = dr("dbg", [128, taps], "ExternalOutput")
    tap_off = [0]

    with contextlib.ExitStack() as st:
        P = Prog(nc, st)
        sbuf = lambda n, s, d=F32: st.enter_context(nc.sbuf_tensor(n, s, d))
        uctr = [0]

        def un(n):
            uctr[0] += 1
            return "%s_%d" % (n, uctr[0])

        xT = sbuf("xT_s", [128, 8, T])
        prm = sbuf("prm_s", [128, PRM.n])
        modT = sbuf("modT", [128, 72, 2])
        der = sbuf("der", [128, 9, 8, 2])
        condb = sbuf("condb", [128, 8, 2], BF16)
        ones_bf = sbuf("ones_bf", [128, 128], BF16)
        eps_t = sbuf("eps_t", [128, 4])
        slots = [sbuf(f"slot{i}", [128, 8, 1024], BF16) for i in range(2)]
        ps = st.enter_context(nc.psum_tensor("ps", [128, 8, 512], F32))
        slot_ctr = [0]

        def pcol(name, n=1, off=0, rows=128):
            o = PRM.off[name] + off
            return prm[0:rows, o:o + n]

        def tap(name, ap):
            if taps is None:
                return
            shp = ap.shape
            n = 1
            for s_ in shp[1:]:
                n *= int(s_)
            o = tap_off[0]
            assert o + n <= taps, (name, o, n)
            dst = dbg_d[0:shp[0], o:o + n]
            if len(shp) == 3:
                dst = dst.rearrange("p (a b) -> p a b", b=int(shp[2]))
            P.dma("sp", dst, ap, "tap%d" % len(tap_list), reads=[ap])
            tap_list.append((name, o, tuple(int(s_) for s_ in shp)))
            tap_off[0] = o + n

        P.dma("sp", prm[:], prm_d, "prm", writes=[prm[:]])
        for kc in range(8):
            P.dma("sp", xT[:, kc, 0:NL], xT_d[kc * 128:(kc + 1) * 128, :], "xin%d" % kc, writes=[xT[:, kc, 0:NL]])
            P.dma("sp", xT[:, kc, NL:T], cT_d[kc * 128:(kc + 1) * 128, :], "cin%d" % kc, writes=[xT[:, kc, NL:T]])
        P.memset("dve", ones_bf[:], 1.0 / 1024.0)
        P.memset("dve", eps_t[:, 0:1], 1e-6)
        P.memset("dve", eps_t[:, 1:2], 1e-12)
        P.memset("dve", eps_t[:, 2:3], 64e-5)
        P.memset("dve", eps_t[:, 3:4], 1e-5)
        condv = pcol("cond", 16).rearrange("p (k s) -> p k s", s=2)
        P.act(condb[:], condv, AF.Silu)

        def next_slot():
            s_ = slot_ctr[0] % 2
            slot_ctr[0] += 1
            return s_, slots[s_]

        def adaln(l):
            pmod = ps[:, 7, 0:144].rearrange("p (j s) -> p j s", s=2)
            for i in range(9):
                si, sl = next_slot()
                for hf in range(2):
                    P.dma("pool", sl[:, :, hf * 512:(hf + 1) * 512],
                          adaw_d[l, :, i * 1024 + hf * 512: i * 1024 + (hf + 1) * 512].rearrange("(kc p) n -> p kc n", p=128),
                          "slot%d_%d" % (si, hf), writes=[sl[:, :, hf * 512:(hf + 1) * 512]])
                for dc in range(8):
                    j = i * 8 + dc
                    for kc in range(8):
                        P.mm(pmod[:, j, :], sl[:, kc, dc * 128:(dc + 1) * 128], condb[:, kc, :],
                             start=(kc == 0), stop=(kc == 7))
            adab = pcol(f"L{l}_adab", 72)
            P.tt("dve", modT[:], pmod, adab.unsqueeze(2).to_broadcast([128, 72, 2]), ALU.add)
            mv = modT[:].rearrange("p (i k) s -> p i k s", k=8)
            ng = pcol(f"L{l}_ng", 24).rearrange("p (i k) -> p i k", k=8)
            for (oa, ob, os_, isc, ish, ig, gi, half) in ((0, 1, 2, 1, 0, 2, 0, 0.5), (3, 4, 5, 4, 3, 5, 1, 1.0),
                                                         (6, 7, 8, 7, 6, 8, 2, 0.5)):
                P.stt("dve", der[:, oa], mv[:, isc], 1.0, ng[:, gi].unsqueeze(2).to_broadcast([128, 8, 2]),
                      ALU.add, ALU.mult)
                P.copy("dve", der[:, ob], mv[:, ish])
                P.ts("dve", der[:, os_], mv[:, ig], half, None, ALU.mult)

        def norm_stats(rstd, sqb):
            for ti, (c0, n) in enumerate(TILES):
                bank = ti % 2
                for kc in range(8):
                    P.act(sqb[:, kc, 0:n], xT[:, kc, c0:c0 + n], AF.Square)
                for kc in range(8):
                    P.mm(ps[:, bank, 0:n], ones_bf[:], sqb[:, kc, 0:n], start=(kc == 0), stop=(kc == 7))
                P.act(rstd[:, c0:c0 + n], ps[:, bank, 0:n], AF.Sqrt, bias=eps_t[:, 0:1], scale=1.0)
                P.recip(rstd[:, c0:c0 + n], rstd[:, c0:c0 + n])

        def make_h(hout, c0, n, rstd, ia, ib, tmp):
            s_ = 0 if c0 < NL else 1
            for kc in range(8):
                eng = "dve" if kc % 2 == 0 else "pool"
                P.tt(eng, tmp[:, kc % 2, 0:n], xT[:, kc, c0:c0 + n], rstd[:, c0:c0 + n], ALU.mult)
                P.act(hout[:, kc, 0:n], tmp[:, kc % 2, 0:n], AF.Identity,
                      bias=der[:, ib, kc, s_:s_ + 1], scale=der[:, ia, kc, s_:s_ + 1])

        def ffn(l, f):
            ia, ib, isg = (0, 1, 2) if f == 0 else (6, 7, 8)
            with contextlib.ExitStack() as sc:
                sb = lambda n, s, d=F32: sc.enter_context(nc.sbuf_tensor(un(n), s, d))
                rstd = sb("f_rstd", [128, T])
                sqb = sb("f_sqb", [128, 8, 512], BF16)
                hT = sb("f_hT", [128, 8, T], BF16)
                actb = sb("f_act", [128, 4, T], BF16)
                tmp = sb("f_tmp", [128, 2, 512])
                sg = sb("f_sg", [128, 2, 512])
                norm_stats(rstd, sqb)
                for (c0, n) in TILES:
                    make_h(hT[:, :, c0:c0 + n], c0, n, rstd, ia, ib, tmp)
                groups = [(0, 4), (4, 4), (8, 4), (12, 4), (16, 4), (20, 2)]
                for (m0, G) in groups:
                    sa, slA = next_slot()
                    P.dma("pool", slA[:, :, 0:G * 128],
                          win_d[l, f, :, m0 * 128:(m0 + G) * 128].rearrange("(kc p) n -> p kc n", p=128),
                          "slot%d_0" % sa, writes=[slA[:, :, 0:G * 128]])
                    P.dma("pool", slA[:, :, 512:512 + G * 128],
                          win_d[l, f, :, DFF + m0 * 128:DFF + (m0 + G) * 128].rearrange("(kc p) n -> p kc n", p=128),
                          "slot%d_1" % sa, writes=[slA[:, :, 512:512 + G * 128]])
                    sb_i, slB = next_slot()
                    P.dma("pool", slB[:, 0:G, :],
                          wout_d[l, f, m0 * 128:(m0 + G) * 128, :].rearrange("(m p) n -> p m n", p=128),
                          "slot%d_0" % sb_i, writes=[slB[:, 0:G, :]])
                    cnt = 0
                    for (c0, n) in TILES:
                        for mi in range(G):
                            bg = cnt % 2
                            bu = 2 + cnt % 2
                            cnt += 1
                            for kc in range(8):
                                P.mm(ps[:, bg, 0:n], slA[:, kc, mi * 128:(mi + 1) * 128], hT[:, kc, c0:c0 + n],
                                     start=(kc == 0), stop=(kc == 7))
                            for kc in range(8):
                                P.mm(ps[:, bu, 0:n], slA[:, kc, 512 + mi * 128:512 + (mi + 1) * 128], hT[:, kc, c0:c0 + n],
                                     start=(kc == 0), stop=(kc == 7))
                            P.act(sg[:, bg, 0:n], ps[:, bg, 0:n], AF.Silu)
                            P.tt("dve", actb[:, mi, c0:c0 + n], sg[:, bg, 0:n], ps[:, bu, 0:n], ALU.mult)
                    cnt = 0
                    for dc in range(8):
                        for (c0, n) in TILES:
                            s_ = 0 if c0 < NL else 1
                            bo = 4 + cnt % 2
                            cnt += 1
                            for mi in range(G):
                                P.mm(ps[:, bo, 0:n], slB[:, mi, dc * 128:(dc + 1) * 128], actb[:, mi, c0:c0 + n],
                                     start=(mi == 0), stop=(mi == G - 1))
                            P.stt("dve", xT[:, dc, c0:c0 + n], ps[:, bo, 0:n], der[:, isg, dc, s_:s_ + 1],
                                  xT[:, dc, c0:c0 + n], ALU.mult, ALU.add)
                P.barrier()

        def final():
            with contextlib.ExitStack() as sc:
                sb = lambda n, s, d=F32: sc.enter_context(nc.sbuf_tensor(un(n), s, d))
                rstd = sb("z_rstd", [128, T])
                sqb = sb("z_sqb", [128, 8, 512], BF16)
                ot = sb("z_out", [128, 8, NL])
                norm_stats(rstd, sqb)
                for kc in range(8):
                    P.stt("dve", ot[:, kc, :], xT[:, kc, 0:NL], pcol("fg", 1, kc),
                          rstd[:, 0:NL], ALU.mult, ALU.mult)
                    P.dma("sp", out_d[kc * 128:(kc + 1) * 128, :], ot[:, kc, :], "out", reads=[ot[:, kc, :]])
                P.wait_dma("sp", "out")
                P.barrier()

        ones1 = sbuf("ones1", [128, 128], BF16)
        P.memset("dve", ones1[:], 1.0)
        identf = pcol("ident", 128)

        def rwkv(l, rstd, catA):
            pre = f"L{l}_"
            with contextlib.ExitStack() as sc:
                sb = lambda n, s, d=F32: sc.enter_context(nc.sbuf_tensor(un(n), s, d))
                yf = sb("r_yf", [128, 2, T], BF16)
                hTt = sb("r_hTt", [128, 8, 256], BF16)
                tmp = sb("r_tmp", [128, 2, 256])
                raw = sb("r_raw", [128, 2, 260])
                f = sb("r_f", [128, 9, 128])
                W = sb("r_W", [128, 16, 2, 128])
                lwb = sb("r_lwb", [128, 3, 256], BF16)
                lob = sb("r_lob", [128, 3, 128], BF16)
                c0t = sb("r_c0", [128, 9])
                gam = sb("r_gam", [128, 2, 2])
                rmask = sb("r_rmask", [128, 128])
                sq = sb("r_sq", [128, 2, 128], F32R)
                ar = sb("r_ar", [128, 2, 2, 2, 64], F32R)
                bt = sb("r_bt", [128, 2, 128], F32R)
                kt = sb("r_kt", [128, 2, 128], F32R)
                bhT = sb("r_bhT", [64, 2, 256], F32R)
                khT = sb("r_khT", [64, 2, 256], F32R)
                vTp = sb("r_vTp", [64, 2, 4, 128], F32R)
                S1 = sb("r_S1", [64, 2, 4, 128], F32R)
                S2 = sb("r_S2", [64, 2, 4, 128], F32R)
                S3 = sb("r_S3", [64, 2, 4, 64], F32R)
                Na = sb("r_Na", [64, 2, 4, 64], F32R)
                NaT = sb("r_NaT", [64, 2, 4, 64], F32R)
                X = sb("r_X", [64, 2, 4, 64], F32R)
                Zs = sb("r_Zs", [64, 4, 64], F32R)
                Us = sb("r_Us", [64, 4, 128], F32R)
                Hst = sb("r_H", [128, 2, 128], F32R)
                bones = sb("r_bones", [128, 128], F32R)

                Wt = lambda i: W[:, i]
                zt = sb("r_zt", [128, 2, 128])
                P.memset("dve", zt[:], 0.0)
                P.memset("dve", rmask[:], 0.0)
                P.memset("dve", rmask[0:64, 0:64], 1.0)
                P.memset("dve", rmask[64:128, 64:128], 1.0)
                P.copy("dve", bones[:], rmask[:])
                P.memset("dve", rmask[:], 1.0)
                P.memset("dve", rmask[:, 0:1], 0.0)
                P.memset("dve", rmask[:, 64:65], 0.0)
                for c_ in range(2):
                    for h_ in range(4):
                        P.copy("dve", vTp[:, c_, h_, :], zt[0:64, 0, :])
                for h_ in range(4):
                    P.copy("dve", Us[:, h_, :], zt[0:64, 0, :])
                s0i, sl0 = next_slot()
                s1i, sl1 = next_slot()
                for hf in range(2):
                    P.dma("pool", sl0[:, :, hf * 512:(hf + 1) * 512],
                          mwin_d[l, :, hf * 512:(hf + 1) * 512].rearrange("(kc p) n -> p kc n", p=128),
                          "slot%d_%d" % (s0i, hf), writes=[sl0[:, :, hf * 512:(hf + 1) * 512]])
                P.dma("pool", sl1[:, :, 0:128], mwin_d[l, :, 1024:1152].rearrange("(kc p) n -> p kc n", p=128),
                      "slot%d_0" % s1i, writes=[sl1[:, :, 0:128]])
                P.dma("pool", lwb[:], lw_d[l].rearrange("p (i n) -> p i n", n=256), "lwb", writes=[lwb[:]])
                mu = pcol(pre + "mu", 18)
                P.tt("dve", c0t[:], mu[:, 0:9], mu[:, 9:18], ALU.add)
                P.ts("dve", c0t[:], c0t[:], -1.0, 1.0, ALU.mult, ALU.add)

                def v4(ap):
                    return ap.rearrange("p h (c t) -> p h c t", t=64)

                def segment(d, s0):
                    if RW_STAGE < 0:
                        return
                    q0, q1 = (0, NL) if s0 < NL else (NL, T)
                    s1 = s0 + 128
                    lo = s0 - 64 if s0 - 64 >= q0 else q0
                    hi = lo + 256
                    if hi > q1:
                        hi = q1
                        lo = hi - 256
                    ncol = hi - lo
                    off = s0 - lo
                    make_h(hTt, lo, ncol, rstd, 3, 4, tmp)
                    if RW_STAGE < 0.5:
                        return
                    for c in range(9):
                        bank = c % 2
                        for kc in range(8):
                            w = sl0[:, kc, c * 128:(c + 1) * 128] if c < 8 else sl1[:, kc, 0:128]
                            P.mm(ps[:, bank, 0:ncol], w, hTt[:, kc, 0:ncol], start=(kc == 0), stop=(kc == 7))
                        rw = raw[:, bank, :]
                        P.copy("act", rw[:, 1:1 + ncol], ps[:, bank, 0:ncol])
                        if off == 0:
                            P.memset("pool", rw[:, 0:1], 0.0)
                        if off + 128 == ncol:
                            P.memset("pool", rw[:, 1 + ncol:2 + ncol], 0.0)
                        fc = f[:, c, :]
                        P.ts("dve", fc, rw[:, 1 + off:129 + off], c0t[:, c:c + 1], None, ALU.mult)
                        P.ts("pool", tmp[:, 0, 0:128], rw[:, off:128 + off], mu[:, c:c + 1], None, ALU.mult)
                        P.ts("pool", tmp[:, 1, 0:128], rw[:, 2 + off:130 + off], mu[:, 9 + c:10 + c], None, ALU.mult)
                        P.tt("dve", fc, fc, tmp[:, 0, 0:128], ALU.add)
                        P.tt("dve", fc, fc, tmp[:, 1, 0:128], ALU.add)
                    if RW_STAGE < 1:
                        return
                    dr_ = slice(d * 64, (d + 1) * 64)
                    P.act(lob[dr_, 0, :], f[dr_, 6, :], AF.Tanh)
                    P.copy("dve", lob[:, 1, :], f[:, 7, :])
                    if d == 1:
                        P.act(lob[:, 2, :], f[:, 8, :], AF.Sigmoid)
                    sg = Wt(0)
                    for hc in range(2):
                        pm = ps[:, 2, hc * 128:(hc + 1) * 128]
                        P.mm(pm, lwb[dr_, 0, hc * 128:(hc + 1) * 128], lob[dr_, 0, :])
                        P.act(sg[:, hc, :], pm, AF.Sigmoid, bias=pcol(pre + "w0", 1, d * 2 + hc), scale=1.0)
                    a_t = {0: Wt(1), 1: Wt(2)}
                    for dd in ((0,) if d == 0 else (0, 1)):
                        ddr = slice(dd * 64, (dd + 1) * 64)
                        for hc in range(2):
                            pm = ps[:, 2, 256 + hc * 128:256 + (hc + 1) * 128]
                            P.mm(pm, lwb[ddr, 1, hc * 128:(hc + 1) * 128], lob[ddr, 1, :])
                            P.act(a_t[dd][:, hc, :], pm, AF.Sigmoid, bias=pcol(pre + "a0", 1, dd * 2 + hc), scale=1.0)
                    gt = Wt(3)
                    if d == 1:
                        for hc in range(2):
                            pm = ps[:, 2, hc * 128:(hc + 1) * 128]
                            P.mm(pm, lwb[:, 2, hc * 128:(hc + 1) * 128], lob[:, 2, :])
                            P.copy("act", gt[:, hc, :], pm)
                    if RW_STAGE < 2:
                        return
                    kks, kkn, t6 = Wt(4), Wt(5), Wt(6)
                    for hc in range(2):
                        P.ts("dve", kks[:, hc, :], f[:, 2 + hc, :], pcol(pre + "kk", 1, hc), None, ALU.mult)
                    P.act(sq[:], kks, AF.Square)
                    pk = ps[:, 2, 0:256].rearrange("p (h t) -> p h t", t=128)
                    for hc in range(2):
                        P.mm(pk[:, hc, :], bones[:], sq[:, hc, :])
                    P.act(t6, pk, AF.Sqrt, bias=eps_t[:, 1:2], scale=1.0)
                    P.recip(t6, t6)
                    P.tt("dve", kkn, kks, t6, ALU.mult)
                    acur = a_t[d]
                    kd, bv = Wt(7), Wt(8)
                    for hc in range(2):
                        P.ts("dve", t6[:, hc, :], acur[:, hc, :], -1.0, pcol(pre + "ka", 1, hc), ALU.add, ALU.mult)
                    P.stt("dve", kd, t6, 1.0, f[:, 2:4, :], ALU.add, ALU.mult)
                    P.tt("pool", bv, kkn, acur, ALU.mult)
                    if RW_STAGE < 3:
                        return
                    cs, ex, ci, en = Wt(9), Wt(10), Wt(11), Wt(12)
                    for hc in range(2):
                        P.scan(cs[:, hc, :], rmask[:], sg[:, hc, :])
                    csv = v4(cs)
                    tot = csv[:, :, :, 63:64]
                    totb = tot.to_broadcast([128, 2, 2, 64])
                    if d == 0:
                        ci = cs
                        P.tt("dve", ex, cs, sg, ALU.subtract)
                        P.tt("dve", v4(en), totb, csv, ALU.subtract)
                    else:
                        P.tt("dve", v4(ex), totb, csv, ALU.subtract)
                        P.tt("dve", ci, ex, sg, ALU.add)
                        P.tt("dve", en, cs, sg, ALU.subtract)
                    Eex, Ein, Einv, Eend = Wt(6), Wt(13), Wt(14), Wt(15)
                    P.act(Eex, ex, AF.Exp, scale=-CDEC)
                    P.act(Ein, ci, AF.Exp, scale=-CDEC)
                    P.act(Einv, ci, AF.Exp, scale=CDEC)
                    P.act(Eend, en, AF.Exp, scale=-CDEC)
                    P.act(gam[:], csv[:, :, :, 63], AF.Exp, scale=-CDEC)
                    if RW_STAGE < 4:
                        return
                    P.stt("dve", ar[:, :, :, 0, :], v4(kkn), -1.0, v4(Eex), ALU.mult, ALU.mult)
                    P.tt("dve", ar[:, :, :, 1, :], v4(f[:, 0:2, :]), v4(Ein), ALU.mult)
                    P.tt("dve", bt[:], bv, Einv, ALU.mult)
                    P.tt("dve", kt[:], kd, Einv, ALU.mult)
                    bh, kh = Wt(9), Wt(10)
                    P.tt("pool", bh, bv, Eend, ALU.mult)
                    P.tt("pool", kh, kd, Eend, ALU.mult)
                    if RW_STAGE < 5:
                        return
                    for ch in range(2):
                        pt = ps[0:64, 3, :]
                        for hc in range(2):
                            P.transpose(pt[:, hc * 128:(hc + 1) * 128], bh[:, hc, ch * 64:(ch + 1) * 64], identf)
                            P.transpose(pt[:, 256 + hc * 128:256 + (hc + 1) * 128], kh[:, hc, ch * 64:(ch + 1) * 64], identf)
                        if RW_STAGE < 5.2:
                            continue
                        if RW_VAR != "noact":
                            P.copy("act", bhT[:, ch, :], pt[:, 0:256])
                        if RW_VAR != "nodve":
                            P.copy("dve", khT[:, ch, :], pt[:, 256:512])
                        if RW_STAGE < 5.4:
                            continue
                        pv = ps[0:64, 2, 256:512]
                        for hc in range(2):
                            P.transpose(pv[:, hc * 128:(hc + 1) * 128], f[:, 4 + hc, ch * 64:(ch + 1) * 64], identf)
                        if RW_STAGE < 5.6:
                            continue
                        pv4 = pv.rearrange("p (h e i) -> p h e i", e=2, i=64)
                        vt5 = vTp[:, ch].rearrange("p (h e) (s i) -> p h e s i", e=2, i=64)
                        for e in range(2):
                            P.copy("dve" if e == 0 else "act", vt5[:, :, e, e, :], pv4[:, :, e, :])
                    if RW_STAGE < 6:
                        return
                    mk = pcol("mask", 128, 0 if d == 0 else 128, rows=64)
                    mk3 = pcol("mask", 64, 128 if d == 0 else 0, rows=64)
                    p3 = ps[0:64, 6, :].rearrange("p (c h t) -> p c h t", h=4, t=64)
                    for ch in range(2):
                        p1 = ps[0:64, 4, :].rearrange("p (h x) -> p h x", x=128)
                        p2 = ps[0:64, 5, :].rearrange("p (h x) -> p h x", x=128)
                        ar2 = ar[:].rearrange("p a c x t -> p a c (x t)")
                        cs_ = slice(ch * 64, (ch + 1) * 64)
                        for typ, horder in ((0, (0, 2, 1, 3)), (1, (1, 3, 0, 2)), (2, (0, 2, 1, 3))):
                            for h in horder:
                                hc, pr = h // 2, (h % 2) * 64
                                rhs = ar2[pr:pr + 64, hc, ch, :]
                                if typ == 0:
                                    P.mm(p1[:, h, :], bt[pr:pr + 64, hc, cs_], rhs)
                                elif typ == 1:
                                    P.mm(p2[:, h, :], kt[pr:pr + 64, hc, cs_], rhs)
                                else:
                                    P.mm(p3[:, ch, h, :], ar[pr:pr + 64, hc, ch, 0, :], bt[pr:pr + 64, hc, cs_])
                        mkb = mk.unsqueeze(1).to_broadcast([64, 4, 128])
                        if RW_STAGE < 6.1:
                            continue
                        P.tt("dve", S1[:, ch], ps[0:64, 4, :].rearrange("p (h x) -> p h x", x=128), mkb, ALU.mult)
                        if RW_STAGE < 6.2:
                            continue
                        P.tt("dve", S2[:, ch], ps[0:64, 5, :].rearrange("p (h x) -> p h x", x=128), mkb, ALU.mult)
                    if RW_STAGE < 6.4:
                        return
                    P.tt("dve", S3[:].rearrange("p c h t -> p (c h) t"), ps[0:64, 6, :].rearrange("p (g t) -> p g t", t=64),
                         mk3.unsqueeze(1).to_broadcast([64, 8, 64]), ALU.mult)
                    if RW_STAGE < 7:
                        return
                    idb = pcol("ident", 64, rows=64).unsqueeze(1).to_broadcast([64, 8, 64])
                    g8 = lambda ap: ap.rearrange("p c h t -> p (c h) t")
                    P.tt("dve", g8(X[:]), g8(S1[:, :, :, 0:64]), idb, ALU.add)
                    N1 = S1[:, :, :, 0:64]
                    N1T = S3[:]
                    pA = ps[0:64, 4, :].rearrange("p (c h t) -> p c h t", h=4, t=64)
                    pB = ps[0:64, 5, :].rearrange("p (c h t) -> p c h t", h=4, t=64)
                    pC = ps[0:64, 6, :].rearrange("p (c h t) -> p c h t", h=4, t=64)
                    for lev in range(5):
                        last = lev == 4
                        for ch in range(2):
                            for h in range(4):
                                if not last:
                                    P.mm(pA[:, ch, h, :], N1T[:, ch, h, :], N1[:, ch, h, :])
                                P.mm(pB[:, ch, h, :], N1[:, ch, h, :], N1T[:, ch, h, :])
                        if not last:
                            P.copy("act", Na[:], pA)
                        P.copy("dve", NaT[:], pB)
                        for ch in range(2):
                            for h in range(4):
                                P.mm(pC[:, ch, h, :], NaT[:, ch, h, :], X[:, ch, h, :])
                        P.tt("dve", X[:], X[:], pC, ALU.add)
                        N1, N1T = Na[:], NaT[:]
                    if RW_STAGE < 8:
                        return
                    ysum = Wt(4)
                    Us5 = Us[:].rearrange("p (h e) (s i) -> p h e s i", e=2, i=64)
                    for ch in ((0, 1) if d == 0 else (1, 0)):
                        pZ = ps[0:64, 7, 0:256].rearrange("p (h i) -> p h i", i=64)
                        pU = ps[0:64, 7, 256:512].rearrange("p (h i) -> p h i", i=64)
                        for h in (0, 2, 1, 3):
                            hc, e = h // 2, h % 2
                            pr = e * 64
                            P.mm(pZ[:, h, :], ar[pr:pr + 64, hc, ch, 0, :], Hst[pr:pr + 64, hc, pr:pr + 64],
                                 start=True, stop=False)
                            P.mm(pZ[:, h, :], S2[:, ch, h, 0:64], vTp[:, ch, h, pr:pr + 64], start=False, stop=True)
                        P.copy("act", Zs[:], pZ)
                        for h in range(4):
                            P.mm(pU[:, h, :], X[:, ch, h, :], Zs[:, h, :])
                        pU5 = pU.rearrange("p (h e) i -> p h e i", e=2)
                        for e in range(2):
                            P.copy("dve" if e == 0 else "act", Us5[:, :, e, e, :], pU5[:, :, e, :])
                        pY = ps[:, 3, 0:128].rearrange("p (h t) -> p h t", t=64)
                        for hc in range(2):
                            P.mm(pY[:, hc, :], Hst[:, hc, :], ar[:, hc, ch, 1, :], start=True, stop=False)
                            for e in range(2):
                                h = hc * 2 + e
                                P.mm(pY[:, hc, :], Us[:, h, :], S1[:, ch, h, 64:128], start=False, stop=False)
                                P.mm(pY[:, hc, :], vTp[:, ch, h, :], S2[:, ch, h, 64:128], start=False, stop=(e == 1))
                        pH = ps[:, 3, 256:512].rearrange("p (h x) -> p h x", x=128)
                        for hc in range(2):
                            for e in range(2):
                                h = hc * 2 + e
                                P.mm(pH[:, hc, :], bhT[:, ch, hc * 128:(hc + 1) * 128], Us[:, h, :],
                                     start=(e == 0), stop=False)
                                P.mm(pH[:, hc, :], khT[:, ch, hc * 128:(hc + 1) * 128], vTp[:, ch, h, :],
                                     start=False, stop=(e == 1))
                        for hc in range(2):
                            for e in range(2):
                                pr = e * 64
                                P.stt("dve", Hst[pr:pr + 64, hc, pr:pr + 64], Hst[pr:pr + 64, hc, pr:pr + 64],
                                      gam[pr:pr + 64, hc, ch:ch + 1], pH[pr:pr + 64, hc, pr:pr + 64], ALU.mult, ALU.add)
                        tc_ = slice(s0 + ch * 64, s0 + (ch + 1) * 64)
                        if d == 0:
                            P.copy("act", yf[:, :, tc_], pY)
                        else:
                            P.tt("dve", ysum[:, :, ch * 64:(ch + 1) * 64], pY, yf[:, :, tc_], ALU.add)
                    if d == 0:
                        return
                    y = ysum
                    P.copy("act", sq[:], y)
                    pm = ps[:, 2, 0:256].rearrange("p (h t) -> p h t", t=128)
                    pm2 = ps[:, 2, 256:512].rearrange("p (h t) -> p h t", t=128)
                    for hc in range(2):
                        P.mm(pm[:, hc, :], bones[:], sq[:, hc, :])
                    yc = Wt(5)
                    P.stt("dve", yc, pm, -1.0 / 64.0, y, ALU.mult, ALU.add)
                    P.act(sq[:], yc, AF.Square)
                    for hc in range(2):
                        P.mm(pm2[:, hc, :], bones[:], sq[:, hc, :])
                    rs = Wt(6)
                    P.act(rs, pm2, AF.Sqrt, bias=eps_t[:, 2:3], scale=1.0 / 64.0)
                    P.recip(rs, rs)
                    P.tt("dve", yc, yc, rs, ALU.mult)
                    for hc in range(2):
                        P.ts("dve", yc[:, hc, :], yc[:, hc, :], pcol(pre + "lng", 1, hc), pcol(pre + "lnb", 1, hc),
                             ALU.mult, ALU.add)
                    t7 = Wt(7)
                    P.tt("dve", t7, a_t[0], a_t[1], ALU.add)
                    for hc in range(2):
                        P.ts("dve", t7[:, hc, :], t7[:, hc, :], -2.0, pcol(pre + "ka", 1, hc), ALU.add, ALU.mult)
                    P.stt("dve", t7, t7, 2.0, f[:, 2:4, :], ALU.add, ALU.mult)
                    for hc in range(2):
                        P.stt("dve", sq[:, hc, :], f[:, hc, :], pcol(pre + "rk", 1, hc), t7[:, hc, :], ALU.mult, ALU.mult)
                    for hc in range(2):
                        P.mm(pm[:, hc, :], bones[:], sq[:, hc, :])
                    P.tt("dve", t7, pm, f[:, 4:6, :], ALU.mult)
                    P.tt("dve", yc, yc, t7, ALU.add)
                    P.tt("dve", catA[:, :, s0:s0 + 128], yc, gt, ALU.mult)

                for d in range(2):
                    P.copy("dve", Hst[:], zt[:])
                    if d == 0:
                        segs = [NL, NL + 128] + [i * 128 for i in range(16)]
                    else:
                        segs = [NL + 128, NL] + [i * 128 for i in range(15, -1, -1)]
                    for s0 in segs[:RW_NSEG]:
                        segment(d, s0)
                P.barrier()

        def conv(l, rstd, catB):
            pre = f"L{l}_"
            LB, CB = 0, NL + 30
            with contextlib.ExitStack() as sc:
                sb = lambda n, s, d=F32: sc.enter_context(nc.sbuf_tensor(un(n), s, d))
                hTt = sb("c_hTt", [128, 8, 512], BF16)
                tmp = sb("c_tmp", [128, 2, 512])
                hp = sb("c_hp", [128, 2, T + 60])
                acc = sb("c_acc", [128, 2, T])
                wk = sb("c_wk", [128, 2, 512])
                ab = sb("c_ab", [128, 2, 512], F32R)
                o256 = sb("c_o256", [128, 128], F32R)
                P.memset("dve", tmp[:, 0, 0:128], 1.0 / 256.0)
                P.copy("dve", o256[:], tmp[:, 0, 0:128])
                P.memset("pool", hp[:], 0.0)
                si, sl = next_slot()
                P.dma("pool", sl[:, :, 0:512], mwin_d[l, :, 1152:1664].rearrange("(kc p) n -> p kc n", p=128),
                      "slot%d_0" % si, writes=[sl[:, :, 0:512]])
                for (c0, n) in TILES:
                    make_h(hTt, c0, n, rstd, 3, 4, tmp)
                    base = (LB + 15 + c0) if c0 < NL else (CB + 15 + c0 - NL)
                    for hc in range(2):
                        for kc in range(8):
                            P.mm(ps[:, hc, 0:n], sl[:, kc, hc * 128:(hc + 1) * 128], hTt[:, kc, 0:n],
                                 start=(kc == 0), stop=(kc == 7))
                        for kc in range(8):
                            P.mm(ps[:, 2 + hc, 0:n], sl[:, kc, 256 + hc * 128:256 + (hc + 1) * 128], hTt[:, kc, 0:n],
                                 start=(kc == 0), stop=(kc == 7))
                        P.act(wk[:, hc, 0:n], ps[:, 2 + hc, 0:n], AF.Sigmoid)
                        P.tt("dve", hp[:, hc, base:base + n], ps[:, hc, 0:n], wk[:, hc, 0:n], ALU.mult)
                dww = pcol(pre + "dww", 62)
                for hc in range(2):
                    for (a0_, n, pb) in ((0, NL, LB), (NL, NCX, CB)):
                        o = acc[:, hc, a0_:a0_ + n]
                        P.ts("dve", o, hp[:, hc, pb:pb + n], dww[:, hc * 31:hc * 31 + 1], None, ALU.mult)
                        for k in range(1, 31):
                            P.stt("dve", o, hp[:, hc, pb + k:pb + k + n], dww[:, hc * 31 + k:hc * 31 + k + 1], o,
                                  ALU.mult, ALU.add)
                for (c0, n) in TILES:
                    for hc in range(2):
                        P.act(ab[:, hc, 0:n], acc[:, hc, c0:c0 + n], AF.Identity, bias=pcol(pre + "dwb", 1, hc), scale=1.0)
                    for hc in range(2):
                        P.mm(ps[:, 4, 0:n], o256[:], ab[:, hc, 0:n], start=(hc == 0), stop=(hc == 1))
                    for hc in range(2):
                        P.tt("dve", wk[:, hc, 0:n], ab[:, hc, 0:n], ps[:, 4, 0:n], ALU.subtract)
                    P.act(ab[:, :, 0:n], wk[:, :, 0:n], AF.Square)
                    for hc in range(2):
                        P.mm(ps[:, 5, 0:n], o256[:], ab[:, hc, 0:n], start=(hc == 0), stop=(hc == 1))
                    P.act(tmp[:, 0, 0:n], ps[:, 5, 0:n], AF.Sqrt, bias=eps_t[:, 3:4], scale=1.0)
                    P.recip(tmp[:, 0, 0:n], tmp[:, 0, 0:n])
                    for hc in range(2):
                        P.tt("dve", wk[:, hc, 0:n], wk[:, hc, 0:n], tmp[:, 0, 0:n], ALU.mult)
                        P.ts("dve", wk[:, hc, 0:n], wk[:, hc, 0:n], pcol(pre + "clg", 1, hc), pcol(pre + "clb", 1, hc),
                             ALU.mult, ALU.add)
                        P.act(catB[:, hc, c0:c0 + n], wk[:, hc, 0:n], AF.Silu)
                P.barrier()

        def attention(l, rstd, catC):
            pre = f"L{l}_"
            lam_init = 0.8 - 0.6 * math.exp(-0.3 * l)
            with contextlib.ExitStack() as sc:
                sb = lambda n, s, d=F32: sc.enter_context(nc.sbuf_tensor(un(n), s, d))
                rope = sb("a_rope", [128, 2, NL])
                hTt = sb("a_hTt", [128, 8, 512], BF16)
                tmp = sb("a_tmp", [128, 2, 512])
                qT = sb("a_qT", [128, T], BF16)
                kT = sb("a_kT", [128, T], BF16)
                vtok = sb("a_vtok", [128, 18, 128], BF16)
                eT = sb("a_eT", [128, 2, 512], BF16)
                ot = sb("a_ot", [128, 2, 512])
                lamt = sb("a_lam", [128, 8])
                lamb = sb("a_lamb", [128, 2], BF16)
                P.dma("sp", rope[:], rope_d[:, :, 0:NL], "rope", writes=[rope[:]])
                lv = pcol(pre + "lam", 4)
                P.tt("dve", lamb[:, 0:1], lv[:, 0:1], lv[:, 1:2], ALU.mult)
                P.tt("dve", lamb[:, 1:2], lv[:, 2:3], lv[:, 3:4], ALU.mult)
                P.mm(ps[:, 7, 0:2], ones1[:], lamb[:])
                P.act(lamt[:, 0:2], ps[:, 7, 0:2], AF.Exp)
                P.tt("dve", lamt[:, 2:3], lamt[:, 1:2], lamt[:, 0:1], ALU.subtract)
                P.ts("dve", lamt[:, 3:4], lamt[:, 2:3], -lam_init, None, ALU.add)
                P.ts("dve", lamt[:, 4:5], pcol(pre + "dng", 1), 1.0 - lam_init, None, ALU.mult)
                neglam = lamt[:, 3:4]
                dsc = lamt[:, 4:5]
                for h in range(4):
                    si, sl = next_slot()
                    srcs = (1664 + h * 128, PIN + h * 128, 2176 + h * 128, PIN + 512 + h * 128, 2688 + h * 128)
                    for j, c_ in enumerate(srcs):
                        P.dma("pool", sl[:, :, j * 128:(j + 1) * 128],
                              mwin_d[l, :, c_:c_ + 128].rearrange("(kc p) n -> p kc n", p=128),
                              "slot%d_%d" % (si, j), writes=[sl[:, :, j * 128:(j + 1) * 128]])
                    for (c0, n) in TILES:
                        make_h(hTt, c0, n, rstd, 3, 4, tmp)
                        for j in range(5):
                            for kc in range(8):
                                P.mm(ps[:, j, 0:n], sl[:, kc, j * 128:(j + 1) * 128], hTt[:, kc, 0:n],
                                     start=(kc == 0), stop=(kc == 7))
                        if c0 < NL:
                            for (dst, j) in ((qT, 0), (kT, 2)):
                                P.tt("dve", ot[:, 0, 0:n], ps[:, j, 0:n], rope[:, 0, c0:c0 + n], ALU.mult)
                                P.tt("dve", ot[:, 1, 0:n], ps[:, j + 1, 0:n], rope[:, 1, c0:c0 + n], ALU.mult)
                                P.tt("pool", dst[:, c0:c0 + n], ot[:, 0, 0:n], ot[:, 1, 0:n], ALU.add)
                        else:
                            P.copy("act", qT[:, c0:c0 + n], ps[:, 0, 0:n])
                            P.copy("act", kT[:, c0:c0 + n], ps[:, 2, 0:n])
                        P.copy("act", tmp[:, 0, 0:n], ps[:, 4, 0:n])
                        for b_ in range(n // 128):
                            P.transpose(ps[:, 5, b_ * 128:(b_ + 1) * 128], tmp[:, 0, b_ * 128:(b_ + 1) * 128], identf)
                        kc0 = c0 // 128
                        P.copy("dve", vtok[:, kc0:kc0 + n // 128, :],
                               ps[:, 5, 0:n].rearrange("p (b e) -> p b e", e=128))
                    qgroups = [(c0, n, list(range(18))) for (c0, n) in TILES[:4]] + [(NL, NCX, [16, 17])]
                    for (qc0, qn, kchunks) in qgroups:
                        for m in range(2):
                            mr = slice(m * 64, (m + 1) * 64)
                            pO = ps[:, 2 + 2 * m, 0:qn]
                            pZ = ps[:, 3 + 2 * m, 0:qn]
                            nk = len(kchunks)
                            for i, kc in enumerate(kchunks):
                                sb_ = ps[:, i % 2, 0:qn]
                                P.mm(sb_, kT[mr, kc * 128:(kc + 1) * 128], qT[mr, qc0:qc0 + qn])
                                P.act(eT[:, i % 2, 0:qn], sb_, AF.Exp, scale=0.125)
                                P.mm(pO, vtok[:, kc, :], eT[:, i % 2, 0:qn], start=(i == 0), stop=(i == nk - 1))
                                P.mm(pZ, ones1[:], eT[:, i % 2, 0:qn], start=(i == 0), stop=(i == nk - 1))
                            P.recip(ot[:, m, 0:qn], pZ)
                            P.tt("dve", ot[:, m, 0:qn], pO, ot[:, m, 0:qn], ALU.mult)
                        o = tmp[:, 0, 0:qn]
                        P.stt("dve", o, ot[:, 1, 0:qn], neglam, ot[:, 0, 0:qn], ALU.mult, ALU.add)
                        P.act(eT[:, 0, 0:qn], o, AF.Square)
                        P.mm(ps[:, 6, 0:qn], ones1[:], eT[:, 0, 0:qn])
                        P.act(tmp[:, 1, 0:qn], ps[:, 6, 0:qn], AF.Sqrt, bias=eps_t[:, 3:4], scale=1.0 / 128.0)
                        P.recip(tmp[:, 1, 0:qn], tmp[:, 1, 0:qn])
                        P.stt("dve", catC[:, h, qc0:qc0 + qn], o, dsc, tmp[:, 1, 0:qn], ALU.mult, ALU.mult)
                P.barrier()

        def mixer(l):
            with contextlib.ExitStack() as sc:
                sb = lambda n, s, d=F32: sc.enter_context(nc.sbuf_tensor(un(n), s, d))
                rstd = sb("m_rstd", [128, T])
                catA = sb("m_catA", [128, 2, T], BF16)
                with nc.sbuf_tensor(un("m_sqb"), [128, 8, 512], BF16) as sqb:
                    norm_stats(rstd, sqb)
                    P.barrier()
                if "rwkv" not in SKIP:
                    rwkv(l, rstd, catA)
                catB = sb("m_catB", [128, 2, T], BF16)
                if "conv" not in SKIP:
                    conv(l, rstd, catB)
                catC = sb("m_catC", [128, 4, T], BF16)
                if "attn" not in SKIP:
                    attention(l, rstd, catC)
                if taps is not None and l == 0:
                    dt_ = sb("m_dbg", [128, T])
                    for kc in range(8):
                        src = catA[:, kc] if kc < 2 else (catB[:, kc - 2] if kc < 4 else catC[:, kc - 4])
                        P.copy("dve", dt_[:], src)
                        tap("cat%d" % kc, dt_[:])
                si, sl = next_slot()
                for hf in range(2):
                    P.dma("pool", sl[:, :, hf * 512:(hf + 1) * 512],
                          mwout_d[l, :, hf * 512:(hf + 1) * 512].rearrange("(kc p) n -> p kc n", p=128),
                          "slot%d_%d" % (si, hf), writes=[sl[:, :, hf * 512:(hf + 1) * 512]])
                cnt = 0
                for dc in range(8):
                    for (c0, n) in TILES:
                        s_ = 0 if c0 < NL else 1
                        bo = cnt % 2
                        cnt += 1
                        for kc in range(8):
                            src = catA[:, kc] if kc < 2 else (catB[:, kc - 2] if kc < 4 else catC[:, kc - 4])
                            P.mm(ps[:, bo, 0:n], sl[:, kc, dc * 128:(dc + 1) * 128], src[:, c0:c0 + n],
                                 start=(kc == 0), stop=(kc == 7))
                        P.stt("dve", xT[:, dc, c0:c0 + n], ps[:, bo, 0:n], der[:, 5, dc, s_:s_ + 1],
                              xT[:, dc, c0:c0 + n], ALU.mult, ALU.add)
                P.barrier()

        stages = []
        for l in range(n_layers):
            stages += [("ada%d" % l, lambda l=l: adaln(l)), ("ffa%d" % l, lambda l=l: ffn(l, 0)),
                       ("mix%d" % l, lambda l=l: mixer(l)), ("ffb%d" % l, lambda l=l: ffn(l, 1))]
        for name, fn in stages:
            fn()
            if stop_at == name:
                break
        if taps is not None:
            for kc in range(8):
                tap("x_kc%d" % kc, xT[:, kc, :])
        final()
        if taps is not None:
            for i in range(len(tap_list)):
                P.wait_dma("sp", "tap%d" % i)
        P.replay()
        build.info = dict(n_ins=dict(P.n_ins), nsem=P.nsem, taps=tap_list)
    return nc


_NC_CACHE = {}


def _host_inputs(inp):
    inp = {k: np.asarray(v) for k, v in inp.items()}
    perm = _perm_cols()
    mw = np.asarray(inp["mix_w_in"], np.float32)
    mw_ext = np.concatenate([mw, mw[:, :, 1664 + perm]], axis=2)
    mw_ext = np.ascontiguousarray(mw_ext)
    rope = _rope_tables()
    lw = np.stack([np.concatenate([np.asarray(inp["rwkv_w2"][l], np.float32).reshape(128, 256),
                                   np.asarray(inp["rwkv_a2"][l], np.float32).reshape(128, 256),
                                   np.asarray(inp["rwkv_g2"][l], np.float32)], axis=1) for l in range(2)], axis=0)
    shared = {
        "rope": rope,
        "lw": np.ascontiguousarray(lw),
        "ada_w": np.ascontiguousarray(inp["ada_w"], np.float32),
        "ffn_w_in": np.ascontiguousarray(inp["ffn_w_in"], np.float32),
        "ffn_w_out": np.ascontiguousarray(inp["ffn_w_out"], np.float32),
        "mix_w_in": mw_ext,
        "mix_w_out": np.ascontiguousarray(inp["mix_w_out"], np.float32),
    }
    maps = []
    for b in range(8):
        m = dict(shared)
        m["xT"] = np.ascontiguousarray(np.asarray(inp["x"][b], np.float32).T)
        m["cT"] = np.ascontiguousarray(np.asarray(inp["ctx"][b], np.float32).T)
        m["prm"] = _pack_prm(inp, b)
        maps.append(m)
    return maps


def kernel(**inputs):
    maps = _host_inputs(inputs)
    if "nc" not in _NC_CACHE:
        _NC_CACHE["nc"] = build()
    nc = _NC_CACHE["nc"]
    res = run_bass_kernel_spmd(nc, maps, core_ids=list(range(8)))
    out = np.stack([np.ascontiguousarray(res.results[b]["outT"].T) for b in range(8)], axis=0)
    return out.astype(np.float32)
```

```python
import contextlib
import math
import numpy as np
import concourse.bass as bass
import concourse.mybir as mybir
from concourse.bass_utils import run_bass_kernel_spmd

F32 = mybir.dt.float32
F32R = mybir.dt.float32r
BF16 = mybir.dt.bfloat16
ALU = mybir.AluOpType
AF = mybir.ActivationFunctionType
AX = mybir.AxisListType

ENG = ("pe", "dve", "act", "pool", "sp")
_ESZ = {F32: 4, F32R: 4, BF16: 2}
EPOCH = 4000

D = 1024
NL = 2048
NCX = 256
T = NL + NCX
DFF = 2816
PIN = 3200
CDEC = math.exp(-0.5)
SKIP = set()
RW_NSEG = 18
RW_STAGE = 99
RW_VAR = ""


def _region(ap):
    t = ap.tensor
    row = 1
    for s in list(t.shape)[1:]:
        row *= int(s)
    off = int(ap.offset)
    pairs = ap.ap
    p0 = off // row
    f0 = off % row
    pstep, pc = pairs[0]
    if pstep == 0:
        pc = 1
    ext = 1
    for st, c in pairs[1:]:
        ext += (int(c) - 1) * abs(int(st))
    esz = _ESZ[ap.dtype]
    b0, b1 = f0 * esz, (f0 + ext) * esz
    if t.name == "ps":
        b0 = (b0 // 2048) * 2048
        b1 = ((b1 + 2047) // 2048) * 2048
    return (t.name, p0, p0 + int(pc), b0, b1)


class Prog:
    def __init__(self, nc, stack):
        self.nc = nc
        self.stack = stack
        self.q = {e: [] for e in ENG}
        self.esem = {e: [] for e in ENG}
        self.ecnt = {e: 0 for e in ENG}
        self.pend = {e: False for e in ENG}
        self.last = {e: None for e in ENG}
        self.pe_base = None
        self.pe_idx = None
        self.waited = {e: {} for e in ENG}
        self.recs = {}
        self.dsem = {}
        self.nsem = 0
        self.allsems = []
        self.n_ins = {e: 0 for e in ENG}
        for e in ENG:
            self._new_epoch(e)

    def _mksem(self, name):
        self.nsem += 1
        h = self.nc.alloc_semaphore(name=name)
        self.allsems.append(h)
        return h

    def _new_epoch(self, e):
        s = self._mksem(f"s_{e}_{len(self.esem[e])}")
        self.esem[e].append(s)
        self.ecnt[e] = 0

    def _deps(self, eng, reads, writes, is_dma):
        deps = {}

        def need(r):
            k = id(r[5])
            v = deps.get(k)
            if v is None or v[1] < r[6]:
                deps[k] = (r[5], r[6])

        for ap in reads:
            name, p0, p1, f0, f1 = _region(ap)
            for r in self.recs.get(name, ()):
                if (r[4] or (name == "ps" and r[7] != eng)) and r[0] < p1 and p0 < r[1] and r[2] < f1 and f0 < r[3]:
                    need(r)
        for ap in writes:
            name, p0, p1, f0, f1 = _region(ap)
            for r in self.recs.get(name, ()):
                if r[0] < p1 and p0 < r[1] and r[2] < f1 and f0 < r[3]:
                    if (not is_dma) and r[7] == eng and eng == "pe":
                        continue
                    need(r)
        return deps

    def _record(self, reads, writes, eng, sem, val):
        for ap in writes:
            name, p0, p1, f0, f1 = _region(ap)
            lst = self.recs.setdefault(name, [])
            lst[:] = [r for r in lst if not (p0 <= r[0] and r[1] <= p1 and f0 <= r[2] and r[3] <= f1)]
            lst.append((p0, p1, f0, f1, True, sem, val, eng))
        for ap in reads:
            name, p0, p1, f0, f1 = _region(ap)
            lst = self.recs.setdefault(name, [])
            lst[:] = [r for r in lst if not ((not r[4]) and r[5] is sem and p0 <= r[0] and r[1] <= p1
                                             and f0 <= r[2] and r[3] <= f1)]
            lst.append((p0, p1, f0, f1, False, sem, val, eng))

    def _waits(self, eng, deps):
        w = []
        for k, (sem, val) in deps.items():
            if self.waited[eng].get(k, 0) < val:
                self.waited[eng][k] = val
                w.append((sem, val))
        return w

    def op(self, eng, fn, reads=(), writes=(), inc=True, pe_base=None):
        reads = [a for a in reads if a is not None and not isinstance(a, (int, float))]
        deps = self._deps(eng, reads, writes, False)
        if eng == "pe" and pe_base is not None:
            if self.pe_base is not None and pe_base != self.pe_base and self.pe_idx is not None:
                if self.pend["pe"]:
                    w_, f_, _s, _a = self.q["pe"][self.pe_idx]
                    sem_ = self.esem["pe"][-1]
                    self.ecnt["pe"] += 1
                    self.q["pe"][self.pe_idx] = (w_, f_, sem_, 1)
                    self.pend["pe"] = False
                    self.last["pe"] = (sem_, self.ecnt["pe"])
                ls, lv = self.last["pe"]
                deps[id(ls)] = (ls, max(lv, deps.get(id(ls), (ls, 0))[1]))
            self.pe_base = pe_base
        waits = self._waits(eng, deps)
        sem = self.esem[eng][-1]
        val = self.ecnt[eng] + 1
        if inc:
            self.ecnt[eng] = val
            self.pend[eng] = False
            self.last[eng] = (sem, val)
        else:
            self.pend[eng] = True
        self.q[eng].append((waits, fn, sem if inc else None, 1))
        if eng == "pe":
            self.pe_idx = len(self.q[eng]) - 1
        self.n_ins[eng] += 1
        self._record(reads, writes, eng, sem, val)
        if inc and val >= EPOCH:
            self._new_epoch(eng)

    def dma(self, eng, out, in_, key, reads=(), writes=(), **kw):
        deps = self._deps(eng, reads, writes, True)
        waits = self._waits(eng, deps)
        if key not in self.dsem:
            self.dsem[key] = [self._mksem("d_" + str(key).replace(" ", "")), 0]
        ent = self.dsem[key]
        ent[1] += 16
        sem, val = ent[0], ent[1]
        self.q[eng].append((waits, lambda e: e.dma_start(out=out, in_=in_, **kw), sem, 16))
        self.n_ins[eng] += 1
        self._record(reads, writes, None, sem, val)

    def wait_dma(self, eng, key):
        ent = self.dsem[key]
        self.q[eng].append(([(ent[0], ent[1])], None, None, 0))

    def barrier(self):
        for e in ENG:
            if self.pend[e]:
                self.op(e, lambda x: x.nop(), (), (), inc=True)
        targets = [(e, self.last[e]) for e in ENG if self.last[e] is not None]
        targets = [(e, t[0], t[1]) for (e, t) in targets]
        targets += [(None, s, v) for (s, v) in self.dsem.values()]
        for e in ENG:
            w = []
            for (te, s, v) in targets:
                if te == e:
                    continue
                if self.waited[e].get(id(s), 0) < v:
                    self.waited[e][id(s)] = v
                    w.append((s, v))
            if w:
                self.q[e].append((w, None, None, 0))
        self.recs = {}

    def replay(self):
        nc = self.nc
        for h in self.allsems:
            nc.gpsimd.sem_clear(h)
        nc.all_engine_barrier()
        with nc.Block() as block:
            def run(engname):
                def body(eobj):
                    for waits, fn, sem, amt in self.q[engname]:
                        for (s, v) in waits:
                            eobj.wait_ge(s, v)
                        if fn is None:
                            continue
                        ins = fn(eobj)
                        if sem is not None:
                            ins.then_inc(sem, amt)
                return body
            block.tensor(run("pe"))
            block.vector(run("dve"))
            block.scalar(run("act"))
            block.gpsimd(run("pool"))
            block.sync(run("sp"))

    def mm(self, out, lhsT, rhs, start=True, stop=True):
        self.op("pe", lambda e: e.matmul(out, lhsT, rhs, start=start, stop=stop),
                [lhsT, rhs], [out], inc=stop, pe_base=(int(lhsT.base_partition()), int(lhsT.partition_size()) > 64))

    def transpose(self, out, in_, ident):
        self.op("pe", lambda e: e.transpose(out, in_, ident), [in_, ident], [out], pe_base=(0, True))

    def tt(self, eng, out, a, b, op):
        self.op(eng, lambda e: e.tensor_tensor(out, a, b, op), [a, b], [out])

    def ts(self, eng, out, a, s1, s2, op0, op1=None):
        if op1 is None:
            self.op(eng, lambda e: e.tensor_scalar(out, a, s1, None, op0), [a, s1], [out])
        else:
            self.op(eng, lambda e: e.tensor_scalar(out, a, s1, s2, op0, op1), [a, s1, s2], [out])

    def stt(self, eng, out, a, s, b, op0, op1):
        self.op(eng, lambda e: e.scalar_tensor_tensor(out, a, s, b, op0, op1), [a, s, b], [out])

    def copy(self, eng, out, a):
        if eng == "act":
            self.op(eng, lambda e: e.copy(out, a), [a], [out])
        else:
            self.op(eng, lambda e: e.tensor_copy(out, a), [a], [out])

    def act(self, out, a, func, bias=None, scale=None):
        kw = {}
        if bias is not None:
            kw["bias"] = bias
        if scale is not None:
            kw["scale"] = scale
        self.op("act", lambda e: e.activation(out, a, func, **kw), [a, bias, scale], [out])

    def memset(self, eng, out, v):
        self.op(eng, lambda e: e.memset(out, v), [], [out])

    def recip(self, out, a):
        self.op("dve", lambda e: e.reciprocal(out, a), [a], [out])

    def scan(self, out, d0, d1):
        self.op("dve", lambda e: e.tensor_tensor_scan(out, d0, d1, 0.0, ALU.mult, ALU.add), [d0, d1], [out])


class _Cols:
    def __init__(self):
        self.off = {}
        self.n = 0

    def add(self, name, n):
        self.off[name] = self.n
        self.n += n
        return self.off[name]


def _prm_layout():
    c = _Cols()
    for l in range(2):
        pre = f"L{l}_"
        for name, n in (("adab", 72), ("ng", 24), ("mu", 18), ("w0", 4), ("a0", 4), ("kk", 2), ("ka", 2),
                        ("rk", 2), ("lng", 2), ("lnb", 2), ("dww", 62), ("dwb", 2), ("clg", 2), ("clb", 2),
                        ("dng", 1), ("lam", 4)):
            c.add(pre + name, n)
    c.add("fg", 8)
    c.add("ident", 128)
    c.add("mask", 256)
    c.add("cond", 16)
    return c


PRM = _prm_layout()


def _fm(v, nchunk):
    return np.ascontiguousarray(np.asarray(v, np.float32).reshape(nchunk, 128).T)


def _pack_prm(inp, b):
    P = np.zeros((128, PRM.n), np.float32)

    def put(name, arr):
        o = PRM.off[name]
        P[: arr.shape[0], o:o + arr.shape[1]] = arr

    for l in range(2):
        pre = f"L{l}_"
        put(pre + "adab", _fm(inp["ada_b"][l], 72))
        put(pre + "ng", np.concatenate([_fm(inp["norm_g"][l, i], 8) for i in range(3)], axis=1))
        put(pre + "mu", np.concatenate([_fm(inp["rwkv_mu"][l, d], 9) for d in range(2)], axis=1))
        put(pre + "w0", np.concatenate([_fm(inp["rwkv_w0"][l, d], 2) for d in range(2)], axis=1))
        put(pre + "a0", np.concatenate([_fm(inp["rwkv_a0"][l, d], 2) for d in range(2)], axis=1))
        put(pre + "kk", _fm(inp["rwkv_kk"][l], 2))
        put(pre + "ka", _fm(inp["rwkv_ka"][l], 2))
        put(pre + "rk", _fm(np.asarray(inp["rwkv_rk"][l]).reshape(256), 2))
        put(pre + "lng", _fm(inp["rwkv_ln_g"][l], 2))
        put(pre + "lnb", _fm(inp["rwkv_ln_b"][l], 2))
        dw = np.asarray(inp["conv_dw_w"][l], np.float32)
        put(pre + "dww", np.concatenate([np.ascontiguousarray(dw[:, hc * 128:(hc + 1) * 128].T) for hc in range(2)], axis=1))
        put(pre + "dwb", _fm(inp["conv_dw_b"][l], 2))
        put(pre + "clg", _fm(inp["conv_ln_g"][l], 2))
        put(pre + "clb", _fm(inp["conv_ln_b"][l], 2))
        put(pre + "dng", _fm(inp["diff_norm_g"][l], 1))
        put(pre + "lam", np.ascontiguousarray(np.asarray(inp["diff_lam"][l], np.float32).T))
    put("fg", _fm(inp["final_g"], 8))
    put("ident", np.eye(128, dtype=np.float32))
    i = np.arange(64)
    lt = (i[:, None] < i[None, :]).astype(np.float32)
    le = (i[:, None] <= i[None, :]).astype(np.float32)
    put("mask", np.concatenate([lt, le, lt.T, le.T], axis=1))
    cond = np.stack([_fm(inp["c"][b], 8), _fm(inp["c_ctx"], 8)], axis=2).reshape(128, 16)
    put("cond", cond)
    return P


def _rope_tables():
    n = np.arange(NL)
    row = (n // 64).astype(np.float32)
    col = (n % 64).astype(np.float32)
    inv = (1.0 / (10000.0 ** (np.arange(16, dtype=np.float32) * 2.0 / 32.0))).astype(np.float32)
    tab = np.zeros((128, 2, T), np.float32)
    tab[:, 0, NL:] = 1.0
    for p in range(128):
        d = p % 64
        axis = d // 32
        half = (d % 32) // 16
        f = d % 16
        pos = row if axis == 0 else col
        ang = (pos * inv[f]).astype(np.float32)
        tab[p, 0, :NL] = np.cos(ang)
        tab[p, 1, :NL] = np.sin(ang) * (-1.0 if half == 0 else 1.0)
    return tab


def _perm_cols():
    idx = np.arange(1024)
    d = idx % 64
    half = (d % 32) // 16
    partner = np.where(half == 0, idx + 16, idx - 16)
    return partner


TILES = [(0, 512), (512, 512), (1024, 512), (1536, 512), (2048, 256)]


def build(n_layers=2, taps=None, stop_at=None):
    nc = bass.Bass("TRN2", target_bir_lowering=False)
    dr = lambda n, s, k="ExternalInput": nc.dram_tensor(n, s, F32, kind=k).ap()
    xT_d = dr("xT", [D, NL])
    cT_d = dr("cT", [D, NCX])
    prm_d = dr("prm", [128, PRM.n])
    rope_d = dr("rope", [128, 2, T])
    lw_d = dr("lw", [2, 128, 768])
    adaw_d = dr("ada_w", [2, D, 9 * D])
    win_d = dr("ffn_w_in", [2, 2, D, 2 * DFF])
    wout_d = dr("ffn_w_out", [2, 2, DFF, D])
    mwin_d = dr("mix_w_in", [2, D, PIN + 1024])
    mwout_d = dr("mix_w_out", [2, D, D])
    out_d = dr("outT", [D, NL], "ExternalOutput")
    tap_list = []
    dbg_d = None
    if taps is not None:
        dbg_d = dr("dbg", [128, taps], "ExternalOutput")
    tap_off = [0]

    with nc.cleanup_on_exit(), contextlib.ExitStack() as st:
        P = Prog(nc, st)
        sbuf = lambda n, s, d=F32: st.enter_context(nc.sbuf_tensor(n, s, d))
        uctr = [0]

        def un(n):
            uctr[0] += 1
            return "%s_%d" % (n, uctr[0])

        xT = sbuf("xT_s", [128, 8, T])
        prm = sbuf("prm_s", [128, PRM.n])
        modT = sbuf("modT", [128, 72, 2])
        der = sbuf("der", [128, 9, 8, 2])
        condb = sbuf("condb", [128, 8, 2], BF16)
        ones_bf = sbuf("ones_bf", [128, 128], BF16)
        eps_t = sbuf("eps_t", [128, 4])
        slots = [sbuf(f"slot{i}", [128, 8, 1024], BF16) for i in range(2)]
        ps = st.enter_context(nc.psum_tensor("ps", [128, 8, 512], F32))
        slot_ctr = [0]

        def pcol(name, n=1, off=0, rows=128):
            o = PRM.off[name] + off
            return prm[0:rows, o:o + n]

        def tap(name, ap):
            if taps is None:
                return
            shp = ap.shape
            n = 1
            for s_ in shp[1:]:
                n *= int(s_)
            o = tap_off[0]
            assert o + n <= taps, (name, o, n)
            dst = dbg_d[0:shp[0], o:o + n]
            if len(shp) == 3:
                dst = dst.rearrange("p (a b) -> p a b", b=int(shp[2]))
            P.dma("sp", dst, ap, "tap%d" % len(tap_list), reads=[ap])
            tap_list.append((name, o, tuple(int(s_) for s_ in shp)))
            tap_off[0] = o + n

        P.dma("sp", prm[:], prm_d, "prm", writes=[prm[:]])
        for kc in range(8):
            P.dma("sp", xT[:, kc, 0:NL], xT_d[kc * 128:(kc + 1) * 128, :], "xin%d" % kc, writes=[xT[:, kc, 0:NL]])
            P.dma("sp", xT[:, kc, NL:T], cT_d[kc * 128:(kc + 1) * 128, :], "cin%d" % kc, writes=[xT[:, kc, NL:T]])
        P.memset("dve", ones_bf[:], 1.0 / 1024.0)
        P.memset("dve", eps_t[:, 0:1], 1e-6)
        P.memset("dve", eps_t[:, 1:2], 1e-12)
        P.memset("dve", eps_t[:, 2:3], 64e-5)
        P.memset("dve", eps_t[:, 3:4], 1e-5)
        condv = pcol("cond", 16).rearrange("p (k s) -> p k s", s=2)
        P.act(condb[:], condv, AF.Silu)

        def next_slot():
            s_ = slot_ctr[0] % 2
            slot_ctr[0] += 1
            return s_, slots[s_]

        def adaln(l):
            pmod = ps[:, 7, 0:144].rearrange("p (j s) -> p j s", s=2)
            for i in range(9):
                si, sl = next_slot()
                for hf in range(2):
                    P.dma("pool", sl[:, :, hf * 512:(hf + 1) * 512],
                          adaw_d[l, :, i * 1024 + hf * 512: i * 1024 + (hf + 1) * 512].rearrange("(kc p) n -> p kc n", p=128),
                          "slot%d_%d" % (si, hf), writes=[sl[:, :, hf * 512:(hf + 1) * 512]])
                for dc in range(8):
                    j = i * 8 + dc
                    for kc in range(8):
                        P.mm(pmod[:, j, :], sl[:, kc, dc * 128:(dc + 1) * 128], condb[:, kc, :],
                             start=(kc == 0), stop=(kc == 7))
            adab = pcol(f"L{l}_adab", 72)
            P.tt("dve", modT[:], pmod, adab.unsqueeze(2).to_broadcast([128, 72, 2]), ALU.add)
            mv = modT[:].rearrange("p (i k) s -> p i k s", k=8)
            ng = pcol(f"L{l}_ng", 24).rearrange("p (i k) -> p i k", k=8)
            for (oa, ob, os_, isc, ish, ig, gi, half) in ((0, 1, 2, 1, 0, 2, 0, 0.5), (3, 4, 5, 4, 3, 5, 1, 1.0),
                                                         (6, 7, 8, 7, 6, 8, 2, 0.5)):
                P.stt("dve", der[:, oa], mv[:, isc], 1.0, ng[:, gi].unsqueeze(2).to_broadcast([128, 8, 2]),
                      ALU.add, ALU.mult)
                P.copy("dve", der[:, ob], mv[:, ish])
                P.ts("dve", der[:, os_], mv[:, ig], half, None, ALU.mult)

        def norm_stats(rstd, sqb):
            for ti, (c0, n) in enumerate(TILES):
                bank = ti % 2
                for kc in range(8):
                    P.act(sqb[:, kc, 0:n], xT[:, kc, c0:c0 + n], AF.Square)
                for kc in range(8):
                    P.mm(ps[:, bank, 0:n], ones_bf[:], sqb[:, kc, 0:n], start=(kc == 0), stop=(kc == 7))
                P.act(rstd[:, c0:c0 + n], ps[:, bank, 0:n], AF.Sqrt, bias=eps_t[:, 0:1], scale=1.0)
                P.recip(rstd[:, c0:c0 + n], rstd[:, c0:c0 + n])

        def make_h(hout, c0, n, rstd, ia, ib, tmp):
            s_ = 0 if c0 < NL else 1
            for kc in range(8):
                eng = "dve" if kc % 2 == 0 else "pool"
                P.tt(eng, tmp[:, kc % 2, 0:n], xT[:, kc, c0:c0 + n], rstd[:, c0:c0 + n], ALU.mult)
                P.act(hout[:, kc, 0:n], tmp[:, kc % 2, 0:n], AF.Identity,
                      bias=der[:, ib, kc, s_:s_ + 1], scale=der[:, ia, kc, s_:s_ + 1])

        def ffn(l, f):
            ia, ib, isg = (0, 1, 2) if f == 0 else (6, 7, 8)
            with contextlib.ExitStack() as sc:
                sb = lambda n, s, d=F32: sc.enter_context(nc.sbuf_tensor(un(n), s, d))
                rstd = sb("f_rstd", [128, T])
                sqb = sb("f_sqb", [128, 8, 512], BF16)
                hT = sb("f_hT", [128, 8, T], BF16)
                actb = sb("f_act", [128, 4, T], BF16)
                tmp = sb("f_tmp", [128, 2, 512])
                sg = sb("f_sg", [128, 2, 512])
                norm_stats(rstd, sqb)
                for (c0, n) in TILES:
                    make_h(hT[:, :, c0:c0 + n], c0, n, rstd, ia, ib, tmp)
                groups = [(0, 4), (4, 4), (8, 4), (12, 4), (16, 4), (20, 2)]
                for (m0, G) in groups:
                    sa, slA = next_slot()
                    P.dma("pool", slA[:, :, 0:G * 128],
                          win_d[l, f, :, m0 * 128:(m0 + G) * 128].rearrange("(kc p) n -> p kc n", p=128),
                          "slot%d_0" % sa, writes=[slA[:, :, 0:G * 128]])
                    P.dma("pool", slA[:, :, 512:512 + G * 128],
                          win_d[l, f, :, DFF + m0 * 128:DFF + (m0 + G) * 128].rearrange("(kc p) n -> p kc n", p=128),
                          "slot%d_1" % sa, writes=[slA[:, :, 512:512 + G * 128]])
                    sb_i, slB = next_slot()
                    P.dma("pool", slB[:, 0:G, :],
                          wout_d[l, f, m0 * 128:(m0 + G) * 128, :].rearrange("(m p) n -> p m n", p=128),
                          "slot%d_0" % sb_i, writes=[slB[:, 0:G, :]])
                    cnt = 0
                    for (c0, n) in TILES:
                        for mi in range(G):
                            bg = cnt % 2
                            bu = 2 + cnt % 2
                            cnt += 1
                            for kc in range(8):
                                P.mm(ps[:, bg, 0:n], slA[:, kc, mi * 128:(mi + 1) * 128], hT[:, kc, c0:c0 + n],
                                     start=(kc == 0), stop=(kc == 7))
                            for kc in range(8):
                                P.mm(ps[:, bu, 0:n], slA[:, kc, 512 + mi * 128:512 + (mi + 1) * 128], hT[:, kc, c0:c0 + n],
                                     start=(kc == 0), stop=(kc == 7))
                            P.act(sg[:, bg, 0:n], ps[:, bg, 0:n], AF.Silu)
                            P.tt("dve", actb[:, mi, c0:c0 + n], sg[:, bg, 0:n], ps[:, bu, 0:n], ALU.mult)
                    cnt = 0
                    for dc in range(8):
                        for (c0, n) in TILES:
                            s_ = 0 if c0 < NL else 1
                            bo = 4 + cnt % 2
                            cnt += 1
                            for mi in range(G):
                                P.mm(ps[:, bo, 0:n], slB[:, mi, dc * 128:(dc + 1) * 128], actb[:, mi, c0:c0 + n],
                                     start=(mi == 0), stop=(mi == G - 1))
                            P.stt("dve", xT[:, dc, c0:c0 + n], ps[:, bo, 0:n], der[:, isg, dc, s_:s_ + 1],
                                  xT[:, dc, c0:c0 + n], ALU.mult, ALU.add)
                P.barrier()

        def final():
            with contextlib.ExitStack() as sc:
                sb = lambda n, s, d=F32: sc.enter_context(nc.sbuf_tensor(un(n), s, d))
                rstd = sb("z_rstd", [128, T])
                sqb = sb("z_sqb", [128, 8, 512], BF16)
                ot = sb("z_out", [128, 8, NL])
                norm_stats(rstd, sqb)
                for kc in range(8):
                    P.stt("dve", ot[:, kc, :], xT[:, kc, 0:NL], pcol("fg", 1, kc),
                          rstd[:, 0:NL], ALU.mult, ALU.mult)
                    P.dma("sp", out_d[kc * 128:(kc + 1) * 128, :], ot[:, kc, :], "out", reads=[ot[:, kc, :]])
                P.wait_dma("sp", "out")
                P.barrier()

        ones1 = sbuf("ones1", [128, 128], BF16)
        P.memset("dve", ones1[:], 1.0)
        identf = pcol("ident", 128)

        def rwkv(l, rstd, catA):
            pre = f"L{l}_"
            with contextlib.ExitStack() as sc:
                sb = lambda n, s, d=F32: sc.enter_context(nc.sbuf_tensor(un(n), s, d))
                yf = sb("r_yf", [128, 2, T], BF16)
                hTt = sb("r_hTt", [128, 8, 256], BF16)
                tmp = sb("r_tmp", [128, 2, 256])
                raw = sb("r_raw", [128, 2, 260])
                f = sb("r_f", [128, 9, 128])
                W = sb("r_W", [128, 16, 2, 128])
                lwb = sb("r_lwb", [128, 3, 256], BF16)
                lob = sb("r_lob", [128, 3, 128], BF16)
                c0t = sb("r_c0", [128, 9])
                gam = sb("r_gam", [128, 2, 2])
                rmask = sb("r_rmask", [128, 128])
                sq = sb("r_sq", [128, 2, 128], F32R)
                ar = sb("r_ar", [128, 2, 2, 2, 64], F32R)
                bt = sb("r_bt", [128, 2, 128], F32R)
                kt = sb("r_kt", [128, 2, 128], F32R)
                bhT = sb("r_bhT", [64, 2, 256], F32R)
                khT = sb("r_khT", [64, 2, 256], F32R)
                vTp = sb("r_vTp", [64, 2, 4, 128], F32R)
                S1 = sb("r_S1", [64, 2, 4, 128], F32R)
                S2 = sb("r_S2", [64, 2, 4, 128], F32R)
                S3 = sb("r_S3", [64, 2, 4, 64], F32R)
                Na = sb("r_Na", [64, 2, 4, 64], F32R)
                NaT = sb("r_NaT", [64, 2, 4, 64], F32R)
                X = sb("r_X", [64, 2, 4, 64], F32R)
                Zs = sb("r_Zs", [64, 4, 64], F32R)
                Us = sb("r_Us", [64, 4, 128], F32R)
                Hst = sb("r_H", [128, 2, 128], F32R)
                bones = sb("r_bones", [128, 128], F32R)

                Wt = lambda i: W[:, i]
                zt = sb("r_zt", [128, 2, 128])
                P.memset("dve", zt[:], 0.0)
                P.memset("dve", rmask[:], 0.0)
                P.memset("dve", rmask[0:64, 0:64], 1.0)
                P.memset("dve", rmask[64:128, 64:128], 1.0)
                P.copy("dve", bones[:], rmask[:])
                P.memset("dve", rmask[:], 1.0)
                P.memset("dve", rmask[:, 0:1], 0.0)
                P.memset("dve", rmask[:, 64:65], 0.0)
                for c_ in range(2):
                    for h_ in range(4):
                        P.copy("dve", vTp[:, c_, h_, :], zt[0:64, 0, :])
                for h_ in range(4):
                    P.copy("dve", Us[:, h_, :], zt[0:64, 0, :])
                s0i, sl0 = next_slot()
                s1i, sl1 = next_slot()
                for hf in range(2):
                    P.dma("pool", sl0[:, :, hf * 512:(hf + 1) * 512],
                          mwin_d[l, :, hf * 512:(hf + 1) * 512].rearrange("(kc p) n -> p kc n", p=128),
                          "slot%d_%d" % (s0i, hf), writes=[sl0[:, :, hf * 512:(hf + 1) * 512]])
                P.dma("pool", sl1[:, :, 0:128], mwin_d[l, :, 1024:1152].rearrange("(kc p) n -> p kc n", p=128),
                      "slot%d_0" % s1i, writes=[sl1[:, :, 0:128]])
                P.dma("pool", lwb[:], lw_d[l].rearrange("p (i n) -> p i n", n=256), "lwb", writes=[lwb[:]])
                mu = pcol(pre + "mu", 18)
                P.tt("dve", c0t[:], mu[:, 0:9], mu[:, 9:18], ALU.add)
                P.ts("dve", c0t[:], c0t[:], -1.0, 1.0, ALU.mult, ALU.add)

                def v4(ap):
                    return ap.rearrange("p h (c t) -> p h c t", t=64)

                def segment(d, s0):
                    if RW_STAGE < 0:
                        return
                    q0, q1 = (0, NL) if s0 < NL else (NL, T)
                    s1 = s0 + 128
                    lo = s0 - 64 if s0 - 64 >= q0 else q0
                    hi = lo + 256
                    if hi > q1:
                        hi = q1
                        lo = hi - 256
                    ncol = hi - lo
                    off = s0 - lo
                    make_h(hTt, lo, ncol, rstd, 3, 4, tmp)
                    if RW_STAGE < 0.5:
                        return
                    for c in range(9):
                        bank = c % 2
                        for kc in range(8):
                            w = sl0[:, kc, c * 128:(c + 1) * 128] if c < 8 else sl1[:, kc, 0:128]
                            P.mm(ps[:, bank, 0:ncol], w, hTt[:, kc, 0:ncol], start=(kc == 0), stop=(kc == 7))
                        rw = raw[:, bank, :]
                        P.copy("act", rw[:, 1:1 + ncol], ps[:, bank, 0:ncol])
                        if off == 0:
                            P.memset("pool", rw[:, 0:1], 0.0)
                        if off + 128 == ncol:
                            P.memset("pool", rw[:, 1 + ncol:2 + ncol], 0.0)
                        fc = f[:, c, :]
                        P.ts("dve", fc, rw[:, 1 + off:129 + off], c0t[:, c:c + 1], None, ALU.mult)
                        P.ts("pool", tmp[:, 0, 0:128], rw[:, off:128 + off], mu[:, c:c + 1], None, ALU.mult)
                        P.ts("pool", tmp[:, 1, 0:128], rw[:, 2 + off:130 + off], mu[:, 9 + c:10 + c], None, ALU.mult)
                        P.tt("dve", fc, fc, tmp[:, 0, 0:128], ALU.add)
                        P.tt("dve", fc, fc, tmp[:, 1, 0:128], ALU.add)
                    if RW_STAGE < 1:
                        return
                    dr_ = slice(d * 64, (d + 1) * 64)
                    P.act(lob[dr_, 0, :], f[dr_, 6, :], AF.Tanh)
                    P.copy("dve", lob[:, 1, :], f[:, 7, :])
                    if d == 1:
                        P.act(lob[:, 2, :], f[:, 8, :], AF.Sigmoid)
                    sg = Wt(0)
                    for hc in range(2):
                        pm = ps[:, 2, hc * 128:(hc + 1) * 128]
                        P.mm(pm, lwb[dr_, 0, hc * 128:(hc + 1) * 128], lob[dr_, 0, :])
                        P.act(sg[:, hc, :], pm, AF.Sigmoid, bias=pcol(pre + "w0", 1, d * 2 + hc), scale=1.0)
                    a_t = {0: Wt(1), 1: Wt(2)}
                    for dd in ((0,) if d == 0 else (0, 1)):
                        ddr = slice(dd * 64, (dd + 1) * 64)
                        for hc in range(2):
                            pm = ps[:, 2, 256 + hc * 128:256 + (hc + 1) * 128]
                            P.mm(pm, lwb[ddr, 1, hc * 128:(hc + 1) * 128], lob[ddr, 1, :])
                            P.act(a_t[dd][:, hc, :], pm, AF.Sigmoid, bias=pcol(pre + "a0", 1, dd * 2 + hc), scale=1.0)
                    gt = Wt(3)
                    if d == 1:
                        for hc in range(2):
                            pm = ps[:, 2, hc * 128:(hc + 1) * 128]
                            P.mm(pm, lwb[:, 2, hc * 128:(hc + 1) * 128], lob[:, 2, :])
                            P.copy("act", gt[:, hc, :], pm)
                    if RW_STAGE < 2:
                        return
                    kks, kkn, t6 = Wt(4), Wt(5), Wt(6)
                    for hc in range(2):
                        P.ts("dve", kks[:, hc, :], f[:, 2 + hc, :], pcol(pre + "kk", 1, hc), None, ALU.mult)
                    P.act(sq[:], kks, AF.Square)
                    pk = ps[:, 2, 0:256].rearrange("p (h t) -> p h t", t=128)
                    for hc in range(2):
                        P.mm(pk[:, hc, :], bones[:], sq[:, hc, :])
                    P.act(t6, pk, AF.Sqrt, bias=eps_t[:, 1:2], scale=1.0)
                    P.recip(t6, t6)
                    P.tt("dve", kkn, kks, t6, ALU.mult)
                    acur = a_t[d]
                    kd, bv = Wt(7), Wt(8)
                    for hc in range(2):
                        P.ts("dve", t6[:, hc, :], acur[:, hc, :], -1.0, pcol(pre + "ka", 1, hc), ALU.add, ALU.mult)
                    P.stt("dve", kd, t6, 1.0, f[:, 2:4, :], ALU.add, ALU.mult)
                    P.tt("pool", bv, kkn, acur, ALU.mult)
                    if RW_STAGE < 3:
                        return
                    cs, ex, ci, en = Wt(9), Wt(10), Wt(11), Wt(12)
                    for hc in range(2):
                        P.scan(cs[:, hc, :], rmask[:], sg[:, hc, :])
                    csv = v4(cs)
                    tot = csv[:, :, :, 63:64]
                    totb = tot.to_broadcast([128, 2, 2, 64])
                    if d == 0:
                        ci = cs
                        P.tt("dve", ex, cs, sg, ALU.subtract)
                        P.tt("dve", v4(en), totb, csv, ALU.subtract)
                    else:
                        P.tt("dve", v4(ex), totb, csv, ALU.subtract)
                        P.tt("dve", ci, ex, sg, ALU.add)
                        P.tt("dve", en, cs, sg, ALU.subtract)
                    Eex, Ein, Einv, Eend = Wt(6), Wt(13), Wt(14), Wt(15)
                    P.act(Eex, ex, AF.Exp, scale=-CDEC)
                    P.act(Ein, ci, AF.Exp, scale=-CDEC)
                    P.act(Einv, ci, AF.Exp, scale=CDEC)
                    P.act(Eend, en, AF.Exp, scale=-CDEC)
                    P.act(gam[:], csv[:, :, :, 63], AF.Exp, scale=-CDEC)
                    if RW_STAGE < 4:
                        return
                    P.stt("dve", ar[:, :, :, 0, :], v4(kkn), -1.0, v4(Eex), ALU.mult, ALU.mult)
                    P.tt("dve", ar[:, :, :, 1, :], v4(f[:, 0:2, :]), v4(Ein), ALU.mult)
                    P.tt("dve", bt[:], bv, Einv, ALU.mult)
                    P.tt("dve", kt[:], kd, Einv, ALU.mult)
                    bh, kh = Wt(9), Wt(10)
                    P.tt("pool", bh, bv, Eend, ALU.mult)
                    P.tt("pool", kh, kd, Eend, ALU.mult)
                    if RW_STAGE < 5:
                        return
                    for ch in range(2):
                        pt = ps[0:64, 3, :]
                        for hc in range(2):
                            P.transpose(pt[:, hc * 128:(hc + 1) * 128], bh[:, hc, ch * 64:(ch + 1) * 64], identf)
                            P.transpose(pt[:, 256 + hc * 128:256 + (hc + 1) * 128], kh[:, hc, ch * 64:(ch + 1) * 64], identf)
                        if RW_STAGE < 5.2:
                            continue
                        if RW_VAR != "noact":
                            P.copy("act", bhT[:, ch, :], pt[:, 0:256])
                        if RW_VAR != "nodve":
                            P.copy("dve", khT[:, ch, :], pt[:, 256:512])
                        if RW_STAGE < 5.4:
                            continue
                        pv = ps[0:64, 2, 256:512]
                        for hc in range(2):
                            P.transpose(pv[:, hc * 128:(hc + 1) * 128], f[:, 4 + hc, ch * 64:(ch + 1) * 64], identf)
                        if RW_STAGE < 5.6:
                            continue
                        pv4 = pv.rearrange("p (h e i) -> p h e i", e=2, i=64)
                        vt5 = vTp[:, ch].rearrange("p (h e) (s i) -> p h e s i", e=2, i=64)
                        for e in range(2):
                            P.copy("dve" if e == 0 else "act", vt5[:, :, e, e, :], pv4[:, :, e, :])
                    if RW_STAGE < 6:
                        return
                    mk = pcol("mask", 128, 0 if d == 0 else 128, rows=64)
                    mk3 = pcol("mask", 64, 128 if d == 0 else 0, rows=64)
                    p3 = ps[0:64, 6, :].rearrange("p (c h t) -> p c h t", h=4, t=64)
                    for ch in range(2):
                        p1 = ps[0:64, 4, :].rearrange("p (h x) -> p h x", x=128)
                        p2 = ps[0:64, 5, :].rearrange("p (h x) -> p h x", x=128)
                        ar2 = ar[:].rearrange("p a c x t -> p a c (x t)")
                        cs_ = slice(ch * 64, (ch + 1) * 64)
                        for typ, horder in ((0, (0, 2, 1, 3)), (1, (1, 3, 0, 2)), (2, (0, 2, 1, 3))):
                            for h in horder:
                                hc, pr = h // 2, (h % 2) * 64
                                rhs = ar2[pr:pr + 64, hc, ch, :]
                                if typ == 0:
                                    P.mm(p1[:, h, :], bt[pr:pr + 64, hc, cs_], rhs)
                                elif typ == 1:
                                    P.mm(p2[:, h, :], kt[pr:pr + 64, hc, cs_], rhs)
                                else:
                                    P.mm(p3[:, ch, h, :], ar[pr:pr + 64, hc, ch, 0, :], bt[pr:pr + 64, hc, cs_])
                        mkb = mk.unsqueeze(1).to_broadcast([64, 4, 128])
                        if RW_STAGE < 6.1:
                            continue
                        P.tt("dve", S1[:, ch], ps[0:64, 4, :].rearrange("p (h x) -> p h x", x=128), mkb, ALU.mult)
                        if RW_STAGE < 6.2:
                            continue
                        P.tt("dve", S2[:, ch], ps[0:64, 5, :].rearrange("p (h x) -> p h x", x=128), mkb, ALU.mult)
                    if RW_STAGE < 6.4:
                        return
                    P.tt("dve", S3[:].rearrange("p c h t -> p (c h) t"), ps[0:64, 6, :].rearrange("p (g t) -> p g t", t=64),
                         mk3.unsqueeze(1).to_broadcast([64, 8, 64]), ALU.mult)
                    if RW_STAGE < 7:
                        return
                    idb = pcol("ident", 64, rows=64).unsqueeze(1).to_broadcast([64, 8, 64])
                    g8 = lambda ap: ap.rearrange("p c h t -> p (c h) t")
                    P.tt("dve", g8(X[:]), g8(S1[:, :, :, 0:64]), idb, ALU.add)
                    N1 = S1[:, :, :, 0:64]
                    N1T = S3[:]
                    pA = ps[0:64, 4, :].rearrange("p (c h t) -> p c h t", h=4, t=64)
                    pB = ps[0:64, 5, :].rearrange("p (c h t) -> p c h t", h=4, t=64)
                    pC = ps[0:64, 6, :].rearrange("p (c h t) -> p c h t", h=4, t=64)
                    for lev in range(5):
                        last = lev == 4
                        for ch in range(2):
                            for h in range(4):
                                if not last:
                                    P.mm(pA[:, ch, h, :], N1T[:, ch, h, :], N1[:, ch, h, :])
                                P.mm(pB[:, ch, h, :], N1[:, ch, h, :], N1T[:, ch, h, :])
                        if not last:
                            P.copy("act", Na[:], pA)
                        P.copy("dve", NaT[:], pB)
                        for ch in range(2):
                            for h in range(4):
                                P.mm(pC[:, ch, h, :], NaT[:, ch, h, :], X[:, ch, h, :])
                        P.tt("dve", X[:], X[:], pC, ALU.add)
                        N1, N1T = Na[:], NaT[:]
                    if RW_STAGE < 8:
                        return
                    ysum = Wt(4)
                    Us5 = Us[:].rearrange("p (h e) (s i) -> p h e s i", e=2, i=64)
                    for ch in ((0, 1) if d == 0 else (1, 0)):
                        pZ = ps[0:64, 7, 0:256].rearrange("p (h i) -> p h i", i=64)
                        pU = ps[0:64, 7, 256:512].rearrange("p (h i) -> p h i", i=64)
                        for h in (0, 2, 1, 3):
                            hc, e = h // 2, h % 2
                            pr = e * 64
                            P.mm(pZ[:, h, :], ar[pr:pr + 64, hc, ch, 0, :], Hst[pr:pr + 64, hc, pr:pr + 64],
                                 start=True, stop=False)
                            P.mm(pZ[:, h, :], S2[:, ch, h, 0:64], vTp[:, ch, h, pr:pr + 64], start=False, stop=True)
                        P.copy("act", Zs[:], pZ)
                        for h in range(4):
                            P.mm(pU[:, h, :], X[:, ch, h, :], Zs[:, h, :])
                        pU5 = pU.rearrange("p (h e) i -> p h e i", e=2)
                        for e in range(2):
                            P.copy("dve" if e == 0 else "act", Us5[:, :, e, e, :], pU5[:, :, e, :])
                        pY = ps[:, 3, 0:128].rearrange("p (h t) -> p h t", t=64)
                        for hc in range(2):
                            P.mm(pY[:, hc, :], Hst[:, hc, :], ar[:, hc, ch, 1, :], start=True, stop=False)
                            for e in range(2):
                                h = hc * 2 + e
                                P.mm(pY[:, hc, :], Us[:, h, :], S1[:, ch, h, 64:128], start=False, stop=False)
                                P.mm(pY[:, hc, :], vTp[:, ch, h, :], S2[:, ch, h, 64:128], start=False, stop=(e == 1))
                        pH = ps[:, 3, 256:512].rearrange("p (h x) -> p h x", x=128)
                        for hc in range(2):
                            for e in range(2):
                                h = hc * 2 + e
                                P.mm(pH[:, hc, :], bhT[:, ch, hc * 128:(hc + 1) * 128], Us[:, h, :],
                                     start=(e == 0), stop=False)
                                P.mm(pH[:, hc, :], khT[:, ch, hc * 128:(hc + 1) * 128], vTp[:, ch, h, :],
                                     start=False, stop=(e == 1))
                        for hc in range(2):
                            for e in range(2):
                                pr = e * 64
                                P.stt("dve", Hst[pr:pr + 64, hc, pr:pr + 64], Hst[pr:pr + 64, hc, pr:pr + 64],
                                      gam[pr:pr + 64, hc, ch:ch + 1], pH[pr:pr + 64, hc, pr:pr + 64], ALU.mult, ALU.add)
                        tc_ = slice(s0 + ch * 64, s0 + (ch + 1) * 64)
                        if d == 0:
                            P.copy("act", yf[:, :, tc_], pY)
                        else:
                            P.tt("dve", ysum[:, :, ch * 64:(ch + 1) * 64], pY, yf[:, :, tc_], ALU.add)
                    if d == 0:
                        return
                    y = ysum
                    P.copy("act", sq[:], y)
                    pm = ps[:, 2, 0:256].rearrange("p (h t) -> p h t", t=128)
                    pm2 = ps[:, 2, 256:512].rearrange("p (h t) -> p h t", t=128)
                    for hc in range(2):
                        P.mm(pm[:, hc, :], bones[:], sq[:, hc, :])
                    yc = Wt(5)
                    P.stt("dve", yc, pm, -1.0 / 64.0, y, ALU.mult, ALU.add)
                    P.act(sq[:], yc, AF.Square)
                    for hc in range(2):
                        P.mm(pm2[:, hc, :], bones[:], sq[:, hc, :])
                    rs = Wt(6)
                    P.act(rs, pm2, AF.Sqrt, bias=eps_t[:, 2:3], scale=1.0 / 64.0)
                    P.recip(rs, rs)
                    P.tt("dve", yc, yc, rs, ALU.mult)
                    for hc in range(2):
                        P.ts("dve", yc[:, hc, :], yc[:, hc, :], pcol(pre + "lng", 1, hc), pcol(pre + "lnb", 1, hc),
                             ALU.mult, ALU.add)
                    t7 = Wt(7)
                    P.tt("dve", t7, a_t[0], a_t[1], ALU.add)
                    for hc in range(2):
                        P.ts("dve", t7[:, hc, :], t7[:, hc, :], -2.0, pcol(pre + "ka", 1, hc), ALU.add, ALU.mult)
                    P.stt("dve", t7, t7, 2.0, f[:, 2:4, :], ALU.add, ALU.mult)
                    for hc in range(2):
                        P.stt("dve", sq[:, hc, :], f[:, hc, :], pcol(pre + "rk", 1, hc), t7[:, hc, :], ALU.mult, ALU.mult)
                    for hc in range(2):
                        P.mm(pm[:, hc, :], bones[:], sq[:, hc, :])
                    P.tt("dve", t7, pm, f[:, 4:6, :], ALU.mult)
                    P.tt("dve", yc, yc, t7, ALU.add)
                    P.tt("dve", catA[:, :, s0:s0 + 128], yc, gt, ALU.mult)

                for d in range(2):
                    P.copy("dve", Hst[:], zt[:])
                    if d == 0:
                        segs = [NL, NL + 128] + [i * 128 for i in range(16)]
                    else:
                        segs = [NL + 128, NL] + [i * 128 for i in range(15, -1, -1)]
                    for s0 in segs[:RW_NSEG]:
                        segment(d, s0)
                P.barrier()

        def conv(l, rstd, catB):
            pre = f"L{l}_"
            LB, CB = 0, NL + 30
            with contextlib.ExitStack() as sc:
                sb = lambda n, s, d=F32: sc.enter_context(nc.sbuf_tensor(un(n), s, d))
                hTt = sb("c_hTt", [128, 8, 512], BF16)
                tmp = sb("c_tmp", [128, 2, 512])
                hp = sb("c_hp", [128, 2, T + 60])
                acc = sb("c_acc", [128, 2, T])
                wk = sb("c_wk", [128, 2, 512])
                ab = sb("c_ab", [128, 2, 512], F32R)
                o256 = sb("c_o256", [128, 128], F32R)
                P.memset("dve", tmp[:, 0, 0:128], 1.0 / 256.0)
                P.copy("dve", o256[:], tmp[:, 0, 0:128])
                P.memset("pool", hp[:], 0.0)
                si, sl = next_slot()
                P.dma("pool", sl[:, :, 0:512], mwin_d[l, :, 1152:1664].rearrange("(kc p) n -> p kc n", p=128),
                      "slot%d_0" % si, writes=[sl[:, :, 0:512]])
                for (c0, n) in TILES:
                    make_h(hTt, c0, n, rstd, 3, 4, tmp)
                    base = (LB + 15 + c0) if c0 < NL else (CB + 15 + c0 - NL)
                    for hc in range(2):
                        for kc in range(8):
                            P.mm(ps[:, hc, 0:n], sl[:, kc, hc * 128:(hc + 1) * 128], hTt[:, kc, 0:n],
                                 start=(kc == 0), stop=(kc == 7))
                        for kc in range(8):
                            P.mm(ps[:, 2 + hc, 0:n], sl[:, kc, 256 + hc * 128:256 + (hc + 1) * 128], hTt[:, kc, 0:n],
                                 start=(kc == 0), stop=(kc == 7))
                        P.act(wk[:, hc, 0:n], ps[:, 2 + hc, 0:n], AF.Sigmoid)
                        P.tt("dve", hp[:, hc, base:base + n], ps[:, hc, 0:n], wk[:, hc, 0:n], ALU.mult)
                dww = pcol(pre + "dww", 62)
                for hc in range(2):
                    for (a0_, n, pb) in ((0, NL, LB), (NL, NCX, CB)):
                        o = acc[:, hc, a0_:a0_ + n]
                        P.ts("dve", o, hp[:, hc, pb:pb + n], dww[:, hc * 31:hc * 31 + 1], None, ALU.mult)
                        for k in range(1, 31):
                            P.stt("dve", o, hp[:, hc, pb + k:pb + k + n], dww[:, hc * 31 + k:hc * 31 + k + 1], o,
                                  ALU.mult, ALU.add)
                for (c0, n) in TILES:
                    for hc in range(2):
                        P.act(ab[:, hc, 0:n], acc[:, hc, c0:c0 + n], AF.Identity, bias=pcol(pre + "dwb", 1, hc), scale=1.0)
                    for hc in range(2):
                        P.mm(ps[:, 4, 0:n], o256[:], ab[:, hc, 0:n], start=(hc == 0), stop=(hc == 1))
                    for hc in range(2):
                        P.tt("dve", wk[:, hc, 0:n], ab[:, hc, 0:n], ps[:, 4, 0:n], ALU.subtract)
                    P.act(ab[:, :, 0:n], wk[:, :, 0:n], AF.Square)
                    for hc in range(2):
                        P.mm(ps[:, 5, 0:n], o256[:], ab[:, hc, 0:n], start=(hc == 0), stop=(hc == 1))
                    P.act(tmp[:, 0, 0:n], ps[:, 5, 0:n], AF.Sqrt, bias=eps_t[:, 3:4], scale=1.0)
                    P.recip(tmp[:, 0, 0:n], tmp[:, 0, 0:n])
                    for hc in range(2):
                        P.tt("dve", wk[:, hc, 0:n], wk[:, hc, 0:n], tmp[:, 0, 0:n], ALU.mult)
                        P.ts("dve", wk[:, hc, 0:n], wk[:, hc, 0:n], pcol(pre + "clg", 1, hc), pcol(pre + "clb", 1, hc),
                             ALU.mult, ALU.add)
                        P.act(catB[:, hc, c0:c0 + n], wk[:, hc, 0:n], AF.Silu)
                P.barrier()

        def attention(l, rstd, catC):
            pre = f"L{l}_"
            lam_init = 0.8 - 0.6 * math.exp(-0.3 * l)
            with contextlib.ExitStack() as sc:
                sb = lambda n, s, d=F32: sc.enter_context(nc.sbuf_tensor(un(n), s, d))
                rope = sb("a_rope", [128, 2, NL])
                hTt = sb("a_hTt", [128, 8, 512], BF16)
                tmp = sb("a_tmp", [128, 2, 512])
                qT = sb("a_qT", [128, T], BF16)
                kT = sb("a_kT", [128, T], BF16)
                vtok = sb("a_vtok", [128, 18, 128], BF16)
                eT = sb("a_eT", [128, 2, 512], BF16)
                ot = sb("a_ot", [128, 2, 512])
                lamt = sb("a_lam", [128, 8])
                lamb = sb("a_lamb", [128, 2], BF16)
                P.dma("sp", rope[:], rope_d[:, :, 0:NL], "rope", writes=[rope[:]])
                lv = pcol(pre + "lam", 4)
                P.tt("dve", lamb[:, 0:1], lv[:, 0:1], lv[:, 1:2], ALU.mult)
                P.tt("dve", lamb[:, 1:2], lv[:, 2:3], lv[:, 3:4], ALU.mult)
                P.mm(ps[:, 7, 0:2], ones1[:], lamb[:])
                P.act(lamt[:, 0:2], ps[:, 7, 0:2], AF.Exp)
                P.tt("dve", lamt[:, 2:3], lamt[:, 1:2], lamt[:, 0:1], ALU.subtract)
                P.ts("dve", lamt[:, 3:4], lamt[:, 2:3], -lam_init, None, ALU.add)
                P.ts("dve", lamt[:, 4:5], pcol(pre + "dng", 1), 1.0 - lam_init, None, ALU.mult)
                neglam = lamt[:, 3:4]
                dsc = lamt[:, 4:5]
                for h in range(4):
                    si, sl = next_slot()
                    srcs = (1664 + h * 128, PIN + h * 128, 2176 + h * 128, PIN + 512 + h * 128, 2688 + h * 128)
                    for j, c_ in enumerate(srcs):
                        P.dma("pool", sl[:, :, j * 128:(j + 1) * 128],
                              mwin_d[l, :, c_:c_ + 128].rearrange("(kc p) n -> p kc n", p=128),
                              "slot%d_%d" % (si, j), writes=[sl[:, :, j * 128:(j + 1) * 128]])
                    for (c0, n) in TILES:
                        make_h(hTt, c0, n, rstd, 3, 4, tmp)
                        for j in range(5):
                            for kc in range(8):
                                P.mm(ps[:, j, 0:n], sl[:, kc, j * 128:(j + 1) * 128], hTt[:, kc, 0:n],
                                     start=(kc == 0), stop=(kc == 7))
                        if c0 < NL:
                            for (dst, j) in ((qT, 0), (kT, 2)):
                                P.tt("dve", ot[:, 0, 0:n], ps[:, j, 0:n], rope[:, 0, c0:c0 + n], ALU.mult)
                                P.tt("dve", ot[:, 1, 0:n], ps[:, j + 1, 0:n], rope[:, 1, c0:c0 + n], ALU.mult)
                                P.tt("pool", dst[:, c0:c0 + n], ot[:, 0, 0:n], ot[:, 1, 0:n], ALU.add)
                        else:
                            P.copy("act", qT[:, c0:c0 + n], ps[:, 0, 0:n])
                            P.copy("act", kT[:, c0:c0 + n], ps[:, 2, 0:n])
                        P.copy("act", tmp[:, 0, 0:n], ps[:, 4, 0:n])
                        for b_ in range(n // 128):
                            P.transpose(ps[:, 5, b_ * 128:(b_ + 1) * 128], tmp[:, 0, b_ * 128:(b_ + 1) * 128], identf)
                        kc0 = c0 // 128
                        P.copy("dve", vtok[:, kc0:kc0 + n // 128, :],
                               ps[:, 5, 0:n].rearrange("p (b e) -> p b e", e=128))
                    qgroups = [(c0, n, list(range(18))) for (c0, n) in TILES[:4]] + [(NL, NCX, [16, 17])]
                    for (qc0, qn, kchunks) in qgroups:
                        for m in range(2):
                            mr = slice(m * 64, (m + 1) * 64)
                            pO = ps[:, 2 + 2 * m, 0:qn]
                            pZ = ps[:, 3 + 2 * m, 0:qn]
                            nk = len(kchunks)
                            for i, kc in enumerate(kchunks):
                                sb_ = ps[:, i % 2, 0:qn]
                                P.mm(sb_, kT[mr, kc * 128:(kc + 1) * 128], qT[mr, qc0:qc0 + qn])
                                P.act(eT[:, i % 2, 0:qn], sb_, AF.Exp, scale=0.125)
                                P.mm(pO, vtok[:, kc, :], eT[:, i % 2, 0:qn], start=(i == 0), stop=(i == nk - 1))
                                P.mm(pZ, ones1[:], eT[:, i % 2, 0:qn], start=(i == 0), stop=(i == nk - 1))
                            P.recip(ot[:, m, 0:qn], pZ)
                            P.tt("dve", ot[:, m, 0:qn], pO, ot[:, m, 0:qn], ALU.mult)
                        o = tmp[:, 0, 0:qn]
                        P.stt("dve", o, ot[:, 1, 0:qn], neglam, ot[:, 0, 0:qn], ALU.mult, ALU.add)
                        P.act(eT[:, 0, 0:qn], o, AF.Square)
                        P.mm(ps[:, 6, 0:qn], ones1[:], eT[:, 0, 0:qn])
                        P.act(tmp[:, 1, 0:qn], ps[:, 6, 0:qn], AF.Sqrt, bias=eps_t[:, 3:4], scale=1.0 / 128.0)
                        P.recip(tmp[:, 1, 0:qn], tmp[:, 1, 0:qn])
                        P.stt("dve", catC[:, h, qc0:qc0 + qn], o, dsc, tmp[:, 1, 0:qn], ALU.mult, ALU.mult)
                P.barrier()

        def mixer(l):
            with contextlib.ExitStack() as sc:
                sb = lambda n, s, d=F32: sc.enter_context(nc.sbuf_tensor(un(n), s, d))
                rstd = sb("m_rstd", [128, T])
                catA = sb("m_catA", [128, 2, T], BF16)
                with nc.sbuf_tensor(un("m_sqb"), [128, 8, 512], BF16) as sqb:
                    norm_stats(rstd, sqb)
                    P.barrier()
                if "rwkv" not in SKIP:
                    rwkv(l, rstd, catA)
                catB = sb("m_catB", [128, 2, T], BF16)
                if "conv" not in SKIP:
                    conv(l, rstd, catB)
                catC = sb("m_catC", [128, 4, T], BF16)
                if "attn" not in SKIP:
                    attention(l, rstd, catC)
                if taps is not None and l == 0:
                    dt_ = sb("m_dbg", [128, T])
                    for kc in range(8):
                        src = catA[:, kc] if kc < 2 else (catB[:, kc - 2] if kc < 4 else catC[:, kc - 4])
                        P.copy("dve", dt_[:], src)
                        tap("cat%d" % kc, dt_[:])
                si, sl = next_slot()
                for hf in range(2):
                    P.dma("pool", sl[:, :, hf * 512:(hf + 1) * 512],
                          mwout_d[l, :, hf * 512:(hf + 1) * 512].rearrange("(kc p) n -> p kc n", p=128),
                          "slot%d_%d" % (si, hf), writes=[sl[:, :, hf * 512:(hf + 1) * 512]])
                cnt = 0
                for dc in range(8):
                    for (c0, n) in TILES:
                        s_ = 0 if c0 < NL else 1
                        bo = cnt % 2
                        cnt += 1
                        for kc in range(8):
                            src = catA[:, kc] if kc < 2 else (catB[:, kc - 2] if kc < 4 else catC[:, kc - 4])
                            P.mm(ps[:, bo, 0:n], sl[:, kc, dc * 128:(dc + 1) * 128], src[:, c0:c0 + n],
                                 start=(kc == 0), stop=(kc == 7))
                        P.stt("dve", xT[:, dc, c0:c0 + n], ps[:, bo, 0:n], der[:, 5, dc, s_:s_ + 1],
                              xT[:, dc, c0:c0 + n], ALU.mult, ALU.add)
                P.barrier()

        stages = []
        for l in range(n_layers):
            stages += [("ada%d" % l, lambda l=l: adaln(l)), ("ffa%d" % l, lambda l=l: ffn(l, 0)),
                       ("mix%d" % l, lambda l=l: mixer(l)), ("ffb%d" % l, lambda l=l: ffn(l, 1))]
        for name, fn in stages:
            fn()
            if stop_at == name:
                break
        if taps is not None:
            for kc in range(8):
                tap("x_kc%d" % kc, xT[:, kc, :])
        final()
        if taps is not None:
            for i in range(len(tap_list)):
                P.wait_dma("sp", "tap%d" % i)
        P.replay()
        build.info = dict(n_ins=dict(P.n_ins), nsem=P.nsem, taps=tap_list)
    return nc


_NC_CACHE = {}


def _host_inputs(inp):
    inp = {k: np.asarray(v) for k, v in inp.items()}
    perm = _perm_cols()
    mw = np.asarray(inp["mix_w_in"], np.float32)
    mw_ext = np.concatenate([mw, mw[:, :, 1664 + perm]], axis=2)
    mw_ext = np.ascontiguousarray(mw_ext)
    rope = _rope_tables()
    lw = np.stack([np.concatenate([np.asarray(inp["rwkv_w2"][l], np.float32).reshape(128, 256),
                                   np.asarray(inp["rwkv_a2"][l], np.float32).reshape(128, 256),
                                   np.asarray(inp["rwkv_g2"][l], np.float32)], axis=1) for l in range(2)], axis=0)
    shared = {
        "rope": rope,
        "lw": np.ascontiguousarray(lw),
        "ada_w": np.ascontiguousarray(inp["ada_w"], np.float32),
        "ffn_w_in": np.ascontiguousarray(inp["ffn_w_in"], np.float32),
        "ffn_w_out": np.ascontiguousarray(inp["ffn_w_out"], np.float32),
        "mix_w_in": mw_ext,
        "mix_w_out": np.ascontiguousarray(inp["mix_w_out"], np.float32),
    }
    maps = []
    for b in range(8):
        m = dict(shared)
        m["xT"] = np.ascontiguousarray(np.asarray(inp["x"][b], np.float32).T)
        m["cT"] = np.ascontiguousarray(np.asarray(inp["ctx"][b], np.float32).T)
        m["prm"] = _pack_prm(inp, b)
        maps.append(m)
    return maps


def kernel(**inputs):
    maps = _host_inputs(inputs)
    if "nc" not in _NC_CACHE:
        _NC_CACHE["nc"] = build()
    nc = _NC_CACHE["nc"]
    res = run_bass_kernel_spmd(nc, maps, core_ids=list(range(8)))
    out = np.stack([np.ascontiguousarray(res.results[b]["outT"].T) for b in range(8)], axis=0)
    return out.astype(np.float32)
```

```python
import contextlib
import math
import numpy as np
import concourse.bass as bass
import concourse.mybir as mybir
from concourse.bass_utils import run_bass_kernel_spmd

F32 = mybir.dt.float32
F32R = mybir.dt.float32r
BF16 = mybir.dt.bfloat16
ALU = mybir.AluOpType
AF = mybir.ActivationFunctionType
AX = mybir.AxisListType

ENG = ("pe", "dve", "act", "pool", "sp")
_ESZ = {F32: 4, F32R: 4, BF16: 2}
EPOCH = 4000

D = 1024
NL = 2048
NCX = 256
T = NL + NCX
DFF = 2816
PIN = 3200
CDEC = math.exp(-0.5)
SKIP = set()
RW_NSEG = 18
RW_STAGE = 99
RW_VAR = ""


def _region(ap):
    t = ap.tensor
    row = 1
    for s in list(t.shape)[1:]:
        row *= int(s)
    off = int(ap.offset)
    pairs = ap.ap
    p0 = off // row
    f0 = off % row
    pstep, pc = pairs[0]
    if pstep == 0:
        pc = 1
    ext = 1
    for st, c in pairs[1:]:
        ext += (int(c) - 1) * abs(int(st))
    esz = _ESZ[ap.dtype]
    b0, b1 = f0 * esz, (f0 + ext) * esz
    if t.name == "ps":
        b0 = (b0 // 2048) * 2048
        b1 = ((b1 + 2047) // 2048) * 2048
    return (t.name, p0, p0 + int(pc), b0, b1)


class Prog:
    def __init__(self, nc, stack):
        self.nc = nc
        self.stack = stack
        self.q = {e: [] for e in ENG}
        self.esem = {e: [] for e in ENG}
        self.ecnt = {e: 0 for e in ENG}
        self.pend = {e: False for e in ENG}
        self.last = {e: None for e in ENG}
        self.pe_base = None
        self.pe_idx = None
        self.waited = {e: {} for e in ENG}
        self.recs = {}
        self.dsem = {}
        self.nsem = 0
        self.allsems = []
        self.n_ins = {e: 0 for e in ENG}
        for e in ENG:
            self._new_epoch(e)

    def _mksem(self, name):
        self.nsem += 1
        h = self.nc.alloc_semaphore(name=name)
        self.allsems.append(h)
        return h

    def _new_epoch(self, e):
        s = self._mksem(f"s_{e}_{len(self.esem[e])}")
        self.esem[e].append(s)
        self.ecnt[e] = 0

    def _deps(self, eng, reads, writes, is_dma):
        deps = {}

        def need(r):
            k = id(r[5])
            v = deps.get(k)
            if v is None or v[1] < r[6]:
                deps[k] = (r[5], r[6])

        for ap in reads:
            name, p0, p1, f0, f1 = _region(ap)
            for r in self.recs.get(name, ()):
                if (r[4] or (name == "ps" and r[7] != eng)) and r[0] < p1 and p0 < r[1] and r[2] < f1 and f0 < r[3]:
                    need(r)
        for ap in writes:
            name, p0, p1, f0, f1 = _region(ap)
            for r in self.recs.get(name, ()):
                if r[0] < p1 and p0 < r[1] and r[2] < f1 and f0 < r[3]:
                    if (not is_dma) and r[7] == eng and eng == "pe":
                        continue
                    need(r)
        return deps

    def _record(self, reads, writes, eng, sem, val):
        for ap in writes:
            name, p0, p1, f0, f1 = _region(ap)
            lst = self.recs.setdefault(name, [])
            lst[:] = [r for r in lst if not (p0 <= r[0] and r[1] <= p1 and f0 <= r[2] and r[3] <= f1)]
            lst.append((p0, p1, f0, f1, True, sem, val, eng))
        for ap in reads:
            name, p0, p1, f0, f1 = _region(ap)
            lst = self.recs.setdefault(name, [])
            lst[:] = [r for r in lst if not ((not r[4]) and r[5] is sem and p0 <= r[0] and r[1] <= p1
                                             and f0 <= r[2] and r[3] <= f1)]
            lst.append((p0, p1, f0, f1, False, sem, val, eng))

    def _waits(self, eng, deps):
        w = []
        for k, (sem, val) in deps.items():
            if self.waited[eng].get(k, 0) < val:
                self.waited[eng][k] = val
                w.append((sem, val))
        return w

    def op(self, eng, fn, reads=(), writes=(), inc=True, pe_base=None):
        reads = [a for a in reads if a is not None and not isinstance(a, (int, float))]
        deps = self._deps(eng, reads, writes, False)
        if eng == "pe" and pe_base is not None:
            if (self.pe_base is not None and self.pe_idx is not None
                    and (self.pe_base[1] <= pe_base[0] or pe_base[1] <= self.pe_base[0])):
                if self.pend["pe"]:
                    w_, f_, _s, _a = self.q["pe"][self.pe_idx]
                    sem_ = self.esem["pe"][-1]
                    self.ecnt["pe"] += 1
                    self.q["pe"][self.pe_idx] = (w_, f_, sem_, 1)
                    self.pend["pe"] = False
                    self.last["pe"] = (sem_, self.ecnt["pe"])
                ls, lv = self.last["pe"]
                deps[id(ls)] = (ls, max(lv, deps.get(id(ls), (ls, 0))[1]))
            self.pe_base = pe_base
        waits = self._waits(eng, deps)
        sem = self.esem[eng][-1]
        val = self.ecnt[eng] + 1
        if inc:
            self.ecnt[eng] = val
            self.pend[eng] = False
            self.last[eng] = (sem, val)
        else:
            self.pend[eng] = True
        self.q[eng].append((waits, fn, sem if inc else None, 1))
        if eng == "pe":
            self.pe_idx = len(self.q[eng]) - 1
        self.n_ins[eng] += 1
        self._record(reads, writes, eng, sem, val)
        if inc and val >= EPOCH:
            self._new_epoch(eng)

    def dma(self, eng, out, in_, key, reads=(), writes=(), **kw):
        deps = self._deps(eng, reads, writes, True)
        waits = self._waits(eng, deps)
        if key not in self.dsem:
            self.dsem[key] = [self._mksem("d_" + str(key).replace(" ", "")), 0]
        ent = self.dsem[key]
        ent[1] += 16
        sem, val = ent[0], ent[1]
        self.q[eng].append((waits, lambda e: e.dma_start(out=out, in_=in_, **kw), sem, 16))
        self.n_ins[eng] += 1
        self._record(reads, writes, None, sem, val)

    def wait_dma(self, eng, key):
        ent = self.dsem[key]
        self.q[eng].append(([(ent[0], ent[1])], None, None, 0))

    def barrier(self):
        for e in ENG:
            if self.pend[e]:
                self.op(e, lambda x: x.nop(), (), (), inc=True)
        targets = [(e, self.last[e]) for e in ENG if self.last[e] is not None]
        targets = [(e, t[0], t[1]) for (e, t) in targets]
        targets += [(None, s, v) for (s, v) in self.dsem.values()]
        for e in ENG:
            w = []
            for (te, s, v) in targets:
                if te == e:
                    continue
                if self.waited[e].get(id(s), 0) < v:
                    self.waited[e][id(s)] = v
                    w.append((s, v))
            if w:
                self.q[e].append((w, None, None, 0))
        self.recs = {}

    def replay(self):
        nc = self.nc
        for h in self.allsems:
            nc.gpsimd.sem_clear(h)
        nc.all_engine_barrier()
        with nc.Block() as block:
            def run(engname):
                def body(eobj):
                    for waits, fn, sem, amt in self.q[engname]:
                        for (s, v) in waits:
                            eobj.wait_ge(s, v)
                        if fn is None:
                            continue
                        ins = fn(eobj)
                        if sem is not None:
                            ins.then_inc(sem, amt)
                return body
            block.tensor(run("pe"))
            block.vector(run("dve"))
            block.scalar(run("act"))
            block.gpsimd(run("pool"))
            block.sync(run("sp"))

    def mm(self, out, lhsT, rhs, start=True, stop=True):
        self.op("pe", lambda e: e.matmul(out, lhsT, rhs, start=start, stop=stop),
                [lhsT, rhs], [out], inc=stop, pe_base=(int(lhsT.base_partition()), int(lhsT.base_partition()) + int(lhsT.partition_size())))

    def transpose(self, out, in_, ident):
        self.op("pe", lambda e: e.transpose(out, in_, ident), [in_, ident], [out], pe_base=(0, 128))

    def tt(self, eng, out, a, b, op):
        self.op(eng, lambda e: e.tensor_tensor(out, a, b, op), [a, b], [out])

    def ts(self, eng, out, a, s1, s2, op0, op1=None):
        if op1 is None:
            self.op(eng, lambda e: e.tensor_scalar(out, a, s1, None, op0), [a, s1], [out])
        else:
            self.op(eng, lambda e: e.tensor_scalar(out, a, s1, s2, op0, op1), [a, s1, s2], [out])

    def stt(self, eng, out, a, s, b, op0, op1):
        self.op(eng, lambda e: e.scalar_tensor_tensor(out, a, s, b, op0, op1), [a, s, b], [out])

    def copy(self, eng, out, a):
        if eng == "act":
            self.op(eng, lambda e: e.copy(out, a), [a], [out])
        else:
            self.op(eng, lambda e: e.tensor_copy(out, a), [a], [out])

    def act(self, out, a, func, bias=None, scale=None):
        kw = {}
        if bias is not None:
            kw["bias"] = bias
        if scale is not None:
            kw["scale"] = scale
        self.op("act", lambda e: e.activation(out, a, func, **kw), [a, bias, scale], [out])

    def memset(self, eng, out, v):
        self.op(eng, lambda e: e.memset(out, v), [], [out])

    def recip(self, out, a):
        self.op("dve", lambda e: e.reciprocal(out, a), [a], [out])

    def scan(self, out, d0, d1):
        self.op("dve", lambda e: e.tensor_tensor_scan(out, d0, d1, 0.0, ALU.mult, ALU.add), [d0, d1], [out])


class _Cols:
    def __init__(self):
        self.off = {}
        self.n = 0

    def add(self, name, n):
        self.off[name] = self.n
        self.n += n
        return self.off[name]


def _prm_layout():
    c = _Cols()
    for l in range(2):
        pre = f"L{l}_"
        for name, n in (("adab", 72), ("ng", 24), ("mu", 18), ("w0", 4), ("a0", 4), ("kk", 2), ("ka", 2),
                        ("rk", 2), ("lng", 2), ("lnb", 2), ("dww", 62), ("dwb", 2), ("clg", 2), ("clb", 2),
                        ("dng", 1), ("lam", 4)):
            c.add(pre + name, n)
    c.add("fg", 8)
    c.add("ident", 128)
    c.add("mask", 256)
    c.add("cond", 16)
    return c


PRM = _prm_layout()


def _fm(v, nchunk):
    return np.ascontiguousarray(np.asarray(v, np.float32).reshape(nchunk, 128).T)


def _pack_prm(inp, b):
    P = np.zeros((128, PRM.n), np.float32)

    def put(name, arr):
        o = PRM.off[name]
        P[: arr.shape[0], o:o + arr.shape[1]] = arr

    for l in range(2):
        pre = f"L{l}_"
        put(pre + "adab", _fm(inp["ada_b"][l], 72))
        put(pre + "ng", np.concatenate([_fm(inp["norm_g"][l, i], 8) for i in range(3)], axis=1))
        put(pre + "mu", np.concatenate([_fm(inp["rwkv_mu"][l, d], 9) for d in range(2)], axis=1))
        put(pre + "w0", np.concatenate([_fm(inp["rwkv_w0"][l, d], 2) for d in range(2)], axis=1))
        put(pre + "a0", np.concatenate([_fm(inp["rwkv_a0"][l, d], 2) for d in range(2)], axis=1))
        put(pre + "kk", _fm(inp["rwkv_kk"][l], 2))
        put(pre + "ka", _fm(inp["rwkv_ka"][l], 2))
        put(pre + "rk", _fm(np.asarray(inp["rwkv_rk"][l]).reshape(256), 2))
        put(pre + "lng", _fm(inp["rwkv_ln_g"][l], 2))
        put(pre + "lnb", _fm(inp["rwkv_ln_b"][l], 2))
        dw = np.asarray(inp["conv_dw_w"][l], np.float32)
        put(pre + "dww", np.concatenate([np.ascontiguousarray(dw[:, hc * 128:(hc + 1) * 128].T) for hc in range(2)], axis=1))
        put(pre + "dwb", _fm(inp["conv_dw_b"][l], 2))
        put(pre + "clg", _fm(inp["conv_ln_g"][l], 2))
        put(pre + "clb", _fm(inp["conv_ln_b"][l], 2))
        put(pre + "dng", _fm(inp["diff_norm_g"][l], 1))
        put(pre + "lam", np.ascontiguousarray(np.asarray(inp["diff_lam"][l], np.float32).T))
    put("fg", _fm(inp["final_g"], 8))
    put("ident", np.eye(128, dtype=np.float32))
    i = np.arange(64)
    lt = (i[:, None] < i[None, :]).astype(np.float32)
    le = (i[:, None] <= i[None, :]).astype(np.float32)
    put("mask", np.concatenate([lt, le, lt.T, le.T], axis=1))
    cond = np.stack([_fm(inp["c"][b], 8), _fm(inp["c_ctx"], 8)], axis=2).reshape(128, 16)
    put("cond", cond)
    return P


def _rope_tables():
    n = np.arange(NL)
    row = (n // 64).astype(np.float32)
    col = (n % 64).astype(np.float32)
    inv = (1.0 / (10000.0 ** (np.arange(16, dtype=np.float32) * 2.0 / 32.0))).astype(np.float32)
    tab = np.zeros((128, 2, T), np.float32)
    tab[:, 0, NL:] = 1.0
    for p in range(128):
        d = p % 64
        axis = d // 32
        half = (d % 32) // 16
        f = d % 16
        pos = row if axis == 0 else col
        ang = (pos * inv[f]).astype(np.float32)
        tab[p, 0, :NL] = np.cos(ang)
        tab[p, 1, :NL] = np.sin(ang) * (-1.0 if half == 0 else 1.0)
    return tab


def _perm_cols():
    idx = np.arange(1024)
    d = idx % 64
    half = (d % 32) // 16
    partner = np.where(half == 0, idx + 16, idx - 16)
    return partner


TILES = [(0, 512), (512, 512), (1024, 512), (1536, 512), (2048, 256)]


def build(n_layers=2, taps=None, stop_at=None):
    nc = bass.Bass("TRN2", target_bir_lowering=False)
    dr = lambda n, s, k="ExternalInput": nc.dram_tensor(n, s, F32, kind=k).ap()
    xT_d = dr("xT", [D, NL])
    cT_d = dr("cT", [D, NCX])
    prm_d = dr("prm", [128, PRM.n])
    rope_d = dr("rope", [128, 2, T])
    lw_d = dr("lw", [2, 128, 768])
    adaw_d = dr("ada_w", [2, D, 9 * D])
    win_d = dr("ffn_w_in", [2, 2, D, 2 * DFF])
    wout_d = dr("ffn_w_out", [2, 2, DFF, D])
    mwin_d = dr("mix_w_in", [2, D, PIN + 1024])
    mwout_d = dr("mix_w_out", [2, D, D])
    out_d = dr("outT", [D, NL], "ExternalOutput")
    tap_list = []
    dbg_d = None
    if taps is not None:
        dbg_d = dr("dbg", [128, taps], "ExternalOutput")
    tap_off = [0]

    with nc.cleanup_on_exit(), contextlib.ExitStack() as st:
        P = Prog(nc, st)
        sbuf = lambda n, s, d=F32: st.enter_context(nc.sbuf_tensor(n, s, d))
        uctr = [0]

        def un(n):
            uctr[0] += 1
            return "%s_%d" % (n, uctr[0])

        xT = sbuf("xT_s", [128, 8, T])
        prm = sbuf("prm_s", [128, PRM.n])
        modT = sbuf("modT", [128, 72, 2])
        der = sbuf("der", [128, 9, 8, 2])
        condb = sbuf("condb", [128, 8, 2], BF16)
        ones_bf = sbuf("ones_bf", [128, 128], BF16)
        eps_t = sbuf("eps_t", [128, 4])
        slots = [sbuf(f"slot{i}", [128, 8, 1024], BF16) for i in range(2)]
        ps = st.enter_context(nc.psum_tensor("ps", [128, 8, 512], F32))
        slot_ctr = [0]

        def pcol(name, n=1, off=0, rows=128):
            o = PRM.off[name] + off
            return prm[0:rows, o:o + n]

        def tap(name, ap):
            if taps is None:
                return
            shp = ap.shape
            n = 1
            for s_ in shp[1:]:
                n *= int(s_)
            o = tap_off[0]
            assert o + n <= taps, (name, o, n)
            dst = dbg_d[0:shp[0], o:o + n]
            if len(shp) == 3:
                dst = dst.rearrange("p (a b) -> p a b", b=int(shp[2]))
            P.dma("sp", dst, ap, "tap%d" % len(tap_list), reads=[ap])
            tap_list.append((name, o, tuple(int(s_) for s_ in shp)))
            tap_off[0] = o + n

        P.dma("sp", prm[:], prm_d, "prm", writes=[prm[:]])
        for kc in range(8):
            P.dma("sp", xT[:, kc, 0:NL], xT_d[kc * 128:(kc + 1) * 128, :], "xin%d" % kc, writes=[xT[:, kc, 0:NL]])
            P.dma("sp", xT[:, kc, NL:T], cT_d[kc * 128:(kc + 1) * 128, :], "cin%d" % kc, writes=[xT[:, kc, NL:T]])
        P.memset("dve", ones_bf[:], 1.0 / 1024.0)
        P.memset("dve", eps_t[:, 0:1], 1e-6)
        P.memset("dve", eps_t[:, 1:2], 1e-12)
        P.memset("dve", eps_t[:, 2:3], 64e-5)
        P.memset("dve", eps_t[:, 3:4], 1e-5)
        condv = pcol("cond", 16).rearrange("p (k s) -> p k s", s=2)
        P.act(condb[:], condv, AF.Silu)

        def next_slot():
            s_ = slot_ctr[0] % 2
            slot_ctr[0] += 1
            return s_, slots[s_]

        def adaln(l):
            pmod = ps[:, 7, 0:144].rearrange("p (j s) -> p j s", s=2)
            for i in range(9):
                si, sl = next_slot()
                for hf in range(2):
                    P.dma("pool", sl[:, :, hf * 512:(hf + 1) * 512],
                          adaw_d[l, :, i * 1024 + hf * 512: i * 1024 + (hf + 1) * 512].rearrange("(kc p) n -> p kc n", p=128),
                          "slot%d_%d" % (si, hf), writes=[sl[:, :, hf * 512:(hf + 1) * 512]])
                for dc in range(8):
                    j = i * 8 + dc
                    for kc in range(8):
                        P.mm(pmod[:, j, :], sl[:, kc, dc * 128:(dc + 1) * 128], condb[:, kc, :],
                             start=(kc == 0), stop=(kc == 7))
            adab = pcol(f"L{l}_adab", 72)
            P.tt("dve", modT[:], pmod, adab.unsqueeze(2).to_broadcast([128, 72, 2]), ALU.add)
            mv = modT[:].rearrange("p (i k) s -> p i k s", k=8)
            ng = pcol(f"L{l}_ng", 24).rearrange("p (i k) -> p i k", k=8)
            for (oa, ob, os_, isc, ish, ig, gi, half) in ((0, 1, 2, 1, 0, 2, 0, 0.5), (3, 4, 5, 4, 3, 5, 1, 1.0),
                                                         (6, 7, 8, 7, 6, 8, 2, 0.5)):
                P.stt("dve", der[:, oa], mv[:, isc], 1.0, ng[:, gi].unsqueeze(2).to_broadcast([128, 8, 2]),
                      ALU.add, ALU.mult)
                P.copy("dve", der[:, ob], mv[:, ish])
                P.ts("dve", der[:, os_], mv[:, ig], half, None, ALU.mult)

        def norm_stats(rstd, sqb):
            for ti, (c0, n) in enumerate(TILES):
                bank = ti % 2
                for kc in range(8):
                    P.act(sqb[:, kc, 0:n], xT[:, kc, c0:c0 + n], AF.Square)
                for kc in range(8):
                    P.mm(ps[:, bank, 0:n], ones_bf[:], sqb[:, kc, 0:n], start=(kc == 0), stop=(kc == 7))
                P.act(rstd[:, c0:c0 + n], ps[:, bank, 0:n], AF.Sqrt, bias=eps_t[:, 0:1], scale=1.0)
                P.recip(rstd[:, c0:c0 + n], rstd[:, c0:c0 + n])

        def make_h(hout, c0, n, rstd, ia, ib, tmp):
            s_ = 0 if c0 < NL else 1
            for kc in range(8):
                eng = "dve" if kc % 2 == 0 else "pool"
                P.tt(eng, tmp[:, kc % 2, 0:n], xT[:, kc, c0:c0 + n], rstd[:, c0:c0 + n], ALU.mult)
                P.act(hout[:, kc, 0:n], tmp[:, kc % 2, 0:n], AF.Identity,
                      bias=der[:, ib, kc, s_:s_ + 1], scale=der[:, ia, kc, s_:s_ + 1])

        def ffn(l, f):
            ia, ib, isg = (0, 1, 2) if f == 0 else (6, 7, 8)
            with contextlib.ExitStack() as sc:
                sb = lambda n, s, d=F32: sc.enter_context(nc.sbuf_tensor(un(n), s, d))
                rstd = sb("f_rstd", [128, T])
                sqb = sb("f_sqb", [128, 8, 512], BF16)
                hT = sb("f_hT", [128, 8, T], BF16)
                actb = sb("f_act", [128, 4, T], BF16)
                tmp = sb("f_tmp", [128, 2, 512])
                sg = sb("f_sg", [128, 2, 512])
                norm_stats(rstd, sqb)
                for (c0, n) in TILES:
                    make_h(hT[:, :, c0:c0 + n], c0, n, rstd, ia, ib, tmp)
                groups = [(0, 4), (4, 4), (8, 4), (12, 4), (16, 4), (20, 2)]
                for (m0, G) in groups:
                    sa, slA = next_slot()
                    P.dma("pool", slA[:, :, 0:G * 128],
                          win_d[l, f, :, m0 * 128:(m0 + G) * 128].rearrange("(kc p) n -> p kc n", p=128),
                          "slot%d_0" % sa, writes=[slA[:, :, 0:G * 128]])
                    P.dma("pool", slA[:, :, 512:512 + G * 128],
                          win_d[l, f, :, DFF + m0 * 128:DFF + (m0 + G) * 128].rearrange("(kc p) n -> p kc n", p=128),
                          "slot%d_1" % sa, writes=[slA[:, :, 512:512 + G * 128]])
                    sb_i, slB = next_slot()
                    P.dma("pool", slB[:, 0:G, :],
                          wout_d[l, f, m0 * 128:(m0 + G) * 128, :].rearrange("(m p) n -> p m n", p=128),
                          "slot%d_0" % sb_i, writes=[slB[:, 0:G, :]])
                    cnt = 0
                    for (c0, n) in TILES:
                        for mi in range(G):
                            bg = cnt % 2
                            bu = 2 + cnt % 2
                            cnt += 1
                            for kc in range(8):
                                P.mm(ps[:, bg, 0:n], slA[:, kc, mi * 128:(mi + 1) * 128], hT[:, kc, c0:c0 + n],
                                     start=(kc == 0), stop=(kc == 7))
                            for kc in range(8):
                                P.mm(ps[:, bu, 0:n], slA[:, kc, 512 + mi * 128:512 + (mi + 1) * 128], hT[:, kc, c0:c0 + n],
                                     start=(kc == 0), stop=(kc == 7))
                            P.act(sg[:, bg, 0:n], ps[:, bg, 0:n], AF.Silu)
                            P.tt("dve", actb[:, mi, c0:c0 + n], sg[:, bg, 0:n], ps[:, bu, 0:n], ALU.mult)
                    cnt = 0
                    for dc in range(8):
                        for (c0, n) in TILES:
                            s_ = 0 if c0 < NL else 1
                            bo = 4 + cnt % 2
                            cnt += 1
                            for mi in range(G):
                                P.mm(ps[:, bo, 0:n], slB[:, mi, dc * 128:(dc + 1) * 128], actb[:, mi, c0:c0 + n],
                                     start=(mi == 0), stop=(mi == G - 1))
                            P.stt("dve", xT[:, dc, c0:c0 + n], ps[:, bo, 0:n], der[:, isg, dc, s_:s_ + 1],
                                  xT[:, dc, c0:c0 + n], ALU.mult, ALU.add)
                P.barrier()

        def final():
            with contextlib.ExitStack() as sc:
                sb = lambda n, s, d=F32: sc.enter_context(nc.sbuf_tensor(un(n), s, d))
                rstd = sb("z_rstd", [128, T])
                sqb = sb("z_sqb", [128, 8, 512], BF16)
                ot = sb("z_out", [128, 8, NL])
                norm_stats(rstd, sqb)
                for kc in range(8):
                    P.stt("dve", ot[:, kc, :], xT[:, kc, 0:NL], pcol("fg", 1, kc),
                          rstd[:, 0:NL], ALU.mult, ALU.mult)
                    P.dma("sp", out_d[kc * 128:(kc + 1) * 128, :], ot[:, kc, :], "out", reads=[ot[:, kc, :]])
                P.wait_dma("sp", "out")
                P.barrier()

        ones1 = sbuf("ones1", [128, 128], BF16)
        P.memset("dve", ones1[:], 1.0)
        identf = pcol("ident", 128)

        def rwkv(l, rstd, catA):
            pre = f"L{l}_"
            with contextlib.ExitStack() as sc:
                sb = lambda n, s, d=F32: sc.enter_context(nc.sbuf_tensor(un(n), s, d))
                yf = sb("r_yf", [128, 2, T], BF16)
                hTt = sb("r_hTt", [128, 8, 256], BF16)
                tmp = sb("r_tmp", [128, 2, 256])
                raw = sb("r_raw", [128, 2, 260])
                f = sb("r_f", [128, 9, 128])
                W = sb("r_W", [128, 16, 2, 128])
                lwb = sb("r_lwb", [128, 3, 256], BF16)
                lob = sb("r_lob", [128, 3, 128], BF16)
                c0t = sb("r_c0", [128, 9])
                gam = sb("r_gam", [128, 2, 2])
                rmask = sb("r_rmask", [128, 128])
                sq = sb("r_sq", [128, 2, 128], F32R)
                ar = sb("r_ar", [128, 2, 2, 2, 64], F32R)
                bt = sb("r_bt", [128, 2, 128], F32R)
                kt = sb("r_kt", [128, 2, 128], F32R)
                bhT = sb("r_bhT", [64, 2, 256], F32R)
                khT = sb("r_khT", [64, 2, 256], F32R)
                vTp = sb("r_vTp", [64, 2, 4, 128], F32R)
                S1 = sb("r_S1", [64, 2, 4, 128], F32R)
                S2 = sb("r_S2", [64, 2, 4, 128], F32R)
                S3 = sb("r_S3", [64, 2, 4, 64], F32R)
                Na = sb("r_Na", [64, 2, 4, 64], F32R)
                NaT = sb("r_NaT", [64, 2, 4, 64], F32R)
                X = sb("r_X", [64, 2, 4, 64], F32R)
                Zs = sb("r_Zs", [64, 4, 64], F32R)
                Us = sb("r_Us", [64, 4, 128], F32R)
                Hst = sb("r_H", [128, 2, 128], F32R)
                bones = sb("r_bones", [128, 128], F32R)

                Wt = lambda i: W[:, i]
                zt = sb("r_zt", [128, 2, 128])
                P.memset("dve", zt[:], 0.0)
                P.memset("dve", rmask[:], 0.0)
                P.memset("dve", rmask[0:64, 0:64], 1.0)
                P.memset("dve", rmask[64:128, 64:128], 1.0)
                P.copy("dve", bones[:], rmask[:])
                P.memset("dve", rmask[:], 1.0)
                P.memset("dve", rmask[:, 0:1], 0.0)
                P.memset("dve", rmask[:, 64:65], 0.0)
                for c_ in range(2):
                    for h_ in range(4):
                        P.copy("dve", vTp[:, c_, h_, :], zt[0:64, 0, :])
                for h_ in range(4):
                    P.copy("dve", Us[:, h_, :], zt[0:64, 0, :])
                s0i, sl0 = next_slot()
                s1i, sl1 = next_slot()
                for hf in range(2):
                    P.dma("pool", sl0[:, :, hf * 512:(hf + 1) * 512],
                          mwin_d[l, :, hf * 512:(hf + 1) * 512].rearrange("(kc p) n -> p kc n", p=128),
                          "slot%d_%d" % (s0i, hf), writes=[sl0[:, :, hf * 512:(hf + 1) * 512]])
                P.dma("pool", sl1[:, :, 0:128], mwin_d[l, :, 1024:1152].rearrange("(kc p) n -> p kc n", p=128),
                      "slot%d_0" % s1i, writes=[sl1[:, :, 0:128]])
                P.dma("pool", lwb[:], lw_d[l].rearrange("p (i n) -> p i n", n=256), "lwb", writes=[lwb[:]])
                mu = pcol(pre + "mu", 18)
                P.tt("dve", c0t[:], mu[:, 0:9], mu[:, 9:18], ALU.add)
                P.ts("dve", c0t[:], c0t[:], -1.0, 1.0, ALU.mult, ALU.add)

                def v4(ap):
                    return ap.rearrange("p h (c t) -> p h c t", t=64)

                def segment(d, s0):
                    if RW_STAGE < 0:
                        return
                    q0, q1 = (0, NL) if s0 < NL else (NL, T)
                    s1 = s0 + 128
                    lo = s0 - 64 if s0 - 64 >= q0 else q0
                    hi = lo + 256
                    if hi > q1:
                        hi = q1
                        lo = hi - 256
                    ncol = hi - lo
                    off = s0 - lo
                    make_h(hTt, lo, ncol, rstd, 3, 4, tmp)
                    if RW_STAGE < 0.5:
                        return
                    for c in range(9):
                        bank = c % 2
                        for kc in range(8):
                            w = sl0[:, kc, c * 128:(c + 1) * 128] if c < 8 else sl1[:, kc, 0:128]
                            P.mm(ps[:, bank, 0:ncol], w, hTt[:, kc, 0:ncol], start=(kc == 0), stop=(kc == 7))
                        rw = raw[:, bank, :]
                        P.copy("act", rw[:, 1:1 + ncol], ps[:, bank, 0:ncol])
                        if off == 0:
                            P.memset("pool", rw[:, 0:1], 0.0)
                        if off + 128 == ncol:
                            P.memset("pool", rw[:, 1 + ncol:2 + ncol], 0.0)
                        fc = f[:, c, :]
                        P.ts("dve", fc, rw[:, 1 + off:129 + off], c0t[:, c:c + 1], None, ALU.mult)
                        P.ts("pool", tmp[:, 0, 0:128], rw[:, off:128 + off], mu[:, c:c + 1], None, ALU.mult)
                        P.ts("pool", tmp[:, 1, 0:128], rw[:, 2 + off:130 + off], mu[:, 9 + c:10 + c], None, ALU.mult)
                        P.tt("dve", fc, fc, tmp[:, 0, 0:128], ALU.add)
                        P.tt("dve", fc, fc, tmp[:, 1, 0:128], ALU.add)
                    if RW_STAGE < 1:
                        return
                    dr_ = slice(d * 64, (d + 1) * 64)
                    P.act(lob[dr_, 0, :], f[dr_, 6, :], AF.Tanh)
                    P.copy("dve", lob[:, 1, :], f[:, 7, :])
                    if d == 1:
                        P.act(lob[:, 2, :], f[:, 8, :], AF.Sigmoid)
                    sg = Wt(0)
                    for hc in range(2):
                        pm = ps[:, 2, hc * 128:(hc + 1) * 128]
                        P.mm(pm, lwb[dr_, 0, hc * 128:(hc + 1) * 128], lob[dr_, 0, :])
                        P.act(sg[:, hc, :], pm, AF.Sigmoid, bias=pcol(pre + "w0", 1, d * 2 + hc), scale=1.0)
                    a_t = {0: Wt(1), 1: Wt(2)}
                    for dd in ((0,) if d == 0 else (0, 1)):
                        ddr = slice(dd * 64, (dd + 1) * 64)
                        for hc in range(2):
                            pm = ps[:, 2, 256 + hc * 128:256 + (hc + 1) * 128]
                            P.mm(pm, lwb[ddr, 1, hc * 128:(hc + 1) * 128], lob[ddr, 1, :])
                            P.act(a_t[dd][:, hc, :], pm, AF.Sigmoid, bias=pcol(pre + "a0", 1, dd * 2 + hc), scale=1.0)
                    gt = Wt(3)
                    if d == 1:
                        for hc in range(2):
                            pm = ps[:, 2, hc * 128:(hc + 1) * 128]
                            P.mm(pm, lwb[:, 2, hc * 128:(hc + 1) * 128], lob[:, 2, :])
                            P.copy("act", gt[:, hc, :], pm)
                    if RW_STAGE < 2:
                        return
                    kks, kkn, t6 = Wt(4), Wt(5), Wt(6)
                    for hc in range(2):
                        P.ts("dve", kks[:, hc, :], f[:, 2 + hc, :], pcol(pre + "kk", 1, hc), None, ALU.mult)
                    P.act(sq[:], kks, AF.Square)
                    pk = ps[:, 2, 0:256].rearrange("p (h t) -> p h t", t=128)
                    for hc in range(2):
                        P.mm(pk[:, hc, :], bones[:], sq[:, hc, :])
                    P.act(t6, pk, AF.Sqrt, bias=eps_t[:, 1:2], scale=1.0)
                    P.recip(t6, t6)
                    P.tt("dve", kkn, kks, t6, ALU.mult)
                    acur = a_t[d]
                    kd, bv = Wt(7), Wt(8)
                    for hc in range(2):
                        P.ts("dve", t6[:, hc, :], acur[:, hc, :], -1.0, pcol(pre + "ka", 1, hc), ALU.add, ALU.mult)
                    P.stt("dve", kd, t6, 1.0, f[:, 2:4, :], ALU.add, ALU.mult)
                    P.tt("pool", bv, kkn, acur, ALU.mult)
                    if RW_STAGE < 3:
                        return
                    cs, ex, ci, en = Wt(9), Wt(10), Wt(11), Wt(12)
                    for hc in range(2):
                        P.scan(cs[:, hc, :], rmask[:], sg[:, hc, :])
                    csv = v4(cs)
                    tot = csv[:, :, :, 63:64]
                    totb = tot.to_broadcast([128, 2, 2, 64])
                    if d == 0:
                        ci = cs
                        P.tt("dve", ex, cs, sg, ALU.subtract)
                        P.tt("dve", v4(en), totb, csv, ALU.subtract)
                    else:
                        P.tt("dve", v4(ex), totb, csv, ALU.subtract)
                        P.tt("dve", ci, ex, sg, ALU.add)
                        P.tt("dve", en, cs, sg, ALU.subtract)
                    Eex, Ein, Einv, Eend = Wt(6), Wt(13), Wt(14), Wt(15)
                    P.act(Eex, ex, AF.Exp, scale=-CDEC)
                    P.act(Ein, ci, AF.Exp, scale=-CDEC)
                    P.act(Einv, ci, AF.Exp, scale=CDEC)
                    P.act(Eend, en, AF.Exp, scale=-CDEC)
                    P.act(gam[:], csv[:, :, :, 63], AF.Exp, scale=-CDEC)
                    if RW_STAGE < 4:
                        return
                    P.stt("dve", ar[:, :, :, 0, :], v4(kkn), -1.0, v4(Eex), ALU.mult, ALU.mult)
                    P.tt("dve", ar[:, :, :, 1, :], v4(f[:, 0:2, :]), v4(Ein), ALU.mult)
                    P.tt("dve", bt[:], bv, Einv, ALU.mult)
                    P.tt("dve", kt[:], kd, Einv, ALU.mult)
                    bh, kh = Wt(9), Wt(10)
                    P.tt("pool", bh, bv, Eend, ALU.mult)
                    P.tt("pool", kh, kd, Eend, ALU.mult)
                    if RW_STAGE < 5:
                        return
                    for ch in range(2):
                        pt = ps[0:64, 3, :]
                        for hc in range(2):
                            P.transpose(pt[:, hc * 128:(hc + 1) * 128], bh[:, hc, ch * 64:(ch + 1) * 64], identf)
                            P.transpose(pt[:, 256 + hc * 128:256 + (hc + 1) * 128], kh[:, hc, ch * 64:(ch + 1) * 64], identf)
                        if RW_STAGE < 5.2:
                            continue
                        if RW_VAR != "noact":
                            P.copy("act", bhT[:, ch, :], pt[:, 0:256])
                        if RW_VAR != "nodve":
                            P.copy("dve", khT[:, ch, :], pt[:, 256:512])
                        if RW_STAGE < 5.4:
                            continue
                        pv = ps[0:64, 2, 256:512]
                        for hc in range(2):
                            P.transpose(pv[:, hc * 128:(hc + 1) * 128], f[:, 4 + hc, ch * 64:(ch + 1) * 64], identf)
                        if RW_STAGE < 5.6:
                            continue
                        pv4 = pv.rearrange("p (h e i) -> p h e i", e=2, i=64)
                        vt5 = vTp[:, ch].rearrange("p (h e) (s i) -> p h e s i", e=2, i=64)
                        for e in range(2):
                            P.copy("dve" if e == 0 else "act", vt5[:, :, e, e, :], pv4[:, :, e, :])
                    if RW_STAGE < 6:
                        return
                    mk = pcol("mask", 128, 0 if d == 0 else 128, rows=64)
                    mk3 = pcol("mask", 64, 128 if d == 0 else 0, rows=64)
                    p3 = ps[0:64, 6, :].rearrange("p (c h t) -> p c h t", h=4, t=64)
                    for ch in range(2):
                        p1 = ps[0:64, 4, :].rearrange("p (h x) -> p h x", x=128)
                        p2 = ps[0:64, 5, :].rearrange("p (h x) -> p h x", x=128)
                        ar2 = ar[:].rearrange("p a c x t -> p a c (x t)")
                        cs_ = slice(ch * 64, (ch + 1) * 64)
                        for typ, horder in ((0, (0, 2, 1, 3)), (1, (1, 3, 0, 2)), (2, (0, 2, 1, 3))):
                            for h in horder:
                                hc, pr = h // 2, (h % 2) * 64
                                rhs = ar2[pr:pr + 64, hc, ch, :]
                                if typ == 0:
                                    P.mm(p1[:, h, :], bt[pr:pr + 64, hc, cs_], rhs)
                                elif typ == 1:
                                    P.mm(p2[:, h, :], kt[pr:pr + 64, hc, cs_], rhs)
                                else:
                                    P.mm(p3[:, ch, h, :], ar[pr:pr + 64, hc, ch, 0, :], bt[pr:pr + 64, hc, cs_])
                        mkb = mk.unsqueeze(1).to_broadcast([64, 4, 128])
                        if RW_STAGE < 6.1:
                            continue
                        P.tt("dve", S1[:, ch], ps[0:64, 4, :].rearrange("p (h x) -> p h x", x=128), mkb, ALU.mult)
                        if RW_STAGE < 6.2:
                            continue
                        P.tt("dve", S2[:, ch], ps[0:64, 5, :].rearrange("p (h x) -> p h x", x=128), mkb, ALU.mult)
                    if RW_STAGE < 6.4:
                        return
                    P.tt("dve", S3[:].rearrange("p c h t -> p (c h) t"), ps[0:64, 6, :].rearrange("p (g t) -> p g t", t=64),
                         mk3.unsqueeze(1).to_broadcast([64, 8, 64]), ALU.mult)
                    if RW_STAGE < 7:
                        return
                    idb = pcol("ident", 64, rows=64).unsqueeze(1).to_broadcast([64, 8, 64])
                    g8 = lambda ap: ap.rearrange("p c h t -> p (c h) t")
                    P.tt("dve", g8(X[:]), g8(S1[:, :, :, 0:64]), idb, ALU.add)
                    N1 = S1[:, :, :, 0:64]
                    N1T = S3[:]
                    pA = ps[0:64, 4, :].rearrange("p (c h t) -> p c h t", h=4, t=64)
                    pB = ps[0:64, 5, :].rearrange("p (c h t) -> p c h t", h=4, t=64)
                    pC = ps[0:64, 6, :].rearrange("p (c h t) -> p c h t", h=4, t=64)
                    for lev in range(5):
                        last = lev == 4
                        for ch in range(2):
                            for h in range(4):
                                if not last:
                                    P.mm(pA[:, ch, h, :], N1T[:, ch, h, :], N1[:, ch, h, :])
                                P.mm(pB[:, ch, h, :], N1[:, ch, h, :], N1T[:, ch, h, :])
                        if not last:
                            P.copy("act", Na[:], pA)
                        P.copy("dve", NaT[:], pB)
                        for ch in range(2):
                            for h in range(4):
                                P.mm(pC[:, ch, h, :], NaT[:, ch, h, :], X[:, ch, h, :])
                        P.tt("dve", X[:], X[:], pC, ALU.add)
                        N1, N1T = Na[:], NaT[:]
                    if RW_STAGE < 8:
                        return
                    ysum = Wt(4)
                    Us5 = Us[:].rearrange("p (h e) (s i) -> p h e s i", e=2, i=64)
                    for ch in ((0, 1) if d == 0 else (1, 0)):
                        pZ = ps[0:64, 7, 0:256].rearrange("p (h i) -> p h i", i=64)
                        pU = ps[0:64, 7, 256:512].rearrange("p (h i) -> p h i", i=64)
                        for h in (0, 2, 1, 3):
                            hc, e = h // 2, h % 2
                            pr = e * 64
                            P.mm(pZ[:, h, :], ar[pr:pr + 64, hc, ch, 0, :], Hst[pr:pr + 64, hc, pr:pr + 64],
                                 start=True, stop=False)
                            P.mm(pZ[:, h, :], S2[:, ch, h, 0:64], vTp[:, ch, h, pr:pr + 64], start=False, stop=True)
                        P.copy("act", Zs[:], pZ)
                        for h in range(4):
                            P.mm(pU[:, h, :], X[:, ch, h, :], Zs[:, h, :])
                        pU5 = pU.rearrange("p (h e) i -> p h e i", e=2)
                        for e in range(2):
                            P.copy("dve" if e == 0 else "act", Us5[:, :, e, e, :], pU5[:, :, e, :])
                        pY = ps[:, 3, 0:128].rearrange("p (h t) -> p h t", t=64)
                        for hc in range(2):
                            P.mm(pY[:, hc, :], Hst[:, hc, :], ar[:, hc, ch, 1, :], start=True, stop=False)
                            for e in range(2):
                                h = hc * 2 + e
                                P.mm(pY[:, hc, :], Us[:, h, :], S1[:, ch, h, 64:128], start=False, stop=False)
                                P.mm(pY[:, hc, :], vTp[:, ch, h, :], S2[:, ch, h, 64:128], start=False, stop=(e == 1))
                        pH = ps[:, 3, 256:512].rearrange("p (h x) -> p h x", x=128)
                        for hc in range(2):
                            for e in range(2):
                                h = hc * 2 + e
                                P.mm(pH[:, hc, :], bhT[:, ch, hc * 128:(hc + 1) * 128], Us[:, h, :],
                                     start=(e == 0), stop=False)
                                P.mm(pH[:, hc, :], khT[:, ch, hc * 128:(hc + 1) * 128], vTp[:, ch, h, :],
                                     start=False, stop=(e == 1))
                        for hc in range(2):
                            for e in range(2):
                                pr = e * 64
                                P.stt("dve", Hst[pr:pr + 64, hc, pr:pr + 64], Hst[pr:pr + 64, hc, pr:pr + 64],
                                      gam[pr:pr + 64, hc, ch:ch + 1], pH[pr:pr + 64, hc, pr:pr + 64], ALU.mult, ALU.add)
                        tc_ = slice(s0 + ch * 64, s0 + (ch + 1) * 64)
                        if d == 0:
                            P.copy("act", yf[:, :, tc_], pY)
                        else:
                            P.tt("dve", ysum[:, :, ch * 64:(ch + 1) * 64], pY, yf[:, :, tc_], ALU.add)
                    if d == 0:
                        return
                    y = ysum
                    P.copy("act", sq[:], y)
                    pm = ps[:, 2, 0:256].rearrange("p (h t) -> p h t", t=128)
                    pm2 = ps[:, 2, 256:512].rearrange("p (h t) -> p h t", t=128)
                    for hc in range(2):
                        P.mm(pm[:, hc, :], bones[:], sq[:, hc, :])
                    yc = Wt(5)
                    P.stt("dve", yc, pm, -1.0 / 64.0, y, ALU.mult, ALU.add)
                    P.act(sq[:], yc, AF.Square)
                    for hc in range(2):
                        P.mm(pm2[:, hc, :], bones[:], sq[:, hc, :])
                    rs = Wt(6)
                    P.act(rs, pm2, AF.Sqrt, bias=eps_t[:, 2:3], scale=1.0 / 64.0)
                    P.recip(rs, rs)
                    P.tt("dve", yc, yc, rs, ALU.mult)
                    for hc in range(2):
                        P.ts("dve", yc[:, hc, :], yc[:, hc, :], pcol(pre + "lng", 1, hc), pcol(pre + "lnb", 1, hc),
                             ALU.mult, ALU.add)
                    t7 = Wt(7)
                    P.tt("dve", t7, a_t[0], a_t[1], ALU.add)
                    for hc in range(2):
                        P.ts("dve", t7[:, hc, :], t7[:, hc, :], -2.0, pcol(pre + "ka", 1, hc), ALU.add, ALU.mult)
                    P.stt("dve", t7, t7, 2.0, f[:, 2:4, :], ALU.add, ALU.mult)
                    for hc in range(2):
                        P.stt("dve", sq[:, hc, :], f[:, hc, :], pcol(pre + "rk", 1, hc), t7[:, hc, :], ALU.mult, ALU.mult)
                    for hc in range(2):
                        P.mm(pm[:, hc, :], bones[:], sq[:, hc, :])
                    P.tt("dve", t7, pm, f[:, 4:6, :], ALU.mult)
                    P.tt("dve", yc, yc, t7, ALU.add)
                    P.tt("dve", catA[:, :, s0:s0 + 128], yc, gt, ALU.mult)

                for d in range(2):
                    P.copy("dve", Hst[:], zt[:])
                    if d == 0:
                        segs = [NL, NL + 128] + [i * 128 for i in range(16)]
                    else:
                        segs = [NL + 128, NL] + [i * 128 for i in range(15, -1, -1)]
                    for s0 in segs[:RW_NSEG]:
                        segment(d, s0)
                P.barrier()

        def conv(l, rstd, catB):
            pre = f"L{l}_"
            LB, CB = 0, NL + 30
            with contextlib.ExitStack() as sc:
                sb = lambda n, s, d=F32: sc.enter_context(nc.sbuf_tensor(un(n), s, d))
                hTt = sb("c_hTt", [128, 8, 512], BF16)
                tmp = sb("c_tmp", [128, 2, 512])
                hp = sb("c_hp", [128, 2, T + 60])
                acc = sb("c_acc", [128, 2, T])
                wk = sb("c_wk", [128, 2, 512])
                ab = sb("c_ab", [128, 2, 512], F32R)
                o256 = sb("c_o256", [128, 128], F32R)
                P.memset("dve", tmp[:, 0, 0:128], 1.0 / 256.0)
                P.copy("dve", o256[:], tmp[:, 0, 0:128])
                P.memset("pool", hp[:], 0.0)
                si, sl = next_slot()
                P.dma("pool", sl[:, :, 0:512], mwin_d[l, :, 1152:1664].rearrange("(kc p) n -> p kc n", p=128),
                      "slot%d_0" % si, writes=[sl[:, :, 0:512]])
                for (c0, n) in TILES:
                    make_h(hTt, c0, n, rstd, 3, 4, tmp)
                    base = (LB + 15 + c0) if c0 < NL else (CB + 15 + c0 - NL)
                    for hc in range(2):
                        for kc in range(8):
                            P.mm(ps[:, hc, 0:n], sl[:, kc, hc * 128:(hc + 1) * 128], hTt[:, kc, 0:n],
                                 start=(kc == 0), stop=(kc == 7))
                        for kc in range(8):
                            P.mm(ps[:, 2 + hc, 0:n], sl[:, kc, 256 + hc * 128:256 + (hc + 1) * 128], hTt[:, kc, 0:n],
                                 start=(kc == 0), stop=(kc == 7))
                        P.act(wk[:, hc, 0:n], ps[:, 2 + hc, 0:n], AF.Sigmoid)
                        P.tt("dve", hp[:, hc, base:base + n], ps[:, hc, 0:n], wk[:, hc, 0:n], ALU.mult)
                dww = pcol(pre + "dww", 62)
                for hc in range(2):
                    for (a0_, n, pb) in ((0, NL, LB), (NL, NCX, CB)):
                        o = acc[:, hc, a0_:a0_ + n]
                        P.ts("dve", o, hp[:, hc, pb:pb + n], dww[:, hc * 31:hc * 31 + 1], None, ALU.mult)
                        for k in range(1, 31):
                            P.stt("dve", o, hp[:, hc, pb + k:pb + k + n], dww[:, hc * 31 + k:hc * 31 + k + 1], o,
                                  ALU.mult, ALU.add)
                for (c0, n) in TILES:
                    for hc in range(2):
                        P.act(ab[:, hc, 0:n], acc[:, hc, c0:c0 + n], AF.Identity, bias=pcol(pre + "dwb", 1, hc), scale=1.0)
                    for hc in range(2):
                        P.mm(ps[:, 4, 0:n], o256[:], ab[:, hc, 0:n], start=(hc == 0), stop=(hc == 1))
                    for hc in range(2):
                        P.tt("dve", wk[:, hc, 0:n], ab[:, hc, 0:n], ps[:, 4, 0:n], ALU.subtract)
                    P.act(ab[:, :, 0:n], wk[:, :, 0:n], AF.Square)
                    for hc in range(2):
                        P.mm(ps[:, 5, 0:n], o256[:], ab[:, hc, 0:n], start=(hc == 0), stop=(hc == 1))
                    P.act(tmp[:, 0, 0:n], ps[:, 5, 0:n], AF.Sqrt, bias=eps_t[:, 3:4], scale=1.0)
                    P.recip(tmp[:, 0, 0:n], tmp[:, 0, 0:n])
                    for hc in range(2):
                        P.tt("dve", wk[:, hc, 0:n], wk[:, hc, 0:n], tmp[:, 0, 0:n], ALU.mult)
                        P.ts("dve", wk[:, hc, 0:n], wk[:, hc, 0:n], pcol(pre + "clg", 1, hc), pcol(pre + "clb", 1, hc),
                             ALU.mult, ALU.add)
                        P.act(catB[:, hc, c0:c0 + n], wk[:, hc, 0:n], AF.Silu)
                P.barrier()

        def attention(l, rstd, catC):
            pre = f"L{l}_"
            lam_init = 0.8 - 0.6 * math.exp(-0.3 * l)
            with contextlib.ExitStack() as sc:
                sb = lambda n, s, d=F32: sc.enter_context(nc.sbuf_tensor(un(n), s, d))
                rope = sb("a_rope", [128, 2, NL])
                hTt = sb("a_hTt", [128, 8, 512], BF16)
                tmp = sb("a_tmp", [128, 2, 512])
                qT = sb("a_qT", [128, T], BF16)
                kT = sb("a_kT", [128, T], BF16)
                vtok = sb("a_vtok", [128, 18, 128], BF16)
                eT = sb("a_eT", [128, 2, 512], BF16)
                ot = sb("a_ot", [128, 2, 512])
                lamt = sb("a_lam", [128, 8])
                lamb = sb("a_lamb", [128, 2], BF16)
                P.dma("sp", rope[:], rope_d[:, :, 0:NL], "rope", writes=[rope[:]])
                lv = pcol(pre + "lam", 4)
                P.tt("dve", lamb[:, 0:1], lv[:, 0:1], lv[:, 1:2], ALU.mult)
                P.tt("dve", lamb[:, 1:2], lv[:, 2:3], lv[:, 3:4], ALU.mult)
                P.mm(ps[:, 7, 0:2], ones1[:], lamb[:])
                P.act(lamt[:, 0:2], ps[:, 7, 0:2], AF.Exp)
                P.tt("dve", lamt[:, 2:3], lamt[:, 1:2], lamt[:, 0:1], ALU.subtract)
                P.ts("dve", lamt[:, 3:4], lamt[:, 2:3], -lam_init, None, ALU.add)
                P.ts("dve", lamt[:, 4:5], pcol(pre + "dng", 1), 1.0 - lam_init, None, ALU.mult)
                neglam = lamt[:, 3:4]
                dsc = lamt[:, 4:5]
                for h in range(4):
                    si, sl = next_slot()
                    srcs = (1664 + h * 128, PIN + h * 128, 2176 + h * 128, PIN + 512 + h * 128, 2688 + h * 128)
                    for j, c_ in enumerate(srcs):
                        P.dma("pool", sl[:, :, j * 128:(j + 1) * 128],
                              mwin_d[l, :, c_:c_ + 128].rearrange("(kc p) n -> p kc n", p=128),
                              "slot%d_%d" % (si, j), writes=[sl[:, :, j * 128:(j + 1) * 128]])
                    for (c0, n) in TILES:
                        make_h(hTt, c0, n, rstd, 3, 4, tmp)
                        for j in range(5):
                            for kc in range(8):
                                P.mm(ps[:, j, 0:n], sl[:, kc, j * 128:(j + 1) * 128], hTt[:, kc, 0:n],
                                     start=(kc == 0), stop=(kc == 7))
                        if c0 < NL:
                            for (dst, j) in ((qT, 0), (kT, 2)):
                                P.tt("dve", ot[:, 0, 0:n], ps[:, j, 0:n], rope[:, 0, c0:c0 + n], ALU.mult)
                                P.tt("dve", ot[:, 1, 0:n], ps[:, j + 1, 0:n], rope[:, 1, c0:c0 + n], ALU.mult)
                                P.tt("pool", dst[:, c0:c0 + n], ot[:, 0, 0:n], ot[:, 1, 0:n], ALU.add)
                        else:
                            P.copy("act", qT[:, c0:c0 + n], ps[:, 0, 0:n])
                            P.copy("act", kT[:, c0:c0 + n], ps[:, 2, 0:n])
                        P.copy("act", tmp[:, 0, 0:n], ps[:, 4, 0:n])
                        for b_ in range(n // 128):
                            P.transpose(ps[:, 5, b_ * 128:(b_ + 1) * 128], tmp[:, 0, b_ * 128:(b_ + 1) * 128], identf)
                        kc0 = c0 // 128
                        P.copy("dve", vtok[:, kc0:kc0 + n // 128, :],
                               ps[:, 5, 0:n].rearrange("p (b e) -> p b e", e=128))
                    qgroups = [(c0, n, list(range(18))) for (c0, n) in TILES[:4]] + [(NL, NCX, [16, 17])]
                    for (qc0, qn, kchunks) in qgroups:
                        for m in range(2):
                            mr = slice(m * 64, (m + 1) * 64)
                            pO = ps[:, 2 + 2 * m, 0:qn]
                            pZ = ps[:, 3 + 2 * m, 0:qn]
                            nk = len(kchunks)
                            def score(i_):
                                kc_ = kchunks[i_]
                                P.mm(ps[:, i_ % 2, 0:qn], kT[mr, kc_ * 128:(kc_ + 1) * 128], qT[mr, qc0:qc0 + qn])
                            score(0)
                            for i, kc in enumerate(kchunks):
                                sb_ = ps[:, i % 2, 0:qn]
                                if i + 1 < nk:
                                    score(i + 1)
                                P.act(eT[:, i % 2, 0:qn], sb_, AF.Exp, scale=0.125)
                                P.mm(pO, vtok[:, kc, :], eT[:, i % 2, 0:qn], start=(i == 0), stop=(i == nk - 1))
                                P.mm(pZ, ones1[:], eT[:, i % 2, 0:qn], start=(i == 0), stop=(i == nk - 1))
                            P.recip(ot[:, m, 0:qn], pZ)
                            P.tt("dve", ot[:, m, 0:qn], pO, ot[:, m, 0:qn], ALU.mult)
                        o = tmp[:, 0, 0:qn]
                        P.stt("dve", o, ot[:, 1, 0:qn], neglam, ot[:, 0, 0:qn], ALU.mult, ALU.add)
                        P.act(eT[:, 0, 0:qn], o, AF.Square)
                        P.mm(ps[:, 6, 0:qn], ones1[:], eT[:, 0, 0:qn])
                        P.act(tmp[:, 1, 0:qn], ps[:, 6, 0:qn], AF.Sqrt, bias=eps_t[:, 3:4], scale=1.0 / 128.0)
                        P.recip(tmp[:, 1, 0:qn], tmp[:, 1, 0:qn])
                        P.stt("dve", catC[:, h, qc0:qc0 + qn], o, dsc, tmp[:, 1, 0:qn], ALU.mult, ALU.mult)
                P.barrier()

        def mixer(l):
            with contextlib.ExitStack() as sc:
                sb = lambda n, s, d=F32: sc.enter_context(nc.sbuf_tensor(un(n), s, d))
                rstd = sb("m_rstd", [128, T])
                catA = sb("m_catA", [128, 2, T], BF16)
                with nc.sbuf_tensor(un("m_sqb"), [128, 8, 512], BF16) as sqb:
                    norm_stats(rstd, sqb)
                    P.barrier()
                if "rwkv" not in SKIP:
                    rwkv(l, rstd, catA)
                catB = sb("m_catB", [128, 2, T], BF16)
                if "conv" not in SKIP:
                    conv(l, rstd, catB)
                catC = sb("m_catC", [128, 4, T], BF16)
                if "attn" not in SKIP:
                    attention(l, rstd, catC)
                if taps is not None and l == 0:
                    dt_ = sb("m_dbg", [128, T])
                    for kc in range(8):
                        src = catA[:, kc] if kc < 2 else (catB[:, kc - 2] if kc < 4 else catC[:, kc - 4])
                        P.copy("dve", dt_[:], src)
                        tap("cat%d" % kc, dt_[:])
                si, sl = next_slot()
                for hf in range(2):
                    P.dma("pool", sl[:, :, hf * 512:(hf + 1) * 512],
                          mwout_d[l, :, hf * 512:(hf + 1) * 512].rearrange("(kc p) n -> p kc n", p=128),
                          "slot%d_%d" % (si, hf), writes=[sl[:, :, hf * 512:(hf + 1) * 512]])
                cnt = 0
                for dc in range(8):
                    for (c0, n) in TILES:
                        s_ = 0 if c0 < NL else 1
                        bo = cnt % 2
                        cnt += 1
                        for kc in range(8):
                            src = catA[:, kc] if kc < 2 else (catB[:, kc - 2] if kc < 4 else catC[:, kc - 4])
                            P.mm(ps[:, bo, 0:n], sl[:, kc, dc * 128:(dc + 1) * 128], src[:, c0:c0 + n],
                                 start=(kc == 0), stop=(kc == 7))
                        P.stt("dve", xT[:, dc, c0:c0 + n], ps[:, bo, 0:n], der[:, 5, dc, s_:s_ + 1],
                              xT[:, dc, c0:c0 + n], ALU.mult, ALU.add)
                P.barrier()

        stages = []
        for l in range(n_layers):
            stages += [("ada%d" % l, lambda l=l: adaln(l)), ("ffa%d" % l, lambda l=l: ffn(l, 0)),
                       ("mix%d" % l, lambda l=l: mixer(l)), ("ffb%d" % l, lambda l=l: ffn(l, 1))]
        for name, fn in stages:
            fn()
            if stop_at == name:
                break
        if taps is not None:
            for kc in range(8):
                tap("x_kc%d" % kc, xT[:, kc, :])
        final()
        if taps is not None:
            for i in range(len(tap_list)):
                P.wait_dma("sp", "tap%d" % i)
        P.replay()
        build.info = dict(n_ins=dict(P.n_ins), nsem=P.nsem, taps=tap_list)
    return nc


_NC_CACHE = {}


def _host_inputs(inp):
    inp = {k: np.asarray(v) for k, v in inp.items()}
    perm = _perm_cols()
    mw = np.asarray(inp["mix_w_in"], np.float32)
    mw_ext = np.concatenate([mw, mw[:, :, 1664 + perm]], axis=2)
    mw_ext = np.ascontiguousarray(mw_ext)
    rope = _rope_tables()
    lw = np.stack([np.concatenate([np.asarray(inp["rwkv_w2"][l], np.float32).reshape(128, 256),
                                   np.asarray(inp["rwkv_a2"][l], np.float32).reshape(128, 256),
                                   np.asarray(inp["rwkv_g2"][l], np.float32)], axis=1) for l in range(2)], axis=0)
    shared = {
        "rope": rope,
        "lw": np.ascontiguousarray(lw),
        "ada_w": np.ascontiguousarray(inp["ada_w"], np.float32),
        "ffn_w_in": np.ascontiguousarray(inp["ffn_w_in"], np.float32),
        "ffn_w_out": np.ascontiguousarray(inp["ffn_w_out"], np.float32),
        "mix_w_in": mw_ext,
        "mix_w_out": np.ascontiguousarray(inp["mix_w_out"], np.float32),
    }
    maps = []
    for b in range(8):
        m = dict(shared)
        m["xT"] = np.ascontiguousarray(np.asarray(inp["x"][b], np.float32).T)
        m["cT"] = np.ascontiguousarray(np.asarray(inp["ctx"][b], np.float32).T)
        m["prm"] = _pack_prm(inp, b)
        maps.append(m)
    return maps


def kernel(**inputs):
    maps = _host_inputs(inputs)
    if "nc" not in _NC_CACHE:
        _NC_CACHE["nc"] = build()
    nc = _NC_CACHE["nc"]
    res = run_bass_kernel_spmd(nc, maps, core_ids=list(range(8)))
    out = np.stack([np.ascontiguousarray(res.results[b]["outT"].T) for b in range(8)], axis=0)
    return out.astype(np.float32)
```
